# Optimizing a Trainium2 kernel written in Bass

```python
import math
import jax, jax.numpy as jnp
from jax import lax
import numpy as np

D_MODEL = 1024
BATCH = 8
SEQ = 2048
DEPTH = 1

MEM_LEN = 256
ROPE_THETA = 500000.0
EPS = 1e-6
NEG = -1e30
Q_BLOCK = 128
N_BRANCH = 3
BRANCH_WIDTH = 512
A_HEADS = 8
A_HEAD_DIM = 64
A_ROT = A_HEAD_DIM // 4
IDX_HEADS = 8
IDX_DIM = 64
IDX_ROT = IDX_DIM // 4
TOPK_MAX = 256
B_HEADS = 8
B_NOPE = 64
B_ROPE = 32
B_VDIM = 64
B_QK = B_NOPE + B_ROPE
B_Q_RANK = 384
B_KV_RANK = 256
M_HEADS = 4
M_HEAD_DIM = 128

SPLITS = (
    A_HEADS * A_HEAD_DIM,
    A_HEADS * A_HEAD_DIM,
    A_HEADS * A_HEAD_DIM,
    IDX_HEADS * IDX_DIM,
    IDX_DIM,
    IDX_HEADS,
    BRANCH_WIDTH,
    B_Q_RANK,
    B_KV_RANK,
    B_ROPE,
    BRANCH_WIDTH,
    M_HEADS * M_HEAD_DIM,
    BRANCH_WIDTH,
    N_BRANCH * D_MODEL,
)
D_IN = (4 * A_HEADS * A_HEAD_DIM + IDX_DIM + IDX_HEADS + BRANCH_WIDTH
        + B_Q_RANK + B_KV_RANK + B_ROPE + BRANCH_WIDTH
        + M_HEADS * M_HEAD_DIM + BRANCH_WIDTH + N_BRANCH * D_MODEL)

kernel_name = "hybrid_dsa_mla_memory_gated_block"


def rms_norm(x, g):
    xf = x.astype(jnp.float32)
    y = xf * lax.rsqrt(jnp.mean(xf * xf, axis=-1, keepdims=True) + EPS)
    return (y * g.astype(jnp.float32)).astype(x.dtype)


def rope_tables(positions, rot_dim):
    inv_freq = ROPE_THETA ** (-(jnp.arange(0, rot_dim, 2, dtype=jnp.float32) / rot_dim))
    ang = positions.astype(jnp.float32)[..., None] * inv_freq
    return jnp.cos(ang), jnp.sin(ang)


def rotate(x, cos, sin):
    half = x.shape[-1] // 2
    x1 = x[..., :half].astype(jnp.float32)
    x2 = x[..., half:].astype(jnp.float32)
    return jnp.concatenate([x1 * cos - x2 * sin, x2 * cos + x1 * sin], axis=-1).astype(x.dtype)


def partial_rope(x, cos, sin):
    rd = 2 * cos.shape[-1]
    return jnp.concatenate([rotate(x[..., :rd], cos, sin), x[..., rd:]], axis=-1)


def to_blocks(t, nb):
    b = t.shape[0]
    return jnp.moveaxis(t.reshape((b, nb, Q_BLOCK) + t.shape[2:]), 1, 0)


def from_blocks(t):
    t = jnp.moveaxis(t, 0, 1)
    return t.reshape((t.shape[0], t.shape[1] * t.shape[2]) + t.shape[3:])


def dsa_attention(q, k, v, qi, ki, wi):
    b, s, h, d = q.shape
    nb = s // Q_BLOCK
    topk = min(TOPK_MAX, s // 4)
    key_pos = jnp.arange(s)
    ki_f = ki.astype(jnp.float32)
    scale = d ** -0.5

    def one_block(args):
        qb, qib, wib, start = args
        q_pos = start + jnp.arange(Q_BLOCK)
        causal = key_pos[None, :] <= q_pos[:, None]
        dots = jnp.einsum('bqhc,bsc->bhqs', qib.astype(jnp.float32), ki_f) * (IDX_DIM ** -0.5)
        index = jnp.einsum('bhqs,bqh->bqs', jax.nn.relu(dots),
                           wib.astype(jnp.float32) * (IDX_HEADS ** -0.5))
        index = jnp.where(causal[None], index, NEG)
        _, idx = lax.top_k(index, topk)
        valid = idx <= q_pos[None, :, None]
        k_sel = jax.vmap(lambda kb, ib: kb[ib])(k, idx)
        v_sel = jax.vmap(lambda vb, ib: vb[ib])(v, idx)
        sc = jnp.einsum('bqhd,bqkhd->bhqk', qb, k_sel).astype(jnp.float32) * scale
        sc = jnp.where(valid[:, None], sc, NEG)
        p = jax.nn.softmax(sc, axis=-1).astype(v.dtype)
        return jnp.einsum('bhqk,bqkhd->bqhd', p, v_sel)

    starts = jnp.arange(nb) * Q_BLOCK
    out = lax.map(one_block, (to_blocks(q, nb), to_blocks(qi, nb), to_blocks(wi, nb), starts))
    return from_blocks(out)


def causal_attention(q, k, v):
    b, s, h, dq = q.shape
    nb = s // Q_BLOCK
    key_pos = jnp.arange(s)
    scale = dq ** -0.5

    def one_block(args):
        qb, start = args
        q_pos = start + jnp.arange(Q_BLOCK)
        sc = jnp.einsum('bqhd,bkhd->bhqk', qb, k).astype(jnp.float32) * scale
        sc = jnp.where((key_pos[None, :] <= q_pos[:, None])[None, None], sc, NEG)
        p = jax.nn.softmax(sc, axis=-1).astype(v.dtype)
        return jnp.einsum('bhqk,bkhd->bqhd', p, v)

    starts = jnp.arange(nb) * Q_BLOCK
    return from_blocks(lax.map(one_block, (to_blocks(q, nb), starts)))


def setup_inputs(seed: int = 0) -> dict:
    key = jax.random.key(seed)
    ks = jax.random.split(key, 24)
    f32 = jnp.float32

    def w(k, shape, fan_in):
        return jax.random.normal(k, shape, f32) * (fan_in ** -0.5)

    def gain(k, shape):
        return 1.0 + 0.02 * jax.random.normal(k, shape, f32)

    x = jax.random.normal(ks[0], (BATCH, SEQ, D_MODEL), f32)
    mem = jax.random.normal(ks[1], (BATCH, MEM_LEN, D_MODEL), f32)
    offsets = jax.random.randint(ks[2], (BATCH, 1), 0, 4096, dtype=jnp.int32)
    positions = offsets + jnp.arange(SEQ, dtype=jnp.int32)[None, :]
    return {
        "x": x,
        "mem": mem,
        "positions": positions,
        "g_norm": gain(ks[3], (DEPTH, D_MODEL)),
        "w_in": w(ks[4], (DEPTH, D_MODEL, D_IN), D_MODEL),
        "g_qn_a": gain(ks[5], (DEPTH, A_HEAD_DIM)),
        "g_kn_a": gain(ks[6], (DEPTH, A_HEAD_DIM)),
        "g_cq": gain(ks[7], (DEPTH, B_Q_RANK)),
        "g_ckv": gain(ks[8], (DEPTH, B_KV_RANK)),
        "w_uq": w(ks[9], (DEPTH, B_Q_RANK, B_HEADS * B_QK), B_Q_RANK),
        "w_ukv": w(ks[10], (DEPTH, B_KV_RANK, B_HEADS * (B_NOPE + B_VDIM)), B_KV_RANK),
        "g_qn_b": gain(ks[11], (DEPTH, B_QK)),
        "g_kn_b": gain(ks[12], (DEPTH, B_QK)),
        "g_mem": gain(ks[13], (DEPTH, D_MODEL)),
        "w_mem_kv": w(ks[14], (DEPTH, D_MODEL, 2 * M_HEADS * M_HEAD_DIM), D_MODEL),
        "g_qn_m": gain(ks[15], (DEPTH, M_HEAD_DIM)),
        "g_kn_m": gain(ks[16], (DEPTH, M_HEAD_DIM)),
        "w_branch": w(ks[17], (DEPTH, N_BRANCH, BRANCH_WIDTH, D_MODEL), BRANCH_WIDTH),
        "w_out": w(ks[18], (DEPTH, D_MODEL, D_MODEL), D_MODEL),
    }


def reference(x, mem, positions, g_norm, w_in, g_qn_a, g_kn_a, g_cq, g_ckv, w_uq, w_ukv,
              g_qn_b, g_kn_b, g_mem, w_mem_kv, g_qn_m, g_kn_m, w_branch, w_out):
    b, s, _ = x.shape
    m_len = mem.shape[1]
    cos_a, sin_a = rope_tables(positions, A_ROT)
    cos_b, sin_b = rope_tables(positions, B_ROPE)
    split_at = [int(o) for o in np.cumsum(SPLITS)[:-1]]

    for layer in range(DEPTH):
        h = rms_norm(x, g_norm[layer])
        proj = h @ w_in[layer]
        (q_a, k_a, v_a, q_i, k_i, w_i, z_a, c_q, c_kv, k_rope, z_b, q_m, z_m,
         gate_logits) = jnp.split(proj, split_at, axis=-1)

        q_a = partial_rope(rms_norm(q_a.reshape(b, s, A_HEADS, A_HEAD_DIM), g_qn_a[layer]),
                           cos_a[:, :, None], sin_a[:, :, None])
        k_a = partial_rope(rms_norm(k_a.reshape(b, s, A_HEADS, A_HEAD_DIM), g_kn_a[layer]),
                           cos_a[:, :, None], sin_a[:, :, None])
        v_a = v_a.reshape(b, s, A_HEADS, A_HEAD_DIM)
        q_i = partial_rope(q_i.reshape(b, s, IDX_HEADS, IDX_DIM), cos_a[:, :, None], sin_a[:, :, None])
        k_i = partial_rope(k_i, cos_a, sin_a)
        o_a = dsa_attention(q_a, k_a, v_a, q_i, k_i, w_i)

        q_b = (rms_norm(c_q, g_cq[layer]) @ w_uq[layer]).reshape(b, s, B_HEADS, B_QK)
        kv_b = (rms_norm(c_kv, g_ckv[layer]) @ w_ukv[layer]).reshape(b, s, B_HEADS, B_NOPE + B_VDIM)
        k_nope, v_b = kv_b[..., :B_NOPE], kv_b[..., B_NOPE:]
        k_b = jnp.concatenate(
            [k_nope, jnp.broadcast_to(k_rope[:, :, None, :], (b, s, B_HEADS, B_ROPE))], axis=-1)
        q_b = rms_norm(q_b, g_qn_b[layer])
        k_b = rms_norm(k_b, g_kn_b[layer])
        q_b = jnp.concatenate(
            [q_b[..., :B_NOPE], rotate(q_b[..., B_NOPE:], cos_b[:, :, None], sin_b[:, :, None])], axis=-1)
        k_b = jnp.concatenate(
            [k_b[..., :B_NOPE], rotate(k_b[..., B_NOPE:], cos_b[:, :, None], sin_b[:, :, None])], axis=-1)
        o_b = causal_attention(q_b, k_b, v_b)

        kv_m = (rms_norm(mem, g_mem[layer]) @ w_mem_kv[layer]).reshape(b, m_len, 2, M_HEADS, M_HEAD_DIM)
        k_m = rms_norm(kv_m[:, :, 0], g_kn_m[layer])
        v_m = kv_m[:, :, 1]
        q_m = rms_norm(q_m.reshape(b, s, M_HEADS, M_HEAD_DIM), g_qn_m[layer])
        sc_m = jnp.einsum('bqhd,bmhd->bhqm', q_m, k_m).astype(jnp.float32) * (M_HEAD_DIM ** -0.5)
        p_m = jax.nn.softmax(sc_m, axis=-1).astype(v_m.dtype)
        o_m = jnp.einsum('bhqm,bmhd->bqhd', p_m, v_m)

        ys = jnp.stack([
            o_a.reshape(b, s, BRANCH_WIDTH) * jax.nn.silu(z_a),
            o_b.reshape(b, s, BRANCH_WIDTH) * jax.nn.silu(z_b),
            o_m.reshape(b, s, BRANCH_WIDTH) * jax.nn.silu(z_m),
        ], axis=0)
        branch = jnp.einsum('nbsw,nwd->nbsd', ys, w_branch[layer])
        gates = jax.nn.sigmoid(
            gate_logits.reshape(b, s, N_BRANCH, D_MODEL).astype(jnp.float32)).astype(x.dtype)
        merged = jnp.einsum('bsnd,nbsd->bsd', gates, branch)
        x = x + merged @ w_out[layer]
    return x
```

```python
import os
import threading
from contextlib import ExitStack
import numpy as np
import concourse.bass as bass
import concourse.mybir as mybir
from concourse.alu_op_type import AluOpType as ALU
from concourse.bass_utils import run_bass_kernel_spmd

F32 = mybir.dt.float32
BF16 = mybir.dt.bfloat16
I32 = mybir.dt.int32
AF = mybir.ActivationFunctionType
AX = mybir.AxisListType

S = 2048
D = 1024
NT = 16
D_IN = 7912
EPS = 1e-6
BIG = 30000.0
C_QA, C_KA, C_VA, C_QI, C_KI, C_WI, C_ZA = 0, 512, 1024, 1536, 2048, 2112, 2120
C_CQ, C_CKV, C_KR, C_ZB, C_QM, C_ZM, C_G = 2632, 3016, 3272, 3304, 3816, 4328, 4840
NBIS = 20


DBG_NAMES = {}


class Res:
    __slots__ = ("name", "w", "r", "sem", "cnt")

    def __init__(self, name):
        self.name = name
        self.w = None
        self.r = {}
        self.sem = None
        self.cnt = 0


class FW:
    ENG = ("pe", "act", "dve", "pool", "sp")

    def __init__(self, nc, es):
        self.nc = nc
        self.es = es
        self.eng = {"pe": nc.tensor, "act": nc.scalar, "dve": nc.vector,
                    "pool": nc.gpsimd, "sp": nc.sync}
        self.epoch = 0
        self.nsem = 0
        self._new_sems()
        self.tot = {e: 0 for e in self.ENG}
        self.waited = {e: {} for e in self.ENG}
        self.dma_res = []
        self.uid = 0

    def _new_sems(self):
        self.sem = {e: self.es.enter_context(self.nc.semaphore("s%d_%s" % (self.epoch, e))) for e in self.ENG}
        self.cnt = {e: 0 for e in self.ENG}
        self.nsem += len(self.ENG)

    def _wait(self, e, ev):
        if ev is None:
            return
        key, semh, val, ep = ev
        if ep is not None and ep < self.epoch:
            return
        if key == e and e == "pe":
            return
        w = self.waited[e]
        if w.get(key, 0) >= val:
            return
        w[key] = val
        self.eng[e].wait_ge(semh, val)

    def _deps(self, e, reads, writes):
        for r in reads:
            self._wait(e, r.w)
        for w in writes:
            self._wait(e, w.w)
            for k, ev in w.r.items():
                if k == e:
                    continue
                self._wait(e, ev)

    def _record(self, ev, reads, writes):
        for r in reads:
            r.r[ev[0]] = ev
        for w in writes:
            w.w = ev
            w.r = {}

    def op(self, e, fn, reads=(), writes=()):
        self._deps(e, reads, writes)
        inst = fn(self.eng[e])
        self.cnt[e] += 1
        self.tot[e] += 1
        inst.then_inc(self.sem[e], 1)
        self._record((e, self.sem[e], self.cnt[e], self.epoch), reads, writes)
        Coop.yield_point()

    def dma(self, q, out, in_, reads=(), writes=()):
        self._deps(q, reads, writes)
        tgt = writes[0] if writes else reads[0]
        if tgt.sem is None:
            self.uid += 1
            tgt.sem = self.es.enter_context(self.nc.semaphore("d%d" % self.uid))
            tgt.name = "d%d_%s" % (self.uid, tgt.name)
            self.dma_res.append(tgt)
            self.nsem += 1
        inst = self.eng[q].dma_start(out=out, in_=in_)
        tgt.cnt += 16
        inst.then_inc(tgt.sem, 16)
        self._record((tgt.name, tgt.sem, tgt.cnt, None), reads, writes)
        Coop.yield_point()

    def barrier(self):
        for e in self.ENG:
            for e2 in self.ENG:
                if e2 != e and self.cnt[e2] > 0:
                    self._wait(e, (e2, self.sem[e2], self.cnt[e2], self.epoch))
            for r in self.dma_res:
                if r.cnt > 0:
                    self._wait(e, (r.name, r.sem, r.cnt, None))
        self.epoch += 1
        self._new_sems()
        for e in self.ENG:
            for k in self.ENG:
                self.waited[e].pop(k, None)

    def maybe_barrier(self, limit=100000):
        if max(self.cnt.values()) > limit:
            self.barrier()


class _Stop(Exception):
    pass


class Coop:
    current = {}

    def __init__(self, fn):
        self.go = threading.Semaphore(0)
        self.done = threading.Semaphore(0)
        self.finished = False
        self.exc = None
        self.result = None

        def body():
            self.go.acquire()
            try:
                self.result = fn()
            except BaseException as e:
                self.exc = e
            self.finished = True
            Coop.current.pop(threading.get_ident(), None)
            self.done.release()
        self.t = threading.Thread(target=body)
        self.t.start()
        Coop.current[self.t.ident] = self

    def step(self):
        if self.finished:
            return False
        self.go.release()
        self.done.acquire()
        if self.exc is not None:
            raise self.exc
        if not self.finished:
            Coop.yield_point()
        return not self.finished

    def finish(self):
        while self.step():
            pass
        self.t.join()
        return self.result

    hold = set()

    @staticmethod
    def yield_point():
        c = Coop.current.get(threading.get_ident())
        if c is not None and threading.get_ident() not in Coop.hold:
            c.done.release()
            c.go.acquire()


class Ring:
    def __init__(self, alloc, name, shape, dt, n):
        self.name, self.shape, self.dt = name, shape, dt
        self.t = [alloc(name + str(k), shape, dt) for k in range(n)]
        self.r = [Res(name + str(k)) for k in range(n)]
        self.i = 0

    def grow(self, alloc, n):
        for _ in range(n):
            k = len(self.t)
            self.t.append(alloc(self.name + "x" + str(k), self.shape, self.dt))
            self.r.append(Res(self.name + "x" + str(k)))

    def shrink(self, n):
        for _ in range(n):
            self.t.pop()
            self.r.pop()

    def next(self):
        ln = LANE.get(threading.get_ident())
        if ln is not None and len(self.t) > 1:
            k = ln % len(self.t)
        else:
            k = self.i % len(self.t)
            self.i += 1
        return self.t[k], self.r[k]


LANE = {}


def run_pairs(fn, items, width=int(os.environ.get("KWIDTH", "2")), stagger=12):
    pending = list(items)
    active = [None] * width
    started = [False] * width
    steps = 0
    while pending or any(a is not None for a in active):
        for ln in range(width):
            if active[ln] is None and pending and (ln == 0 or started[ln] or steps >= stagger * ln):
                it = pending.pop(0)

                def body(it=it, ln=ln):
                    LANE[threading.get_ident()] = ln
                    try:
                        return fn(it)
                    finally:
                        LANE.pop(threading.get_ident(), None)
                active[ln] = Coop(body)
                started[ln] = True
            if active[ln] is not None:
                if not active[ln].step():
                    active[ln].finish()
                    active[ln] = None
        steps += 1


def build_program(dbg=False, stop=99):
    nc = bass.Bass("TRN2", target_bir_lowering=False)
    try:
        _build(nc, stop)
    except _Stop:
        pass
    return nc


def _build(nc, STOP):

    def dram(name, shape, dt=F32, kind="ExternalInput"):
        return nc.dram_tensor(name, list(shape), dt, kind=kind).ap()

    x_d = dram("x", [S, D])
    mem_d = dram("mem", [256, D])
    pos_d = dram("pos", [128, NT], I32)
    win_d = dram("w_in", [D, D_IN])
    wuq_d = dram("w_uq", [384, 768])
    wukv_d = dram("w_ukv", [256, 1024])
    wmem_d = dram("w_mem", [D, 1024])
    wbr_d = dram("w_br", [3 * 512, D])
    wout_d = dram("w_out", [D, D])
    gk_d = dram("gk", [128, 21])
    gh_d = dram("gh", [128, 576])
    cst_d = dram("cst", [128, 408])
    out_d = dram("out", [S, D], F32, kind="ExternalOutput")

    with ExitStack() as es:
        fw = FW(nc, es)

        uniq = [0]

        def sbt(stack, name, shape, dt):
            uniq[0] += 1
            nm = "sb%d_%s" % (uniq[0], name)
            DBG_NAMES[name] = nm
            return stack.enter_context(nc.sbuf_tensor(nm, list(shape), dt))

        def galloc(name, shape, dt):
            return sbt(es, name, shape, dt)

        banks = [es.enter_context(nc.psum_tensor("bank%d" % k, [128, 512], F32)) for k in range(8)]
        bres = [Res("bank%d" % k) for k in range(8)]
        rr = {"pj": 0, "tr": 0, "s": 0, "o": 0}
        role = {"pj": (0, 1), "tr": (2, 3), "s": (4, 5), "o": (6, 7)}

        def bank(kind):
            ln = LANE.get(threading.get_ident())
            if ln is not None and ln >= 2:
                k = {"pj": 4, "tr": 6, "s": 5, "o": 7}[kind] if ln == 2 else {"pj": 5, "tr": 7, "s": 4, "o": 6}[kind]
            elif ln is not None:
                k = role[kind][ln % 2]
            else:
                k = role[kind][rr[kind] % 2]
                rr[kind] += 1
            return banks[k], bres[k]

        cstf = galloc("cstf", [128, 408], F32); r_cstf = Res("cstf")
        gk = galloc("gk", [128, 21], F32); r_gk = Res("gk")
        gh = galloc("gh", [128, 576], F32); r_gh = Res("gh")
        posi = galloc("posi", [128, NT], I32); r_posi = Res("posi")
        fw.dma("sp", cstf[:], cst_d[:, :], writes=[r_cstf])
        fw.dma("sp", gk[:], gk_d[:, :], writes=[r_gk])
        fw.dma("sp", gh[:], gh_d[:, :], writes=[r_gh])
        fw.dma("sp", posi[:], pos_d[:, :], writes=[r_posi])
        identb = galloc("identb", [128, 128], BF16)
        bigI = galloc("bigI", [128, 128], BF16)
        cb01 = galloc("cb01", [128, 128], BF16)
        cmaskf = galloc("cmaskf", [128, 128], F32)
        epst = galloc("epst", [128, 1], F32)
        r_c = Res("consts")
        fw.op("dve", lambda e: e.tensor_copy(out=identb[:], in_=cstf[:, 0:128]), reads=[r_cstf], writes=[r_c])
        fw.op("dve", lambda e: e.tensor_scalar(out=bigI[:], in0=cstf[:, 0:128], scalar1=BIG, scalar2=None, op0=ALU.mult), reads=[r_cstf], writes=[r_c])
        fw.op("dve", lambda e: e.tensor_copy(out=cb01[:], in_=cstf[:, 128:256]), reads=[r_cstf], writes=[r_c])
        fw.op("dve", lambda e: e.tensor_scalar(out=cmaskf[:], in0=cstf[:, 128:256], scalar1=1e30, scalar2=None, op0=ALU.mult), reads=[r_cstf], writes=[r_c])
        fw.op("dve", lambda e: e.memset(epst[:], EPS), writes=[r_c])
        invf = cstf[:, 256:280]
        G_QNA, G_KNA, G_QNB, G_KNB, G_QNM, G_KNM = gh[:, 0:64], gh[:, 64:128], gh[:, 128:224], gh[:, 224:320], gh[:, 320:448], gh[:, 448:576]
        GK_NORM, GK_CQ, GK_CKV, GK_MEM = gk[:, 0:8], gk[:, 8:11], gk[:, 11:13], gk[:, 13:21]

        SIN = galloc("SIN", [128, NT, 24], F32)
        COS = galloc("COS", [128, NT, 24], F32)
        r_rope = Res("rope")
        with ExitStack() as ps_:
            posf = sbt(ps_, "posf", [128, NT], F32)
            u = sbt(ps_, "rp_u", [128, NT, 24], F32)
            v = sbt(ps_, "rp_v", [128, NT, 24], F32)
            ki = sbt(ps_, "rp_ki", [128, NT, 24], I32)
            kf = sbt(ps_, "rp_kf", [128, NT, 24], F32)
            m1 = sbt(ps_, "rp_m1", [128, NT, 24], F32)
            r_t = Res("rp")
            fw.op("dve", lambda e: e.tensor_copy(out=posf[:], in_=posi[:]), reads=[r_posi], writes=[r_t])
            fw.op("dve", lambda e: e.tensor_tensor(out=u[:], in0=posf[:].unsqueeze(2).to_broadcast([128, NT, 24]),
                                                  in1=invf.unsqueeze(1).to_broadcast([128, NT, 24]), op=ALU.mult),
                  reads=[r_t, r_cstf], writes=[r_t])
            for shift, dst in ((0.0, SIN), (0.25, COS)):
                fw.op("dve", lambda e: e.tensor_scalar(out=v[:], in0=u[:], scalar1=shift, scalar2=None, op0=ALU.add), reads=[r_t], writes=[r_t])
                fw.op("dve", lambda e: e.tensor_copy(out=ki[:], in_=v[:]), reads=[r_t], writes=[r_t])
                fw.op("dve", lambda e: e.tensor_copy(out=kf[:], in_=ki[:]), reads=[r_t], writes=[r_t])
                fw.op("dve", lambda e: e.tensor_tensor(out=v[:], in0=v[:], in1=kf[:], op=ALU.subtract), reads=[r_t], writes=[r_t])
                fw.op("dve", lambda e: e.tensor_scalar(out=m1[:], in0=v[:], scalar1=0.5, scalar2=None, op0=ALU.is_gt), reads=[r_t], writes=[r_t])
                fw.op("dve", lambda e: e.tensor_tensor(out=v[:], in0=v[:], in1=m1[:], op=ALU.subtract), reads=[r_t], writes=[r_t])
                fw.op("dve", lambda e: e.tensor_scalar(out=m1[:], in0=v[:], scalar1=-0.5, scalar2=None, op0=ALU.is_lt), reads=[r_t], writes=[r_t])
                fw.op("dve", lambda e: e.tensor_tensor(out=v[:], in0=v[:], in1=m1[:], op=ALU.add), reads=[r_t], writes=[r_t])
                fw.op("act", lambda e, dst=dst: e.activation(out=dst[:], in_=v[:], func=AF.Sin, scale=float(2 * np.pi * (1 - 1e-6))),
                      reads=[r_t], writes=[r_rope])
            fw.barrier()

        print("ckpt", 0, fw.tot)
        if STOP == 0:
            fw.barrier()
            raise _Stop()
        hT = galloc("hT", [128, 8, S], BF16); r_hT = Res("hT")
        yT = [None, None, None]
        r_yT = [Res("yT%d" % n) for n in range(3)]

        stg = Ring(galloc, "stg", [128, 8, 128], F32, 2)
        SUBC = 128
        wbr_ = Ring(galloc, "wb", [128, 8, 512], BF16, 2)
        cast_rr = [0]

        def load_w(src, r0, kch, c0, ncols, gain=None):
            wb, r_wb = wbr_.next()
            for s0 in range(0, ncols, SUBC):
                n = min(SUBC, ncols - s0)
                st, r_st = stg.next()
                fw.dma("sp", st[:, 0:kch, 0:n],
                       src[r0:r0 + kch * 128, c0 + s0:c0 + s0 + n].rearrange("(kc p) n -> p kc n", p=128),
                       writes=[r_st])
                if gain is None:
                    fw.op("pool", lambda e: e.tensor_copy(out=wb[:, 0:kch, s0:s0 + n], in_=st[:, 0:kch, 0:n]), reads=[r_st], writes=[r_wb])
                else:
                    fw.op("pool", lambda e: e.tensor_tensor(out=wb[:, 0:kch, s0:s0 + n], in0=st[:, 0:kch, 0:n],
                                                           in1=gain[:, 0:kch].unsqueeze(2).to_broadcast([128, kch, n]), op=ALU.mult),
                          reads=[r_st, r_gk], writes=[r_wb])
            return wb, r_wb

        class WPipe:
            def __init__(self, specs):
                self.specs = specs
                self.k = 0
                self.loaded = {}

            def _load(self, k):
                if k < len(self.specs) and k not in self.loaded:
                    self.loaded[k] = self.specs[k][1]()

            def prefetch(self):
                self._load(self.k)

            def get(self, name, prefetch=True):
                assert self.specs[self.k][0] == name, (self.specs[self.k][0], name)
                self._load(self.k)
                res = self.loaded.pop(self.k)
                self.k += 1
                if prefetch:
                    self._load(self.k)
                return res

        wpipe = WPipe([
            ("ki", lambda: load_w(win_d, 0, 8, C_KI, 72, GK_NORM)),
            ("qi", lambda: load_w(win_d, 0, 8, C_QI, 512, GK_NORM)),
            ("va", lambda: load_w(win_d, 0, 8, C_VA, 512, GK_NORM)),
            ("ka", lambda: load_w(win_d, 0, 8, C_KA, 512, GK_NORM)),
            ("qa", lambda: load_w(win_d, 0, 8, C_QA, 512, GK_NORM)),
            ("za", lambda: load_w(win_d, 0, 8, C_ZA, 512, GK_NORM)),
            ("ckv", lambda: load_w(win_d, 0, 8, C_CKV, 288, GK_NORM)),
            ("cq", lambda: load_w(win_d, 0, 8, C_CQ, 384, GK_NORM)),
            ("zb", lambda: load_w(win_d, 0, 8, C_ZB, 512, GK_NORM)),
            ("km", lambda: load_w(wmem_d, 0, 8, 0, 512, GK_MEM)),
            ("vm", lambda: load_w(wmem_d, 0, 8, 512, 512, GK_MEM)),
            ("qm", lambda: load_w(win_d, 0, 8, C_QM, 512, GK_NORM)),
            ("zm", lambda: load_w(win_d, 0, 8, C_ZM, 512, GK_NORM)),
        ])

        def mm_tok(ps_ap, r_ps, srcT, r_src, t0, wb, r_wb, kch, c0, n):
            for kc in range(kch):
                fw.op("pe", lambda e, kc=kc: e.matmul(ps_ap, lhsT=srcT[:, kc, t0:t0 + 128], rhs=wb[:, kc, c0:c0 + n],
                                                      start=(kc == 0), stop=(kc == kch - 1)),
                      reads=[r_src, r_wb], writes=[r_ps])

        def transp(ps_ap, r_ps, src_ap, r_src):
            fw.op("pe", lambda e: e.matmul(ps_ap, lhsT=src_ap, rhs=identb[:, :], start=True, stop=True),
                  reads=[r_src, r_c], writes=[r_ps])

        class Widen:
            def __init__(self, rings, extra):
                self.rings, self.extra = rings, extra
                self.stack = ExitStack()
                for r in rings:
                    r.grow(lambda nm, sh, dt: sbt(self.stack, nm, sh, dt), extra)

            def close(self):
                fw.barrier()
                for r in self.rings:
                    r.shrink(self.extra)
                self.stack.close()

        wk_a = Ring(galloc, "wka", [128, 768], F32, 2)
        wk_b = Ring(galloc, "wkb", [128, 768], F32, 2)
        wk_bf = Ring(galloc, "wkbf", [128, 1024], BF16, 2)
        sm = Ring(galloc, "sm", [128, 16], F32, 4)
        rp = Ring(galloc, "rp", [128, 4, 128], F32, 2)

        def rstd_from_ss(ss_ap, r_ss, n_el, out_ap, r_out):
            fw.op("act", lambda e: e.activation(out=out_ap, in_=ss_ap, func=AF.Ln, bias=epst[:], scale=1.0 / n_el),
                  reads=[r_ss, r_c], writes=[r_out])
            fw.op("act", lambda e: e.activation(out=out_ap, in_=out_ap, func=AF.Exp, scale=-0.5), reads=[r_out], writes=[r_out])

        def headnorm(src3, r_src, H, d, gain_ap, dst3, r_dst):
            sq, r_sq = wk_b.next()
            sq3 = sq[:, 0:H * d].rearrange("p (h d) -> p h d", h=H)
            fw.op("act", lambda e: e.activation(out=sq3, in_=src3, func=AF.Square), reads=[r_src], writes=[r_sq])
            st_, r_sm = sm.next()
            fw.op("dve", lambda e: e.tensor_reduce(out=st_[:, 0:H], in_=sq3, axis=AX.X, op=ALU.add), reads=[r_sq], writes=[r_sm])
            rstd_from_ss(st_[:, 0:H], r_sm, d, st_[:, 8:8 + H], r_sm)
            fw.op("dve", lambda e: e.tensor_tensor(out=dst3, in0=src3, in1=st_[:, 8:8 + H].unsqueeze(2).to_broadcast([128, H, d]), op=ALU.mult),
                  reads=[r_src, r_sm], writes=[r_dst])
            fw.op("dve", lambda e: e.tensor_tensor(out=dst3, in0=dst3, in1=gain_ap.unsqueeze(1).to_broadcast([128, H, d]), op=ALU.mult),
                  reads=[r_dst, r_gh], writes=[r_dst])

        def rope(x3, r_x, H, o0, half, i, f0):
            t, r_t = rp.next()
            cs = COS[:, i, f0:f0 + half].unsqueeze(1).to_broadcast([128, H, half])
            sn = SIN[:, i, f0:f0 + half].unsqueeze(1).to_broadcast([128, H, half])
            x1 = x3[:, :, o0:o0 + half]
            x2 = x3[:, :, o0 + half:o0 + 2 * half]

            def tv(k):
                return t[:, k, 0:H * half].rearrange("p (h d) -> p h d", h=H)
            for k, (a, b) in enumerate(((x1, cs), (x2, sn), (x2, cs), (x1, sn))):
                fw.op("dve", lambda e, k=k, a=a, b=b: e.tensor_tensor(out=tv(k), in0=a, in1=b, op=ALU.mult),
                      reads=[r_x, r_rope], writes=[r_t])
            fw.op("dve", lambda e: e.tensor_tensor(out=x1, in0=tv(0), in1=tv(1), op=ALU.subtract), reads=[r_t], writes=[r_x])
            fw.op("dve", lambda e: e.tensor_tensor(out=x2, in0=tv(2), in1=tv(3), op=ALU.add), reads=[r_t], writes=[r_x])

        def to_T(srcbf, r_src, nblk, rows, dstT, r_dst, i, blk0=0, src_stride=None):
            stride = rows if src_stride is None else src_stride
            for b0 in range(0, nblk, 4):
                nb = min(4, nblk - b0)
                pb, r_pb = bank("tr")
                for b in range(nb):
                    transp(pb[0:rows, b * 128:(b + 1) * 128], r_pb, srcbf[:, (b0 + b) * stride:(b0 + b) * stride + rows], r_src)
                src_v = pb[0:rows, 0:nb * 128].rearrange("p (b t) -> p b t", b=nb)
                dst_v = dstT[0:rows, blk0 + b0:blk0 + b0 + nb, i * 128:(i + 1) * 128]
                eng = ("act", "dve")[(b0 // 4 + i) % 2]
                if eng == "act":
                    fw.op("act", lambda e: e.activation(out=dst_v, in_=src_v, func=AF.Copy), reads=[r_pb], writes=[r_dst])
                else:
                    fw.op("dve", lambda e: e.tensor_copy(out=dst_v, in_=src_v), reads=[r_pb], writes=[r_dst])

        with ExitStack() as p0:
            def lalloc(name, shape, dt):
                return sbt(p0, name, shape, dt)
            xin = Ring(lalloc, "xin", [128, D], F32, 4)
            wd = Widen([wk_bf], 2)
            wpipe.prefetch()
            def p0_tile(i):
                xt, r_xt = xin.next()
                fw.dma("sp", xt[:], x_d[i * 128:(i + 1) * 128, :], writes=[r_xt])
                sq, r_sq = wk_bf.next()
                st_, r_sm = sm.next()
                fw.op("act", lambda e: e.activation(out=sq[:], in_=xt[:], func=AF.Square, accum_out=st_[:, 0:1]), reads=[r_xt], writes=[r_sq, r_sm])
                rstd_from_ss(st_[:, 0:1], r_sm, D, st_[:, 1:2], r_sm)
                xb, r_xb = wk_bf.next()
                fw.op("dve", lambda e: e.tensor_scalar(out=xb[:], in0=xt[:], scalar1=st_[:, 1:2], scalar2=None, op0=ALU.mult),
                      reads=[r_xt, r_sm], writes=[r_xb])
                to_T(xb, r_xb, 8, 128, hT, r_hT, i)
            run_pairs(p0_tile, list(range(NT)), width=4)
            wd.close()

        print("ckpt", 1, fw.tot)
        if STOP == 1:
            fw.barrier()
            raise _Stop()
        def gate_tile(i, s_, yc, r_yc, wz, r_wz):
            pz, r_pz = bank("pj")
            mm_tok(pz[:, 0:512], r_pz, hT, r_hT, i * 128, wz, r_wz, 8, 0, 512)
            z1f, r_z1 = wk_a.next()
            z1 = z1f[:, 0:512]
            fw.op("act", lambda e: e.activation(out=z1, in_=pz[:, 0:512], func=AF.Exp, scale=-1.0), reads=[r_pz], writes=[r_z1])
            fw.op("dve", lambda e: e.tensor_scalar(out=z1, in0=z1, scalar1=1.0, scalar2=None, op0=ALU.add), reads=[r_z1], writes=[r_z1])
            fw.op("dve", lambda e: e.reciprocal(out=z1, in_=z1), reads=[r_z1], writes=[r_z1])
            fw.op("dve", lambda e: e.tensor_tensor(out=yc[:, s_, :], in0=z1, in1=pz[:, 0:512], op=ALU.mult), reads=[r_z1, r_pz], writes=[r_yc])

        def attention(p_alloc, make_q, kT_of, r_k, vaug, r_v, H, dv, n_ktiles_of, nq, scale, bias_of, yTn, r_yTn, wz, r_wz, mask_of=None, lag=1, first_q=None, order=None):
            dv1 = dv + 1
            pT = Ring(p_alloc, "pT", [128, 512], BF16, 2 + lag)
            ych = Ring(p_alloc, "ych", [128, nq, 512], BF16, 2)
            rec = Ring(p_alloc, "rec", [128, 8], F32, 2)
            et = Ring(p_alloc, "et", [128, nq, dv], F32, 2)
            nchunks = NT // nq

            def prep(c):
                yc, r_yc = ych.next()
                res = make_q(c, yc, r_yc, first_q if c == order[0] else None)
                if len(res) == 4:
                    return res
                return res[0], res[1], yc, r_yc
            order = list(range(nchunks)) if order is None else order
            cur = prep(order[0])
            for oc, c in enumerate(order):
                t_lo = c * nq
                fw.maybe_barrier()
                qT_of, r_q, yc, r_yc = cur
                nxt = Coop(lambda: prep(order[oc + 1])) if oc + 1 < nchunks else None
                nk_max = n_ktiles_of(t_lo + nq - 1)
                steps = [(h, j) for h in range(H) for j in range(nk_max)]
                pulls = max(2, -(-330 // len(steps)))
                state = {}

                def pv_and_epilogue(pv):
                    h, j, tiles, pt, r_pt, pov, r_po = pv
                    for i in tiles:
                        s_ = i - t_lo
                        fw.op("pe", lambda e: e.matmul(pov[:, s_, :], lhsT=pt[:, s_ * 128:(s_ + 1) * 128], rhs=vaug[:, j, h, :],
                                                       start=(j == 0 and s_ == 0), stop=(j == n_ktiles_of(i) - 1), skip_group_check=True),
                              reads=[r_pt, r_v], writes=[r_po])
                    if j == nk_max - 1:
                        rc, r_rc = rec.next()
                        t_, r_t = et.next()
                        fw.op("dve", lambda e: e.reciprocal(out=rc[:, 0:nq], in_=pov[:, :, dv]), reads=[r_po], writes=[r_rc])
                        fw.op("dve", lambda e: e.tensor_tensor(out=t_[:], in0=pov[:, :, 0:dv],
                                                              in1=rc[:, 0:nq].unsqueeze(2).to_broadcast([128, nq, dv]), op=ALU.mult),
                              reads=[r_po, r_rc], writes=[r_t])
                        fw.op("dve", lambda e: e.tensor_tensor(out=yc[:, :, h * dv:(h + 1) * dv], in0=t_[:], in1=yc[:, :, h * dv:(h + 1) * dv], op=ALU.mult),
                              reads=[r_t, r_yc], writes=[r_yc])

                queue = []
                for (h, j) in steps:
                    if j == 0:
                        po, r_po = bank("o")
                        state["pov"] = (po[:, 0:nq * dv1].rearrange("p (s d) -> p s d", s=nq), r_po)
                    pov, r_po = state["pov"]
                    tiles = [i for i in range(t_lo, t_lo + nq) if j < n_ktiles_of(i)]
                    col0 = (tiles[0] - t_lo) * 128
                    ncol = nq * 128 - col0
                    pss, r_pss = bank("s")
                    biases = [(i, bias_of(i, j)) for i in tiles]
                    biases = [(i, b) for i, b in biases if b is not None]
                    fw.op("pe", lambda e: e.matmul(pss[:, col0:col0 + ncol], lhsT=kT_of(h)[:, j * 128:(j + 1) * 128],
                                                   rhs=qT_of(h)[:, col0:nq * 128],
                                                   start=True, stop=(len(biases) == 0), skip_group_check=True),
                          reads=[r_k, r_q], writes=[r_pss])
                    for bi, (i, (b_ap, r_b)) in enumerate(biases):
                        cc = (i - t_lo) * 128
                        fw.op("pe", lambda e: e.matmul(pss[:, cc:cc + 128], lhsT=b_ap, rhs=bigI[:, :],
                                                       start=False, stop=(bi == len(biases) - 1), skip_group_check=True),
                              reads=[r_b, r_c], writes=[r_pss])
                    pt, r_pt = pT.next()
                    fw.op("act", lambda e: e.activation(out=pt[:, col0:col0 + ncol], in_=pss[:, col0:col0 + ncol], func=AF.Exp, scale=float(scale)),
                          reads=[r_pss], writes=[r_pt])
                    if mask_of is not None:
                        m_ap, r_m = mask_of(j, tiles[0], t_lo + nq)
                        fw.op("dve", lambda e: e.tensor_tensor(out=pt[:, col0:col0 + ncol], in0=pt[:, col0:col0 + ncol], in1=m_ap, op=ALU.mult),
                              reads=[r_pt, r_m], writes=[r_pt])
                    queue.append((h, j, tiles, pt, r_pt, pov, r_po))
                    if len(queue) > lag:
                        pv_and_epilogue(queue.pop(0))
                    if nxt is not None:
                        for _ in range(pulls):
                            if not nxt.step():
                                break
                while queue:
                    pv_and_epilogue(queue.pop(0))
                if nxt is not None:
                    cur = nxt.finish()
                for s_ in range(nq):
                    to_T(yc[:, s_, :], r_yc, 4, 128, yTn, r_yTn, t_lo + s_)

        yT[0] = galloc("yT0", [128, 4, S], BF16)
        mt_off = {}
        off = 0
        for j in range(NT):
            mt_off[j] = off
            off += (NT - j) * 128
        es_mb = ExitStack()
        maskT = sbt(es_mb, "maskT", [128, off], BF16); r_mt = Res("maskT")
        fw.op("dve", lambda e: e.tensor_copy(out=maskT[:, mt_off[0]:mt_off[0] + 128], in_=cstf[:, 280:408]), reads=[r_cstf], writes=[r_mt])
        fw.op("dve", lambda e: e.tensor_copy(out=maskT[:, mt_off[1]:mt_off[1] + 128], in_=cstf[:, 280:408]), reads=[r_cstf], writes=[r_mt])
        fw.op("dve", lambda e: e.memset(maskT[:, mt_off[0] + 128:mt_off[0] + 256], 1.0), writes=[r_mt])

        with ExitStack() as pi:
            def lalloc(name, shape, dt):
                return sbt(pi, name, shape, dt)
            qiT = lalloc("qiT", [128, 4, S], BF16); r_qiT = Res("qiT")
            kiT = lalloc("kiT", [128, 1, S], BF16); r_kiT = Res("kiT")
            wabs = lalloc("wabs", [128, NT, 8], F32)
            wsgn = lalloc("wsgn", [128, NT, 8], F32)
            r_wi = Res("wi")
            wb, r_wb = wpipe.get("ki")
            def ki_tile(i):
                pp, r_pp = bank("pj")
                mm_tok(pp[:, 0:72], r_pp, hT, r_hT, i * 128, wb, r_wb, 8, 0, 72)
                a, r_a = wk_a.next()
                fw.op("act", lambda e: e.activation(out=a[:, 0:72], in_=pp[:, 0:72], func=AF.Copy), reads=[r_pp], writes=[r_a])
                k3 = a[:, 0:64].rearrange("p (h d) -> p h d", h=1)
                rope(k3, r_a, 1, 0, 8, i, 0)
                kb, r_kb = wk_bf.next()
                fw.op("dve", lambda e: e.tensor_copy(out=kb[:, 0:64], in_=a[:, 0:64]), reads=[r_a], writes=[r_kb])
                fw.op("dve", lambda e: e.tensor_copy(out=kb[:, 64:128], in_=a[:, 0:64]), reads=[r_a], writes=[r_kb])
                to_T(kb, r_kb, 1, 128, kiT, r_kiT, i)
                cst_w = float(64 ** -0.5 * 8 ** -0.5)
                fw.op("act", lambda e: e.activation(out=wabs[:, i, :], in_=a[:, 64:72], func=AF.Abs, scale=cst_w),
                      reads=[r_a], writes=[r_wi])
                fw.op("dve", lambda e: e.tensor_scalar(out=wsgn[:, i, :], in0=a[:, 64:72], scalar1=0.0, scalar2=2.0, op0=ALU.is_ge, op1=ALU.mult),
                      reads=[r_a], writes=[r_wi])
                fw.op("dve", lambda e: e.tensor_scalar(out=wsgn[:, i, :], in0=wsgn[:, i, :], scalar1=-1.0, scalar2=None, op0=ALU.add),
                      reads=[r_wi], writes=[r_wi])
            wd = Widen([wk_a, wk_bf, rp], 2)
            run_pairs(ki_tile, list(range(NT)), width=4)
            wb, r_wb = wpipe.get("qi")
            def qi_tile(i):
                pp, r_pp = bank("pj")
                mm_tok(pp[:, 0:512], r_pp, hT, r_hT, i * 128, wb, r_wb, 8, 0, 512)
                a, r_a = wk_a.next()
                fw.op("act", lambda e: e.activation(out=a[:, 0:512], in_=pp[:, 0:512], func=AF.Copy), reads=[r_pp], writes=[r_a])
                q3 = a[:, 0:512].rearrange("p (h d) -> p h d", h=8)
                rope(q3, r_a, 8, 0, 8, i, 0)
                qb, r_qb = wk_bf.next()
                fw.op("dve", lambda e: e.tensor_copy(out=qb[:, 0:512], in_=a[:, 0:512]), reads=[r_a], writes=[r_qb])
                to_T(qb, r_qb, 4, 128, qiT, r_qiT, i)
            run_pairs(qi_tile, list(range(NT)), width=4)
            wd.close()
            if STOP == 2:
                raise _Stop()
            NLI = 3
            irow = Ring(lalloc, "irow", [128, S], F32, NLI)
            jd = Ring(lalloc, "jd", [128, S], BF16, NLI)
            ja_res = [Res("ja%d" % k) for k in range(NLI)]
            bm = Ring(lalloc, "bm", [128, 1], F32, NLI)
            bc = Ring(lalloc, "bc", [128, 2], F32, NLI)
            ba = Ring(lalloc, "ba", [128, 1], F32, NLI)
            MID0 = 0.0031415927

            rts = [Ring(lalloc, "rt%d_" % k, [128, 512], BF16, 2) for k in range(NLI)]
            dsg = Ring(lalloc, "dsg", [128, 8, 128], BF16, NLI)

            def idx_tile(i):
                L = (i + 1) * 128
                ln = LANE.get(threading.get_ident(), 0)
                ir, r_ir = irow.next()
                dg, r_dg = dsg.next()
                fw.op("dve", lambda e: e.tensor_tensor(out=dg[:, :, :], in0=identb[:, :].unsqueeze(1).to_broadcast([128, 8, 128]),
                                                      in1=wsgn[:, i, :].unsqueeze(2).to_broadcast([128, 8, 128]), op=ALU.mult),
                      reads=[r_c, r_wi], writes=[r_dg])
                for kc0 in range(0, L, 512):
                    n = min(512, L - kc0)
                    pacc, r_pacc = banks[3 + ln], bres[3 + ln]
                    pds = [(banks[ln], bres[ln]), (banks[ln], bres[ln])]
                    pend = None
                    for h in range(8):
                        hp = (h % 2) * 64
                        pd, r_pd = pds[h % 2]
                        fw.op("pe", lambda e: e.matmul(pd[:, 0:n], lhsT=qiT[hp:hp + 64, h // 2, i * 128:(i + 1) * 128],
                                                       rhs=kiT[hp:hp + 64, 0, kc0:kc0 + n], start=True, stop=True),
                              reads=[r_qiT, r_kiT], writes=[r_pd])
                        rt, r_rt = rts[ln % NLI].t[h % 2], rts[ln % NLI].r[h % 2]
                        fw.op("act", lambda e: e.activation(out=rt[:, 0:n], in_=pd[:, 0:n], func=AF.Relu, scale=wabs[:, i, h:h + 1]),
                              reads=[r_pd, r_wi], writes=[r_rt])
                        if pend is not None:
                            ph, prt, pr_rt = pend
                            fw.op("pe", lambda e: e.matmul(pacc[:, 0:n], lhsT=dg[:, ph, :], rhs=prt[:, 0:n], start=(ph == 0), stop=False),
                                  reads=[r_dg, pr_rt], writes=[r_pacc])
                        pend = (h, rt, r_rt)
                    ph, prt, pr_rt = pend
                    fw.op("pe", lambda e: e.matmul(pacc[:, 0:n], lhsT=dg[:, ph, :], rhs=prt[:, 0:n], start=False, stop=True),
                          reads=[r_dg, pr_rt], writes=[r_pacc])
                    if kc0 + n == L:
                        d0 = i * 128 - kc0
                        if d0 > 0:
                            fw.op("dve", lambda e: e.tensor_copy(out=ir[:, kc0:kc0 + d0], in_=pacc[:, 0:d0]), reads=[r_pacc], writes=[r_ir])
                        fw.op("dve", lambda e: e.tensor_tensor(out=ir[:, i * 128:L], in0=pacc[:, d0:d0 + 128], in1=cmaskf[:, :], op=ALU.add),
                              reads=[r_pacc, r_c], writes=[r_ir])
                    else:
                        fw.op("dve", lambda e: e.tensor_copy(out=ir[:, kc0:kc0 + n], in_=pacc[:, 0:n]), reads=[r_pacc], writes=[r_ir])
                L1 = max(128, (int(L * 0.5) // 128) * 128)
                L2 = L - L1
                mid_t, r_bm = bm.next()
                c_t, r_bc = bc.next()
                a_t, r_ba = ba.next()
                jdt, r_jd = jd.next()
                jat, r_ja = jdt[:, L1:L], ja_res[ln % NLI]
                mid, cnt, stp, asum = mid_t[:, 0:1], c_t[:, 0:1], c_t[:, 1:2], a_t[:, 0:1]
                fw.op("dve", lambda e: e.memset(mid, MID0), writes=[r_bm])
                W = 32.0
                for k in range(NBIS):
                    fw.op("dve", lambda e: e.tensor_scalar(out=jdt[:, 0:L1], in0=ir[:, 0:L1], scalar1=mid, scalar2=0.0, op0=ALU.is_ge, op1=ALU.add,
                                                          accum_out=cnt), reads=[r_ir, r_bm], writes=[r_jd, r_bc])
                    fw.op("act", lambda e: e.activation(out=jat, in_=ir[:, L1:L], func=AF.Sign, bias=mid, scale=-1.0, accum_out=asum),
                          reads=[r_ir, r_bm], writes=[r_ja, r_ba])
                    fw.op("dve", lambda e: e.scalar_tensor_tensor(out=stp, in0=cnt, scalar=2.0, in1=asum, op0=ALU.mult, op1=ALU.subtract),
                          reads=[r_bc, r_ba], writes=[r_bc])
                    fw.op("dve", lambda e: e.tensor_scalar(out=stp, in0=stp, scalar1=float(511.0 - L2), scalar2=float(W / 2), op0=ALU.is_ge, op1=ALU.mult),
                          reads=[r_bc], writes=[r_bc])
                    q = W / 4 if k < NBIS - 1 else W / 2
                    fw.op("dve", lambda e: e.scalar_tensor_tensor(out=mid, in0=stp, scalar=float(-q), in1=mid, op0=ALU.add, op1=ALU.add),
                          reads=[r_bc, r_bm], writes=[r_bm])
                    W = W / 2
                fw.op("dve", lambda e: e.tensor_scalar(out=jdt[:, 0:L], in0=ir[:, 0:L], scalar1=mid, scalar2=None, op0=ALU.is_ge),
                      reads=[r_ir, r_bm], writes=[r_jd, r_ja])
                Coop.hold.add(threading.get_ident())
                for j0 in range(0, i + 1, 4):
                    nb = min(4, i + 1 - j0)
                    pb, r_pb = banks[6 + (j0 // 4) % 2], bres[6 + (j0 // 4) % 2]
                    for b in range(nb):
                        transp(pb[:, b * 128:(b + 1) * 128], r_pb, jdt[:, (j0 + b) * 128:(j0 + b + 1) * 128], r_jd)
                    for b in range(nb):
                        j = j0 + b
                        d_ = mt_off[j] + (i - j) * 128
                        if (b + i) % 2 == 0:
                            fw.op("act", lambda e: e.activation(out=maskT[:, d_:d_ + 128], in_=pb[:, b * 128:(b + 1) * 128], func=AF.Copy), reads=[r_pb], writes=[r_mt])
                        else:
                            fw.op("dve", lambda e: e.tensor_copy(out=maskT[:, d_:d_ + 128], in_=pb[:, b * 128:(b + 1) * 128]), reads=[r_pb], writes=[r_mt])
                Coop.hold.discard(threading.get_ident())

            run_pairs(idx_tile, list(range(2, NT)), width=NLI, stagger=25)
            fw.barrier()

        print("ckpt", 3, fw.tot)
        if STOP == 3:
            fw.barrier()
            raise _Stop()
        with ExitStack() as pa:
            def lalloc(name, shape, dt):
                return sbt(pa, name, shape, dt)
            kaT = lalloc("kaT", [128, 4, S], BF16)
            vau = lalloc("vau", [128, NT, 8, 65], BF16)
            qch = Ring(lalloc, "qchA", [128, 4, 512], BF16, 2)
            r_k = Res("kA"); r_v = Res("vA")
            fw.op("pool", lambda e: e.memset(vau[:, :, :, 64:65], 1.0), writes=[r_v])

            def qk_tile(i, wb, r_wb, gain_ap, dstT, r_dst, slot):
                pp, r_pp = bank("pj")
                mm_tok(pp[:, 0:512], r_pp, hT, r_hT, i * 128, wb, r_wb, 8, 0, 512)
                a, r_a = wk_a.next()
                fw.op("act", lambda e: e.activation(out=a[:, 0:512], in_=pp[:, 0:512], func=AF.Copy), reads=[r_pp], writes=[r_a])
                a3 = a[:, 0:512].rearrange("p (h d) -> p h d", h=8)
                headnorm(a3, r_a, 8, 64, gain_ap, a3, r_a)
                rope(a3, r_a, 8, 0, 8, i, 0)
                qb, r_qb = wk_bf.next()
                fw.op("dve", lambda e: e.tensor_copy(out=qb[:, 0:512], in_=a[:, 0:512]), reads=[r_a], writes=[r_qb])
                to_T(qb, r_qb, 4, 128, dstT, r_dst, slot)

            wb, r_wb = wpipe.get("va")
            for i in range(NT):
                pp, r_pp = bank("pj")
                mm_tok(pp[:, 0:512], r_pp, hT, r_hT, i * 128, wb, r_wb, 8, 0, 512)
                fw.op("act", lambda e: e.activation(out=vau[:, i, :, 0:64], in_=pp[:, 0:512].rearrange("p (h d) -> p h d", h=8), func=AF.Copy),
                      reads=[r_pp], writes=[r_v])
            wb, r_wb = wpipe.get("ka")
            wq, r_wq = wpipe.get("qa", prefetch=False)
            wd = Widen([wk_a, wk_b, wk_bf, rp], 2)
            qc0, r_qc0 = qch.next()

            def kq_item(it):
                kind, i = it
                if kind == "k":
                    qk_tile(i, wb, r_wb, G_KNA, kaT, r_k, i)
                else:
                    qk_tile(12 + i, wq, r_wq, G_QNA, qc0, r_qc0, i)
            run_pairs(kq_item, [("k", i) for i in range(NT)] + [("q", i) for i in range(4)], width=4)
            wz, r_wz = wpipe.get("za", prefetch=False)
            wd.close()

            def make_q_A(c, yc, r_yc, first):
                items = [("g", s_) for s_ in range(4)]
                if first is not None:
                    qc, r_qc = first
                else:
                    qc, r_qc = qch.next()
                    items = [x for s_ in range(4) for x in (("q", s_), ("g", s_))]

                def item(it):
                    if it[0] == "q":
                        qk_tile(c * 4 + it[1], wq, r_wq, G_QNA, qc, r_qc, it[1])
                    else:
                        gate_tile(c * 4 + it[1], it[1], yc, r_yc, wz, r_wz)
                run_pairs(item, items, stagger=14)
                return (lambda h: qc[(h % 2) * 64:(h % 2) * 64 + 64, h // 2, :]), r_qc

            def mask_A(j, i0, i1):
                o_ = mt_off[j] + (i0 - j) * 128
                return maskT[:, o_:o_ + (i1 - i0) * 128], r_mt
            attention(lalloc, make_q_A,
                      lambda h: kaT[(h % 2) * 64:(h % 2) * 64 + 64, h // 2, :],
                      r_k, vau, r_v, 8, 64, lambda i: i + 1, 4, 64 ** -0.5, lambda i, j: None, yT[0], r_yT[0], wz, r_wz,
                      mask_of=mask_A, lag=4, first_q=(qc0, r_qc0), order=[3, 2, 1, 0])
            wpipe.prefetch()
            fw.barrier()
        es_mb.close()
        print("ckpt", 4, fw.tot)
        if STOP == 4:
            fw.barrier()
            raise _Stop()
        yT[1] = galloc("yT1", [128, 4, S], BF16)

        with ExitStack() as pb:
            def lalloc(name, shape, dt):
                return sbt(pb, name, shape, dt)
            kbT = lalloc("kbT", [96, 8, S], BF16)
            vbu = lalloc("vbu", [128, NT, 8, 65], BF16)
            r_k = Res("kB"); r_v = Res("vB")
            fw.op("pool", lambda e: e.memset(vbu[:, :, :, 64:65], 1.0), writes=[r_v])
            wuq = lalloc("wuq", [128, 3, 768], BF16); r_wuq = Res("wuq")
            latT = Ring(lalloc, "latT", [128, 3, 128], BF16, 2)
            qch = Ring(lalloc, "qchB", [96, 8, 512], BF16, 1)
            pbk = ExitStack()
            wukv = sbt(pbk, "wukv", [128, 2, 1024], BF16); r_wukv = Res("wukv")
            for (dst, r_dst, src, kch, ncols, gain) in ((wuq, r_wuq, wuq_d, 3, 768, GK_CQ), (wukv, r_wukv, wukv_d, 2, 1024, GK_CKV)):
                for s0 in range(0, ncols, 128):
                    st, r_st = stg.next()
                    fw.dma("sp", st[:, 0:kch, 0:128], src[0:kch * 128, s0:s0 + 128].rearrange("(kc p) n -> p kc n", p=128), writes=[r_st])
                    fw.op("pool", lambda e: e.tensor_tensor(out=dst[:, 0:kch, s0:s0 + 128], in0=st[:, 0:kch, 0:128],
                                                           in1=gain[:, 0:kch].unsqueeze(2).to_broadcast([128, kch, 128]), op=ALU.mult),
                          reads=[r_st, r_gk], writes=[r_dst])
            if STOP == 41:
                fw.barrier()
                raise _Stop()

            def latent(i, wb, r_wb, ncols_all, nlat):
                pp, r_pp = bank("pj")
                mm_tok(pp[:, 0:ncols_all], r_pp, hT, r_hT, i * 128, wb, r_wb, 8, 0, ncols_all)
                sq, r_sq = wk_b.next()
                st_, r_sm = sm.next()
                fw.op("act", lambda e: e.activation(out=sq[:, 0:nlat], in_=pp[:, 0:nlat], func=AF.Square, accum_out=st_[:, 0:1]), reads=[r_pp], writes=[r_sq, r_sm])
                rstd_from_ss(st_[:, 0:1], r_sm, nlat, st_[:, 1:2], r_sm)
                cb_, r_cb = wk_bf.next()
                fw.op("dve", lambda e: e.tensor_scalar(out=cb_[:, 0:nlat], in0=pp[:, 0:nlat], scalar1=st_[:, 1:2], scalar2=None, op0=ALU.mult),
                      reads=[r_pp, r_sm], writes=[r_cb])
                lt, r_lt = latT.next()
                to_T(cb_, r_cb, nlat // 128, 128, lt, r_lt, 0)
                return pp, r_pp, lt, r_lt

            def ckb(k):
                if STOP == k:
                    fw.barrier()
                    raise _Stop()
            wd = Widen([wk_a, wk_b, wk_bf, rp, latT], 2)
            wb, r_wb = wpipe.get("ckv")
            def kvb_tile(i):
                pp, r_pp, lt, r_lt = latent(i, wb, r_wb, 288, 256)
                a, r_a = wk_a.next()
                a3 = a[:, 0:768].rearrange("p (h d) -> p h d", h=8)
                fw.op("act", lambda e: e.activation(out=a3[:, :, 64:96], in_=pp[:, 256:288].unsqueeze(1).to_broadcast([128, 8, 32]), func=AF.Copy),
                      reads=[r_pp], writes=[r_a])
                for half in range(2):
                    pq, r_pq = bank("pj")
                    mm_tok(pq[:, 0:512], r_pq, lt, r_lt, 0, wukv, r_wukv, 2, half * 512, 512)
                    pq3 = pq[:, 0:512].rearrange("p (h d) -> p h d", h=4)
                    fw.op("act", lambda e: e.activation(out=a3[:, half * 4:(half + 1) * 4, 0:64], in_=pq3[:, :, 0:64], func=AF.Copy), reads=[r_pq], writes=[r_a])
                    fw.op("act", lambda e: e.activation(out=vbu[:, i, half * 4:(half + 1) * 4, 0:64], in_=pq3[:, :, 64:128], func=AF.Copy), reads=[r_pq], writes=[r_v])
                headnorm(a3, r_a, 8, 96, G_KNB, a3, r_a)
                rope(a3, r_a, 8, 64, 16, i, 8)
                kb, r_kb = wk_bf.next()
                fw.op("dve", lambda e: e.tensor_copy(out=kb[:, 0:768], in_=a[:, 0:768]), reads=[r_a], writes=[r_kb])
                to_T(kb, r_kb, 8, 96, kbT, r_k, i)
            wq, r_wq = wpipe.get("cq", prefetch=False)

            def qb_tile_into(i, qc, r_qc, s_):
                pp, r_pp, lt, r_lt = latent(i, wq, r_wq, 384, 384)
                a, r_a = wk_a.next()
                for half in range(2):
                    pq, r_pq = bank("pj")
                    mm_tok(pq[:, 0:384], r_pq, lt, r_lt, 0, wuq, r_wuq, 3, half * 384, 384)
                    fw.op("act", lambda e: e.activation(out=a[:, half * 384:(half + 1) * 384], in_=pq[:, 0:384], func=AF.Copy), reads=[r_pq], writes=[r_a])
                a3 = a[:, 0:768].rearrange("p (h d) -> p h d", h=8)
                headnorm(a3, r_a, 8, 96, G_QNB, a3, r_a)
                rope(a3, r_a, 8, 64, 16, i, 8)
                qb, r_qb = wk_bf.next()
                fw.op("dve", lambda e: e.tensor_copy(out=qb[:, 0:768], in_=a[:, 0:768]), reads=[r_a], writes=[r_qb])
                to_T(qb, r_qb, 8, 96, qc, r_qc, s_)

            qc0, r_qc0 = qch.next()

            def kq_item(it):
                kind, i = it
                if kind == "k":
                    kvb_tile(i)
                else:
                    qb_tile_into(12 + i, qc0, r_qc0, i)
            run_pairs(kq_item, [("k", i) for i in range(NT)] + [("q", i) for i in range(4)], width=4)
            wd.close()
            fw.barrier()
            pbk.close()
            qch.grow(lalloc, 1)
            wz, r_wz = wpipe.get("zb", prefetch=False)

            def make_q_B(c, yc, r_yc, first):
                items = [("g", s_) for s_ in range(4)]
                if first is not None:
                    qc, r_qc = first
                else:
                    qc, r_qc = qch.next()
                    items = [x for s_ in range(4) for x in (("q", s_), ("g", s_))]

                def item(it):
                    if it[0] == "q":
                        qb_tile_into(c * 4 + it[1], qc, r_qc, it[1])
                    else:
                        gate_tile(c * 4 + it[1], it[1], yc, r_yc, wz, r_wz)
                run_pairs(item, items, stagger=14)
                return (lambda h: qc[0:96, h, :]), r_qc

            def bias_B(i, j):
                if i == j:
                    return (cb01[:, :], r_c)
                return None
            attention(lalloc, make_q_B, lambda h: kbT[0:96, h, :], r_k, vbu, r_v, 8, 64,
                      lambda i: i + 1, 4, 96 ** -0.5, bias_B, yT[1], r_yT[1], wz, r_wz, first_q=(qc0, r_qc0), order=[3, 2, 1, 0], lag=4)
            wpipe.prefetch()
            fw.barrier()
        yT[2] = galloc("yT2", [128, 4, S], BF16)
        print("ckpt", 5, fw.tot)
        if STOP == 5:
            fw.barrier()
            raise _Stop()

        with ExitStack() as pm:
            def lalloc(name, shape, dt):
                return sbt(pm, name, shape, dt)
            kmT = lalloc("kmT", [128, 4, 256], BF16)
            vmu = lalloc("vmu", [128, 2, 4, 129], BF16)
            memT = lalloc("memT", [128, 8, 256], BF16); r_memT = Res("memT")
            qmT = lalloc("qmT", [128, 4, S], BF16); r_qm = Res("qmT")
            y_all = lalloc("yallM", [128, NT, 512], BF16); r_yall = Res("yallM")
            r_k = Res("kM"); r_v = Res("vM")
            fw.op("pool", lambda e: e.memset(vmu[:, :, :, 128:129], 1.0), writes=[r_v])
            pms = ExitStack()
            xin = Ring(lambda nm, sh, dt: sbt(pms, nm, sh, dt), "min", [128, D], F32, 2)
            wd = Widen([wk_a, wk_b, wk_bf, rp], 2)
            for t in range(2):
                xt, r_xt = xin.next()
                fw.dma("sp", xt[:], mem_d[t * 128:(t + 1) * 128, :], writes=[r_xt])
                sq, r_sq = wk_bf.next()
                st_, r_sm = sm.next()
                fw.op("act", lambda e: e.activation(out=sq[:], in_=xt[:], func=AF.Square, accum_out=st_[:, 0:1]), reads=[r_xt], writes=[r_sq, r_sm])
                rstd_from_ss(st_[:, 0:1], r_sm, D, st_[:, 1:2], r_sm)
                xb, r_xb = wk_bf.next()
                fw.op("dve", lambda e: e.tensor_scalar(out=xb[:], in0=xt[:], scalar1=st_[:, 1:2], scalar2=None, op0=ALU.mult), reads=[r_xt, r_sm], writes=[r_xb])
                to_T(xb, r_xb, 8, 128, memT, r_memT, t)

            def qk_tile_m(srcT, r_src, t0, wb, r_wb, gain_ap, dstT, r_dst, slot):
                pp, r_pp = bank("pj")
                mm_tok(pp[:, 0:512], r_pp, srcT, r_src, t0, wb, r_wb, 8, 0, 512)
                a, r_a = wk_a.next()
                fw.op("act", lambda e: e.activation(out=a[:, 0:512], in_=pp[:, 0:512], func=AF.Copy), reads=[r_pp], writes=[r_a])
                a3 = a[:, 0:512].rearrange("p (h d) -> p h d", h=4)
                headnorm(a3, r_a, 4, 128, gain_ap, a3, r_a)
                kb, r_kb = wk_bf.next()
                fw.op("dve", lambda e: e.tensor_copy(out=kb[:, 0:512], in_=a[:, 0:512]), reads=[r_a], writes=[r_kb])
                to_T(kb, r_kb, 4, 128, dstT, r_dst, slot)

            wb, r_wb = wpipe.get("km")
            for t in range(2):
                qk_tile_m(memT, r_memT, t * 128, wb, r_wb, G_KNM, kmT, r_k, t)
            wb, r_wb = wpipe.get("vm")
            for t in range(2):
                pp, r_pp = bank("pj")
                mm_tok(pp[:, 0:512], r_pp, memT, r_memT, t * 128, wb, r_wb, 8, 0, 512)
                fw.op("act", lambda e: e.activation(out=vmu[:, t, :, 0:128], in_=pp[:, 0:512].rearrange("p (h d) -> p h d", h=4), func=AF.Copy),
                      reads=[r_pp], writes=[r_v])
            wq, r_wq = wpipe.get("qm")
            wz, r_wz = wpipe.get("zm", prefetch=False)

            def mq_tile(i):
                qk_tile_m(hT, r_hT, i * 128, wq, r_wq, G_QNM, qmT, r_qm, i)
                gate_tile(i, i, y_all, r_yall, wz, r_wz)
            run_pairs(mq_tile, list(range(NT)), width=4)
            wd.close()
            pms.close()

            def make_q_M(c, yc, r_yc, first):
                return (lambda h: qmT[:, h, c * 256:(c + 1) * 256]), r_qm, y_all[:, 2 * c:2 * c + 2, :], r_yall
            attention(lalloc, make_q_M, lambda h: kmT[:, h, :], r_k, vmu, r_v, 4, 128,
                      lambda i: 2, 2, 128 ** -0.5, lambda i, j: None, yT[2], r_yT[2], wz, r_wz, lag=2)
            fw.barrier()

        print("ckpt", 6, fw.tot)
        if STOP == 6:
            fw.barrier()
            raise _Stop()
        with ExitStack() as pf:
            def lalloc(name, shape, dt):
                return sbt(pf, name, shape, dt)
            mT = lalloc("mT", [128, 8, S], BF16); r_mT = Res("mT")
            pfd = ExitStack()

            def dalloc(name, shape, dt):
                return sbt(pfd, name, shape, dt)
            wbrs = Ring(dalloc, "wbrs", [128, 3, 4, 128], BF16, 2)
            sg = Ring(dalloc, "sg", [128, 512], F32, 2)
            tmp = Ring(dalloc, "tmpf", [128, 512], F32, 2)
            macc = Ring(dalloc, "macc", [128, 512], F32, 2)
            fw.barrier()
            gst2 = dalloc("gst2", [128, 8, 128], F32)
            gst = [stg.t[0][:, :, :], stg.t[1][:, :, :], gst2[:, :, :]]
            r_gst = [stg.r[0], stg.r[1], Res("gst2")]
            bst = dalloc("bst", [128, 3, 4, 128], F32); r_bst = Res("bst")

            def issue_dc(dc):
                for n in range(3):
                    c0 = C_G + n * 1024 + dc * 128
                    fw.dma("sp", gst[n], win_d[0:1024, c0:c0 + 128].rearrange("(kc p) n -> p kc n", p=128), writes=[r_gst[n]])
                for n in range(3):
                    fw.dma("sp", bst[:, n, :, :], wbr_d[n * 512:(n + 1) * 512, dc * 128:(dc + 1) * 128].rearrange("(kc p) n -> p kc n", p=128), writes=[r_bst])

            def cast_dc(dc):
                wg, r_wg = wbr_.next()
                for n in range(3):
                    fw.op("dve", lambda e: e.tensor_tensor(out=wg[:, 0:8, n * 128:(n + 1) * 128], in0=gst[n],
                                                          in1=GK_NORM.unsqueeze(2).to_broadcast([128, 8, 128]), op=ALU.mult),
                          reads=[r_gst[n], r_gk], writes=[r_wg])
                wbn, r_wbn = wbrs.next()
                fw.op("dve", lambda e: e.tensor_copy(out=wbn[:], in_=bst[:]), reads=[r_bst], writes=[r_wbn])
                return wg, r_wg, wbn, r_wbn

            issue_dc(0)
            nxt_w = cast_dc(0)
            for dc in range(8):
                fw.maybe_barrier()
                wg, r_wg, wbn, r_wbn = nxt_w
                if dc + 1 < 8:
                    issue_dc(dc + 1)
                else:
                    wout0 = load_w(wout_d, 0, 8, 0, 512, None)
                for tc in range(4):
                    if tc == 2 and dc + 1 < 8:
                        nxt_w = cast_dc(dc + 1)
                    ma, r_ma = macc.next()
                    for n in range(3):
                        pg, r_pg = bank("pj")
                        for kc in range(8):
                            fw.op("pe", lambda e, kc=kc: e.matmul(pg[:, 0:512], lhsT=wg[:, kc, n * 128:(n + 1) * 128], rhs=hT[:, kc, tc * 512:(tc + 1) * 512],
                                                                  start=(kc == 0), stop=(kc == 7)), reads=[r_wg, r_hT], writes=[r_pg])
                        s1, r_s1 = sg.next()
                        fw.op("act", lambda e: e.activation(out=s1[:], in_=pg[:, 0:512], func=AF.Tanh, scale=0.5), reads=[r_pg], writes=[r_s1])
                        pbr, r_pbr = bank("s")
                        for kw in range(4):
                            fw.op("pe", lambda e, kw=kw: e.matmul(pbr[:, 0:512], lhsT=wbn[:, n, kw, :], rhs=yT[n][:, kw, tc * 512:(tc + 1) * 512],
                                                                  start=(kw == 0), stop=(kw == 3)), reads=[r_wbn, r_yT[n]], writes=[r_pbr])
                        if n == 0:
                            fw.op("dve", lambda e: e.scalar_tensor_tensor(out=ma[:], in0=s1[:], scalar=1.0, in1=pbr[:, 0:512], op0=ALU.add, op1=ALU.mult),
                                  reads=[r_s1, r_pbr], writes=[r_ma])
                        else:
                            t1, r_t1 = tmp.next()
                            fw.op("dve", lambda e: e.scalar_tensor_tensor(out=t1[:], in0=s1[:], scalar=1.0, in1=pbr[:, 0:512], op0=ALU.add, op1=ALU.mult),
                                  reads=[r_s1, r_pbr], writes=[r_t1])
                            fw.op("dve", lambda e: e.tensor_tensor(out=ma[:], in0=ma[:], in1=t1[:], op=ALU.add), reads=[r_ma, r_t1], writes=[r_ma])
                    fw.op("act", lambda e: e.activation(out=mT[:, dc, tc * 512:(tc + 1) * 512], in_=ma[:], func=AF.Copy, scale=0.5), reads=[r_ma], writes=[r_mT])
            fw.barrier()
            pfd.close()
            xin = Ring(lalloc, "xres", [128, 512], F32, 4)
            oo = Ring(lalloc, "oo", [128, 512], F32, 4)
            wouts = [wout0, load_w(wout_d, 0, 8, 512, 512, None)]
            for half in range(2):
                wb, r_wb = wouts[half]
                for i in range(NT):
                    xt, r_xt = xin.next()
                    fw.dma("sp", xt[:], x_d[i * 128:(i + 1) * 128, half * 512:(half + 1) * 512], writes=[r_xt])
                    pp, r_pp = bank("pj")
                    mm_tok(pp[:, 0:512], r_pp, mT, r_mT, i * 128, wb, r_wb, 8, 0, 512)
                    ot, r_ot = oo.next()
                    fw.op("dve", lambda e: e.tensor_tensor(out=ot[:], in0=pp[:, 0:512], in1=xt[:], op=ALU.add), reads=[r_pp, r_xt], writes=[r_ot])
                    fw.dma("act", out_d[i * 128:(i + 1) * 128, half * 512:(half + 1) * 512], ot[:], reads=[r_ot])
            fw.barrier()
        print("inst counts", fw.tot, "sems", fw.nsem, "epochs", fw.epoch)


_NC_CACHE = {}


def _host_consts():
    ident = np.eye(128, dtype=np.float32)
    q = np.arange(128)[:, None]
    k = np.arange(128)[None, :]
    cb = np.where(k > q, -1.0, 0.0).astype(np.float32)
    theta = 500000.0
    fa = theta ** (-(np.arange(0, 16, 2, dtype=np.float32) / 16.0))
    fb = theta ** (-(np.arange(0, 32, 2, dtype=np.float32) / 32.0))
    invf = (np.concatenate([fa, fb]).astype(np.float64) / (2 * np.pi)).astype(np.float32)
    c01t = np.where(q <= k, 1.0, 0.0).astype(np.float32)
    cst = np.concatenate([ident, cb, np.broadcast_to(invf[None, :], (128, 24)), c01t], axis=1)
    return np.ascontiguousarray(cst, dtype=np.float32)


def kernel(x, mem, positions, g_norm, w_in, g_qn_a, g_kn_a, g_cq, g_ckv, w_uq, w_ukv,
           g_qn_b, g_kn_b, g_mem, w_mem_kv, g_qn_m, g_kn_m, w_branch, w_out):
    f = lambda a: np.ascontiguousarray(np.asarray(a), dtype=np.float32)
    x = f(x); mem = f(mem)
    positions = np.asarray(positions).astype(np.int32)
    n = 8
    if "nc" not in _NC_CACHE:
        _NC_CACHE["nc"] = build_program(stop=int(os.environ.get("KSTOP", "99")))
    nc = _NC_CACHE["nc"]
    gk = np.concatenate([f(g_norm)[0].reshape(8, 128).T, f(g_cq)[0].reshape(3, 128).T,
                         f(g_ckv)[0].reshape(2, 128).T, f(g_mem)[0].reshape(8, 128).T], axis=1)
    ghv = np.concatenate([f(g_qn_a)[0], f(g_kn_a)[0], f(g_qn_b)[0], f(g_kn_b)[0], f(g_qn_m)[0], f(g_kn_m)[0]])
    gh = np.broadcast_to(ghv[None, :], (128, 576))
    shared = {
        "w_in": f(w_in)[0], "w_uq": f(w_uq)[0], "w_ukv": f(w_ukv)[0], "w_mem": f(w_mem_kv)[0],
        "w_br": f(w_branch)[0].reshape(3 * 512, D), "w_out": f(w_out)[0],
        "gk": np.ascontiguousarray(gk, dtype=np.float32), "gh": np.ascontiguousarray(gh, dtype=np.float32),
        "cst": _host_consts(),
    }
    in_maps = []
    for b in range(n):
        m = dict(shared)
        m["x"] = x[b]
        m["mem"] = mem[b]
        m["pos"] = np.ascontiguousarray(positions[b].reshape(NT, 128).T)
        in_maps.append(m)
    res = run_bass_kernel_spmd(nc, in_maps, core_ids=list(range(n)))
    return np.stack([np.asarray(r["out"], dtype=np.float32) for r in res.results], axis=0)
```

```python
import os
import threading
from contextlib import ExitStack
import numpy as np
import concourse.bass as bass
import concourse.mybir as mybir
from concourse.alu_op_type import AluOpType as ALU
from concourse.bass_utils import run_bass_kernel_spmd

F32 = mybir.dt.float32
BF16 = mybir.dt.bfloat16
I32 = mybir.dt.int32
AF = mybir.ActivationFunctionType
AX = mybir.AxisListType

S = 2048
D = 1024
NT = 16
D_IN = 7912
EPS = 1e-6
BIG = 30000.0
C_QA, C_KA, C_VA, C_QI, C_KI, C_WI, C_ZA = 0, 512, 1024, 1536, 2048, 2112, 2120
C_CQ, C_CKV, C_KR, C_ZB, C_QM, C_ZM, C_G = 2632, 3016, 3272, 3304, 3816, 4328, 4840
NBIS = 20


DBG_NAMES = {}


class Res:
    __slots__ = ("name", "w", "r", "sem", "cnt")

    def __init__(self, name):
        self.name = name
        self.w = None
        self.r = {}
        self.sem = None
        self.cnt = 0


class FW:
    ENG = ("pe", "act", "dve", "pool", "sp")

    def __init__(self, nc, es):
        self.nc = nc
        self.es = es
        self.eng = {"pe": nc.tensor, "act": nc.scalar, "dve": nc.vector,
                    "pool": nc.gpsimd, "sp": nc.sync}
        self.epoch = 0
        self.nsem = 0
        self._new_sems()
        self.tot = {e: 0 for e in self.ENG}
        self.waited = {e: {} for e in self.ENG}
        self.dma_res = []
        self.uid = 0

    def _new_sems(self):
        self.sem = {e: self.es.enter_context(self.nc.semaphore("s%d_%s" % (self.epoch, e))) for e in self.ENG}
        self.cnt = {e: 0 for e in self.ENG}
        self.nsem += len(self.ENG)

    def _wait(self, e, ev):
        if ev is None:
            return
        key, semh, val, ep = ev
        if ep is not None and ep < self.epoch:
            return
        if key == e and e == "pe":
            return
        w = self.waited[e]
        if w.get(key, 0) >= val:
            return
        w[key] = val
        self.eng[e].wait_ge(semh, val)

    def _deps(self, e, reads, writes):
        for r in reads:
            self._wait(e, r.w)
        for w in writes:
            self._wait(e, w.w)
            for k, ev in w.r.items():
                if k == e:
                    continue
                self._wait(e, ev)

    def _record(self, ev, reads, writes):
        for r in reads:
            r.r[ev[0]] = ev
        for w in writes:
            w.w = ev
            w.r = {}

    def op(self, e, fn, reads=(), writes=()):
        self._deps(e, reads, writes)
        inst = fn(self.eng[e])
        self.cnt[e] += 1
        self.tot[e] += 1
        inst.then_inc(self.sem[e], 1)
        self._record((e, self.sem[e], self.cnt[e], self.epoch), reads, writes)
        Coop.yield_point()

    def dma(self, q, out, in_, reads=(), writes=()):
        self._deps(q, reads, writes)
        tgt = writes[0] if writes else reads[0]
        if tgt.sem is None:
            self.uid += 1
            tgt.sem = self.es.enter_context(self.nc.semaphore("d%d" % self.uid))
            tgt.name = "d%d_%s" % (self.uid, tgt.name)
            self.dma_res.append(tgt)
            self.nsem += 1
        inst = self.eng[q].dma_start(out=out, in_=in_)
        tgt.cnt += 16
        inst.then_inc(tgt.sem, 16)
        self._record((tgt.name, tgt.sem, tgt.cnt, None), reads, writes)
        Coop.yield_point()

    def barrier(self):
        for e in self.ENG:
            for e2 in self.ENG:
                if e2 != e and self.cnt[e2] > 0:
                    self._wait(e, (e2, self.sem[e2], self.cnt[e2], self.epoch))
            for r in self.dma_res:
                if r.cnt > 0:
                    self._wait(e, (r.name, r.sem, r.cnt, None))
        self.epoch += 1
        self._new_sems()
        for e in self.ENG:
            for k in self.ENG:
                self.waited[e].pop(k, None)

    def maybe_barrier(self, limit=100000):
        if max(self.cnt.values()) > limit:
            self.barrier()


class _Stop(Exception):
    pass


class Coop:
    current = {}

    def __init__(self, fn):
        self.go = threading.Semaphore(0)
        self.done = threading.Semaphore(0)
        self.finished = False
        self.exc = None
        self.result = None

        def body():
            self.go.acquire()
            try:
                self.result = fn()
            except BaseException as e:
                self.exc = e
            self.finished = True
            Coop.current.pop(threading.get_ident(), None)
            self.done.release()
        self.t = threading.Thread(target=body)
        self.t.start()
        Coop.current[self.t.ident] = self

    def step(self):
        if self.finished:
            return False
        self.go.release()
        self.done.acquire()
        if self.exc is not None:
            raise self.exc
        if not self.finished:
            Coop.yield_point()
        return not self.finished

    def finish(self):
        while self.step():
            pass
        self.t.join()
        return self.result

    hold = set()

    @staticmethod
    def yield_point():
        c = Coop.current.get(threading.get_ident())
        if c is not None and threading.get_ident() not in Coop.hold:
            c.done.release()
            c.go.acquire()


class Ring:
    def __init__(self, alloc, name, shape, dt, n):
        self.name, self.shape, self.dt = name, shape, dt
        self.t = [alloc(name + str(k), shape, dt) for k in range(n)]
        self.r = [Res(name + str(k)) for k in range(n)]
        self.i = 0

    def grow(self, alloc, n):
        for _ in range(n):
            k = len(self.t)
            self.t.append(alloc(self.name + "x" + str(k), self.shape, self.dt))
            self.r.append(Res(self.name + "x" + str(k)))

    def shrink(self, n):
        for _ in range(n):
            self.t.pop()
            self.r.pop()

    def next(self):
        ln = LANE.get(threading.get_ident())
        if ln is not None and len(self.t) > 1:
            k = ln % len(self.t)
        else:
            k = self.i % len(self.t)
            self.i += 1
        return self.t[k], self.r[k]


LANE = {}


def run_pairs(fn, items, width=int(os.environ.get("KWIDTH", "2")), stagger=12):
    pending = list(items)
    active = [None] * width
    started = [False] * width
    steps = 0
    while pending or any(a is not None for a in active):
        for ln in range(width):
            if active[ln] is None and pending and (ln == 0 or started[ln] or steps >= stagger * ln):
                it = pending.pop(0)

                def body(it=it, ln=ln):
                    LANE[threading.get_ident()] = ln
                    try:
                        return fn(it)
                    finally:
                        LANE.pop(threading.get_ident(), None)
                active[ln] = Coop(body)
                started[ln] = True
            if active[ln] is not None:
                if not active[ln].step():
                    active[ln].finish()
                    active[ln] = None
        steps += 1


def build_program(dbg=False, stop=99):
    nc = bass.Bass("TRN2", target_bir_lowering=False)
    try:
        _build(nc, stop)
    except _Stop:
        pass
    return nc


def _build(nc, STOP):

    def dram(name, shape, dt=F32, kind="ExternalInput"):
        return nc.dram_tensor(name, list(shape), dt, kind=kind).ap()

    x_d = dram("x", [S, D])
    mem_d = dram("mem", [256, D])
    pos_d = dram("pos", [128, NT], I32)
    win_d = dram("w_in", [D, D_IN])
    wuq_d = dram("w_uq", [384, 768])
    wukv_d = dram("w_ukv", [256, 1024])
    wmem_d = dram("w_mem", [D, 1024])
    wbr_d = dram("w_br", [3 * 512, D])
    wout_d = dram("w_out", [D, D])
    gk_d = dram("gk", [128, 21])
    gh_d = dram("gh", [128, 576])
    cst_d = dram("cst", [128, 408])
    out_d = dram("out", [S, D], F32, kind="ExternalOutput")

    with ExitStack() as es:
        fw = FW(nc, es)

        uniq = [0]

        def sbt(stack, name, shape, dt):
            uniq[0] += 1
            nm = "sb%d_%s" % (uniq[0], name)
            DBG_NAMES[name] = nm
            return stack.enter_context(nc.sbuf_tensor(nm, list(shape), dt))

        def galloc(name, shape, dt):
            return sbt(es, name, shape, dt)

        banks = [es.enter_context(nc.psum_tensor("bank%d" % k, [128, 512], F32)) for k in range(8)]
        bres = [Res("bank%d" % k) for k in range(8)]
        rr = {"pj": 0, "tr": 0, "s": 0, "o": 0}
        role = {"pj": (0, 1), "tr": (2, 3), "s": (4, 5), "o": (6, 7)}

        def bank(kind):
            ln = LANE.get(threading.get_ident())
            if ln is not None and ln >= 2:
                k = {"pj": 4, "tr": 6, "s": 5, "o": 7}[kind] if ln == 2 else {"pj": 5, "tr": 7, "s": 4, "o": 6}[kind]
            elif ln is not None:
                k = role[kind][ln % 2]
            else:
                k = role[kind][rr[kind] % 2]
                rr[kind] += 1
            return banks[k], bres[k]

        cstf = galloc("cstf", [128, 408], F32); r_cstf = Res("cstf")
        gk = galloc("gk", [128, 21], F32); r_gk = Res("gk")
        gh = galloc("gh", [128, 576], F32); r_gh = Res("gh")
        posi = galloc("posi", [128, NT], I32); r_posi = Res("posi")
        fw.dma("sp", cstf[:], cst_d[:, :], writes=[r_cstf])
        fw.dma("sp", gk[:], gk_d[:, :], writes=[r_gk])
        fw.dma("sp", gh[:], gh_d[:, :], writes=[r_gh])
        fw.dma("sp", posi[:], pos_d[:, :], writes=[r_posi])
        identb = galloc("identb", [128, 128], BF16)
        bigI = galloc("bigI", [128, 128], BF16)
        cb01 = galloc("cb01", [128, 128], BF16)
        cmaskf = galloc("cmaskf", [128, 128], F32)
        epst = galloc("epst", [128, 1], F32)
        r_c = Res("consts")
        fw.op("dve", lambda e: e.tensor_copy(out=identb[:], in_=cstf[:, 0:128]), reads=[r_cstf], writes=[r_c])
        fw.op("dve", lambda e: e.tensor_scalar(out=bigI[:], in0=cstf[:, 0:128], scalar1=BIG, scalar2=None, op0=ALU.mult), reads=[r_cstf], writes=[r_c])
        fw.op("dve", lambda e: e.tensor_copy(out=cb01[:], in_=cstf[:, 128:256]), reads=[r_cstf], writes=[r_c])
        fw.op("dve", lambda e: e.tensor_scalar(out=cmaskf[:], in0=cstf[:, 128:256], scalar1=1e30, scalar2=None, op0=ALU.mult), reads=[r_cstf], writes=[r_c])
        fw.op("dve", lambda e: e.memset(epst[:], EPS), writes=[r_c])
        invf = cstf[:, 256:280]
        G_QNA, G_KNA, G_QNB, G_KNB, G_QNM, G_KNM = gh[:, 0:64], gh[:, 64:128], gh[:, 128:224], gh[:, 224:320], gh[:, 320:448], gh[:, 448:576]
        GK_NORM, GK_CQ, GK_CKV, GK_MEM = gk[:, 0:8], gk[:, 8:11], gk[:, 11:13], gk[:, 13:21]

        SIN = galloc("SIN", [128, NT, 24], F32)
        COS = galloc("COS", [128, NT, 24], F32)
        r_rope = Res("rope")
        with ExitStack() as ps_:
            posf = sbt(ps_, "posf", [128, NT], F32)
            u = sbt(ps_, "rp_u", [128, NT, 24], F32)
            v = sbt(ps_, "rp_v", [128, NT, 24], F32)
            ki = sbt(ps_, "rp_ki", [128, NT, 24], I32)
            kf = sbt(ps_, "rp_kf", [128, NT, 24], F32)
            m1 = sbt(ps_, "rp_m1", [128, NT, 24], F32)
            r_t = Res("rp")
            fw.op("dve", lambda e: e.tensor_copy(out=posf[:], in_=posi[:]), reads=[r_posi], writes=[r_t])
            fw.op("dve", lambda e: e.tensor_tensor(out=u[:], in0=posf[:].unsqueeze(2).to_broadcast([128, NT, 24]),
                                                  in1=invf.unsqueeze(1).to_broadcast([128, NT, 24]), op=ALU.mult),
                  reads=[r_t, r_cstf], writes=[r_t])
            for shift, dst in ((0.0, SIN), (0.25, COS)):
                fw.op("dve", lambda e: e.tensor_scalar(out=v[:], in0=u[:], scalar1=shift, scalar2=None, op0=ALU.add), reads=[r_t], writes=[r_t])
                fw.op("dve", lambda e: e.tensor_copy(out=ki[:], in_=v[:]), reads=[r_t], writes=[r_t])
                fw.op("dve", lambda e: e.tensor_copy(out=kf[:], in_=ki[:]), reads=[r_t], writes=[r_t])
                fw.op("dve", lambda e: e.tensor_tensor(out=v[:], in0=v[:], in1=kf[:], op=ALU.subtract), reads=[r_t], writes=[r_t])
                fw.op("dve", lambda e: e.tensor_scalar(out=m1[:], in0=v[:], scalar1=0.5, scalar2=None, op0=ALU.is_gt), reads=[r_t], writes=[r_t])
                fw.op("dve", lambda e: e.tensor_tensor(out=v[:], in0=v[:], in1=m1[:], op=ALU.subtract), reads=[r_t], writes=[r_t])
                fw.op("dve", lambda e: e.tensor_scalar(out=m1[:], in0=v[:], scalar1=-0.5, scalar2=None, op0=ALU.is_lt), reads=[r_t], writes=[r_t])
                fw.op("dve", lambda e: e.tensor_tensor(out=v[:], in0=v[:], in1=m1[:], op=ALU.add), reads=[r_t], writes=[r_t])
                fw.op("act", lambda e, dst=dst: e.activation(out=dst[:], in_=v[:], func=AF.Sin, scale=float(2 * np.pi * (1 - 1e-6))),
                      reads=[r_t], writes=[r_rope])
            fw.barrier()

        print("ckpt", 0, fw.tot)
        if STOP == 0:
            fw.barrier()
            raise _Stop()
        hT = galloc("hT", [128, 8, S], BF16); r_hT = Res("hT")
        yT = [None, None, None]
        r_yT = [Res("yT%d" % n) for n in range(3)]

        stg = Ring(galloc, "stg", [128, 8, 128], F32, 2)
        SUBC = 128
        wbr_ = Ring(galloc, "wb", [128, 8, 512], BF16, 2)
        cast_rr = [0]

        def load_w(src, r0, kch, c0, ncols, gain=None):
            wb, r_wb = wbr_.next()
            for s0 in range(0, ncols, SUBC):
                n = min(SUBC, ncols - s0)
                st, r_st = stg.next()
                fw.dma("sp", st[:, 0:kch, 0:n],
                       src[r0:r0 + kch * 128, c0 + s0:c0 + s0 + n].rearrange("(kc p) n -> p kc n", p=128),
                       writes=[r_st])
                if gain is None:
                    fw.op("pool", lambda e: e.tensor_copy(out=wb[:, 0:kch, s0:s0 + n], in_=st[:, 0:kch, 0:n]), reads=[r_st], writes=[r_wb])
                else:
                    fw.op("pool", lambda e: e.tensor_tensor(out=wb[:, 0:kch, s0:s0 + n], in0=st[:, 0:kch, 0:n],
                                                           in1=gain[:, 0:kch].unsqueeze(2).to_broadcast([128, kch, n]), op=ALU.mult),
                          reads=[r_st, r_gk], writes=[r_wb])
            return wb, r_wb

        class WPipe:
            def __init__(self, specs):
                self.specs = specs
                self.k = 0
                self.loaded = {}

            def _load(self, k):
                if k < len(self.specs) and k not in self.loaded:
                    self.loaded[k] = self.specs[k][1]()

            def prefetch(self):
                self._load(self.k)

            def get(self, name, prefetch=True):
                assert self.specs[self.k][0] == name, (self.specs[self.k][0], name)
                self._load(self.k)
                res = self.loaded.pop(self.k)
                self.k += 1
                if prefetch:
                    self._load(self.k)
                return res

        wpipe = WPipe([
            ("ki", lambda: load_w(win_d, 0, 8, C_KI, 72, GK_NORM)),
            ("qi", lambda: load_w(win_d, 0, 8, C_QI, 512, GK_NORM)),
            ("va", lambda: load_w(win_d, 0, 8, C_VA, 512, GK_NORM)),
            ("ka", lambda: load_w(win_d, 0, 8, C_KA, 512, GK_NORM)),
            ("qa", lambda: load_w(win_d, 0, 8, C_QA, 512, GK_NORM)),
            ("za", lambda: load_w(win_d, 0, 8, C_ZA, 512, GK_NORM)),
            ("ckv", lambda: load_w(win_d, 0, 8, C_CKV, 288, GK_NORM)),
            ("cq", lambda: load_w(win_d, 0, 8, C_CQ, 384, GK_NORM)),
            ("zb", lambda: load_w(win_d, 0, 8, C_ZB, 512, GK_NORM)),
            ("km", lambda: load_w(wmem_d, 0, 8, 0, 512, GK_MEM)),
            ("vm", lambda: load_w(wmem_d, 0, 8, 512, 512, GK_MEM)),
            ("qm", lambda: load_w(win_d, 0, 8, C_QM, 512, GK_NORM)),
            ("zm", lambda: load_w(win_d, 0, 8, C_ZM, 512, GK_NORM)),
        ])

        def mm_tok(ps_ap, r_ps, srcT, r_src, t0, wb, r_wb, kch, c0, n):
            for kc in range(kch):
                fw.op("pe", lambda e, kc=kc: e.matmul(ps_ap, lhsT=srcT[:, kc, t0:t0 + 128], rhs=wb[:, kc, c0:c0 + n],
                                                      start=(kc == 0), stop=(kc == kch - 1)),
                      reads=[r_src, r_wb], writes=[r_ps])

        def transp(ps_ap, r_ps, src_ap, r_src):
            fw.op("pe", lambda e: e.matmul(ps_ap, lhsT=src_ap, rhs=identb[:, :], start=True, stop=True),
                  reads=[r_src, r_c], writes=[r_ps])

        class Widen:
            def __init__(self, rings, extra):
                self.rings, self.extra = rings, extra
                self.stack = ExitStack()
                for r in rings:
                    r.grow(lambda nm, sh, dt: sbt(self.stack, nm, sh, dt), extra)

            def close(self):
                fw.barrier()
                for r in self.rings:
                    r.shrink(self.extra)
                self.stack.close()

        wk_a = Ring(galloc, "wka", [128, 768], F32, 2)
        wk_b = Ring(galloc, "wkb", [128, 768], F32, 2)
        wk_bf = Ring(galloc, "wkbf", [128, 1024], BF16, 2)
        sm = Ring(galloc, "sm", [128, 16], F32, 4)
        rp = Ring(galloc, "rp", [128, 4, 128], F32, 2)

        def rstd_from_ss(ss_ap, r_ss, n_el, out_ap, r_out):
            fw.op("act", lambda e: e.activation(out=out_ap, in_=ss_ap, func=AF.Ln, bias=epst[:], scale=1.0 / n_el),
                  reads=[r_ss, r_c], writes=[r_out])
            fw.op("act", lambda e: e.activation(out=out_ap, in_=out_ap, func=AF.Exp, scale=-0.5), reads=[r_out], writes=[r_out])

        def headnorm(src3, r_src, H, d, gain_ap, dst3, r_dst):
            sq, r_sq = wk_b.next()
            sq3 = sq[:, 0:H * d].rearrange("p (h d) -> p h d", h=H)
            fw.op("act", lambda e: e.activation(out=sq3, in_=src3, func=AF.Square), reads=[r_src], writes=[r_sq])
            st_, r_sm = sm.next()
            fw.op("dve", lambda e: e.tensor_reduce(out=st_[:, 0:H], in_=sq3, axis=AX.X, op=ALU.add), reads=[r_sq], writes=[r_sm])
            rstd_from_ss(st_[:, 0:H], r_sm, d, st_[:, 8:8 + H], r_sm)
            fw.op("dve", lambda e: e.tensor_tensor(out=dst3, in0=src3, in1=st_[:, 8:8 + H].unsqueeze(2).to_broadcast([128, H, d]), op=ALU.mult),
                  reads=[r_src, r_sm], writes=[r_dst])
            fw.op("dve", lambda e: e.tensor_tensor(out=dst3, in0=dst3, in1=gain_ap.unsqueeze(1).to_broadcast([128, H, d]), op=ALU.mult),
                  reads=[r_dst, r_gh], writes=[r_dst])

        def rope(x3, r_x, H, o0, half, i, f0):
            t, r_t = rp.next()
            cs = COS[:, i, f0:f0 + half].unsqueeze(1).to_broadcast([128, H, half])
            sn = SIN[:, i, f0:f0 + half].unsqueeze(1).to_broadcast([128, H, half])
            x1 = x3[:, :, o0:o0 + half]
            x2 = x3[:, :, o0 + half:o0 + 2 * half]

            def tv(k):
                return t[:, k, 0:H * half].rearrange("p (h d) -> p h d", h=H)
            for k, (a, b) in enumerate(((x1, cs), (x2, sn), (x2, cs), (x1, sn))):
                fw.op("dve", lambda e, k=k, a=a, b=b: e.tensor_tensor(out=tv(k), in0=a, in1=b, op=ALU.mult),
                      reads=[r_x, r_rope], writes=[r_t])
            fw.op("dve", lambda e: e.tensor_tensor(out=x1, in0=tv(0), in1=tv(1), op=ALU.subtract), reads=[r_t], writes=[r_x])
            fw.op("dve", lambda e: e.tensor_tensor(out=x2, in0=tv(2), in1=tv(3), op=ALU.add), reads=[r_t], writes=[r_x])

        def to_T(srcbf, r_src, nblk, rows, dstT, r_dst, i, blk0=0, src_stride=None):
            stride = rows if src_stride is None else src_stride
            for b0 in range(0, nblk, 4):
                nb = min(4, nblk - b0)
                pb, r_pb = bank("tr")
                for b in range(nb):
                    transp(pb[0:rows, b * 128:(b + 1) * 128], r_pb, srcbf[:, (b0 + b) * stride:(b0 + b) * stride + rows], r_src)
                src_v = pb[0:rows, 0:nb * 128].rearrange("p (b t) -> p b t", b=nb)
                dst_v = dstT[0:rows, blk0 + b0:blk0 + b0 + nb, i * 128:(i + 1) * 128]
                eng = ("act", "dve")[(b0 // 4 + i) % 2]
                if eng == "act":
                    fw.op("act", lambda e: e.activation(out=dst_v, in_=src_v, func=AF.Copy), reads=[r_pb], writes=[r_dst])
                else:
                    fw.op("dve", lambda e: e.tensor_copy(out=dst_v, in_=src_v), reads=[r_pb], writes=[r_dst])

        with ExitStack() as p0:
            def lalloc(name, shape, dt):
                return sbt(p0, name, shape, dt)
            xin = Ring(lalloc, "xin", [128, D], F32, 4)
            wd = Widen([wk_bf], 2)
            wpipe.prefetch()
            def p0_tile(i):
                xt, r_xt = xin.next()
                fw.dma("sp", xt[:], x_d[i * 128:(i + 1) * 128, :], writes=[r_xt])
                sq, r_sq = wk_bf.next()
                st_, r_sm = sm.next()
                fw.op("act", lambda e: e.activation(out=sq[:], in_=xt[:], func=AF.Square, accum_out=st_[:, 0:1]), reads=[r_xt], writes=[r_sq, r_sm])
                rstd_from_ss(st_[:, 0:1], r_sm, D, st_[:, 1:2], r_sm)
                xb, r_xb = wk_bf.next()
                fw.op("dve", lambda e: e.tensor_scalar(out=xb[:], in0=xt[:], scalar1=st_[:, 1:2], scalar2=None, op0=ALU.mult),
                      reads=[r_xt, r_sm], writes=[r_xb])
                to_T(xb, r_xb, 8, 128, hT, r_hT, i)
            run_pairs(p0_tile, list(range(NT)), width=4)
            wd.close()

        print("ckpt", 1, fw.tot)
        if STOP == 1:
            fw.barrier()
            raise _Stop()
        def gate_tile(i, s_, yc, r_yc, wz, r_wz):
            pz, r_pz = bank("pj")
            mm_tok(pz[:, 0:512], r_pz, hT, r_hT, i * 128, wz, r_wz, 8, 0, 512)
            z1f, r_z1 = wk_a.next()
            z1 = z1f[:, 0:512]
            fw.op("act", lambda e: e.activation(out=z1, in_=pz[:, 0:512], func=AF.Exp, scale=-1.0), reads=[r_pz], writes=[r_z1])
            fw.op("dve", lambda e: e.tensor_scalar(out=z1, in0=z1, scalar1=1.0, scalar2=None, op0=ALU.add), reads=[r_z1], writes=[r_z1])
            fw.op("dve", lambda e: e.reciprocal(out=z1, in_=z1), reads=[r_z1], writes=[r_z1])
            fw.op("dve", lambda e: e.tensor_tensor(out=yc[:, s_, :], in0=z1, in1=pz[:, 0:512], op=ALU.mult), reads=[r_z1, r_pz], writes=[r_yc])

        def attention(p_alloc, make_q, kT_of, r_k, vaug, r_v, H, dv, n_ktiles_of, nq, scale, bias_of, yTn, r_yTn, wz, r_wz, mask_of=None, lag=1, first_q=None, order=None):
            dv1 = dv + 1
            pT = Ring(p_alloc, "pT", [128, 512], BF16, 2 + lag)
            ych = Ring(p_alloc, "ych", [128, nq, 512], BF16, 2)
            rec = Ring(p_alloc, "rec", [128, 8], F32, 2)
            et = Ring(p_alloc, "et", [128, nq, dv], F32, 2)
            nchunks = NT // nq

            def prep(c):
                yc, r_yc = ych.next()
                res = make_q(c, yc, r_yc, first_q if c == order[0] else None)
                if len(res) == 4:
                    return res
                return res[0], res[1], yc, r_yc
            order = list(range(nchunks)) if order is None else order
            cur = prep(order[0])
            for oc, c in enumerate(order):
                t_lo = c * nq
                fw.maybe_barrier()
                qT_of, r_q, yc, r_yc = cur
                nxt = Coop(lambda: prep(order[oc + 1])) if oc + 1 < nchunks else None
                nk_max = n_ktiles_of(t_lo + nq - 1)
                steps = [(h, j) for h in range(H) for j in range(nk_max)]
                pulls = max(2, -(-330 // len(steps)))
                state = {}

                def pv_and_epilogue(pv):
                    h, j, tiles, pt, r_pt, pov, r_po = pv
                    for i in tiles:
                        s_ = i - t_lo
                        fw.op("pe", lambda e: e.matmul(pov[:, s_, :], lhsT=pt[:, s_ * 128:(s_ + 1) * 128], rhs=vaug[:, j, h, :],
                                                       start=(j == 0 and s_ == 0), stop=(j == n_ktiles_of(i) - 1), skip_group_check=True),
                              reads=[r_pt, r_v], writes=[r_po])
                    if j == nk_max - 1:
                        rc, r_rc = rec.next()
                        t_, r_t = et.next()
                        fw.op("dve", lambda e: e.reciprocal(out=rc[:, 0:nq], in_=pov[:, :, dv]), reads=[r_po], writes=[r_rc])
                        fw.op("dve", lambda e: e.tensor_tensor(out=t_[:], in0=pov[:, :, 0:dv],
                                                              in1=rc[:, 0:nq].unsqueeze(2).to_broadcast([128, nq, dv]), op=ALU.mult),
                              reads=[r_po, r_rc], writes=[r_t])
                        fw.op("dve", lambda e: e.tensor_tensor(out=yc[:, :, h * dv:(h + 1) * dv], in0=t_[:], in1=yc[:, :, h * dv:(h + 1) * dv], op=ALU.mult),
                              reads=[r_t, r_yc], writes=[r_yc])

                queue = []
                for (h, j) in steps:
                    if j == 0:
                        po, r_po = bank("o")
                        state["pov"] = (po[:, 0:nq * dv1].rearrange("p (s d) -> p s d", s=nq), r_po)
                    pov, r_po = state["pov"]
                    tiles = [i for i in range(t_lo, t_lo + nq) if j < n_ktiles_of(i)]
                    col0 = (tiles[0] - t_lo) * 128
                    ncol = nq * 128 - col0
                    pss, r_pss = bank("s")
                    biases = [(i, bias_of(i, j)) for i in tiles]
                    biases = [(i, b) for i, b in biases if b is not None]
                    fw.op("pe", lambda e: e.matmul(pss[:, col0:col0 + ncol], lhsT=kT_of(h)[:, j * 128:(j + 1) * 128],
                                                   rhs=qT_of(h)[:, col0:nq * 128],
                                                   start=True, stop=(len(biases) == 0), skip_group_check=True),
                          reads=[r_k, r_q], writes=[r_pss])
                    for bi, (i, (b_ap, r_b)) in enumerate(biases):
                        cc = (i - t_lo) * 128
                        fw.op("pe", lambda e: e.matmul(pss[:, cc:cc + 128], lhsT=b_ap, rhs=bigI[:, :],
                                                       start=False, stop=(bi == len(biases) - 1), skip_group_check=True),
                              reads=[r_b, r_c], writes=[r_pss])
                    pt, r_pt = pT.next()
                    fw.op("act", lambda e: e.activation(out=pt[:, col0:col0 + ncol], in_=pss[:, col0:col0 + ncol], func=AF.Exp, scale=float(scale)),
                          reads=[r_pss], writes=[r_pt])
                    if mask_of is not None:
                        m_ap, r_m = mask_of(j, tiles[0], t_lo + nq)
                        fw.op("dve", lambda e: e.tensor_tensor(out=pt[:, col0:col0 + ncol], in0=pt[:, col0:col0 + ncol], in1=m_ap, op=ALU.mult),
                              reads=[r_pt, r_m], writes=[r_pt])
                    queue.append((h, j, tiles, pt, r_pt, pov, r_po))
                    if len(queue) > lag:
                        pv_and_epilogue(queue.pop(0))
                    if nxt is not None:
                        for _ in range(pulls):
                            if not nxt.step():
                                break
                while queue:
                    pv_and_epilogue(queue.pop(0))
                if nxt is not None:
                    cur = nxt.finish()
                for s_ in range(nq):
                    to_T(yc[:, s_, :], r_yc, 4, 128, yTn, r_yTn, t_lo + s_)

        yT[0] = galloc("yT0", [128, 4, S], BF16)
        mt_off = {}
        off = 0
        for j in range(NT):
            mt_off[j] = off
            off += (NT - j) * 128
        es_mb = ExitStack()
        maskT = sbt(es_mb, "maskT", [128, off], BF16); r_mt = Res("maskT")
        fw.op("dve", lambda e: e.tensor_copy(out=maskT[:, mt_off[0]:mt_off[0] + 128], in_=cstf[:, 280:408]), reads=[r_cstf], writes=[r_mt])
        fw.op("dve", lambda e: e.tensor_copy(out=maskT[:, mt_off[1]:mt_off[1] + 128], in_=cstf[:, 280:408]), reads=[r_cstf], writes=[r_mt])
        fw.op("dve", lambda e: e.memset(maskT[:, mt_off[0] + 128:mt_off[0] + 256], 1.0), writes=[r_mt])

        with ExitStack() as pi:
            def lalloc(name, shape, dt):
                return sbt(pi, name, shape, dt)
            qiT = lalloc("qiT", [128, 4, S], BF16); r_qiT = Res("qiT")
            kiT = lalloc("kiT", [128, 1, S], BF16); r_kiT = Res("kiT")
            wabs = lalloc("wabs", [128, NT, 8], F32)
            wsgn = lalloc("wsgn", [128, NT, 8], F32)
            r_wi = Res("wi")
            wb, r_wb = wpipe.get("ki")
            def ki_tile(i):
                pp, r_pp = bank("pj")
                mm_tok(pp[:, 0:72], r_pp, hT, r_hT, i * 128, wb, r_wb, 8, 0, 72)
                a, r_a = wk_a.next()
                fw.op("act", lambda e: e.activation(out=a[:, 0:72], in_=pp[:, 0:72], func=AF.Copy), reads=[r_pp], writes=[r_a])
                k3 = a[:, 0:64].rearrange("p (h d) -> p h d", h=1)
                rope(k3, r_a, 1, 0, 8, i, 0)
                kb, r_kb = wk_bf.next()
                fw.op("dve", lambda e: e.tensor_copy(out=kb[:, 0:64], in_=a[:, 0:64]), reads=[r_a], writes=[r_kb])
                fw.op("dve", lambda e: e.tensor_copy(out=kb[:, 64:128], in_=a[:, 0:64]), reads=[r_a], writes=[r_kb])
                to_T(kb, r_kb, 1, 128, kiT, r_kiT, i)
                cst_w = float(64 ** -0.5 * 8 ** -0.5)
                fw.op("act", lambda e: e.activation(out=wabs[:, i, :], in_=a[:, 64:72], func=AF.Abs, scale=cst_w),
                      reads=[r_a], writes=[r_wi])
                fw.op("dve", lambda e: e.tensor_scalar(out=wsgn[:, i, :], in0=a[:, 64:72], scalar1=0.0, scalar2=2.0, op0=ALU.is_ge, op1=ALU.mult),
                      reads=[r_a], writes=[r_wi])
                fw.op("dve", lambda e: e.tensor_scalar(out=wsgn[:, i, :], in0=wsgn[:, i, :], scalar1=-1.0, scalar2=None, op0=ALU.add),
                      reads=[r_wi], writes=[r_wi])
            wd = Widen([wk_a, wk_bf, rp], 2)
            run_pairs(ki_tile, list(range(NT)), width=4)
            wb, r_wb = wpipe.get("qi")
            def qi_tile(i):
                pp, r_pp = bank("pj")
                mm_tok(pp[:, 0:512], r_pp, hT, r_hT, i * 128, wb, r_wb, 8, 0, 512)
                a, r_a = wk_a.next()
                fw.op("act", lambda e: e.activation(out=a[:, 0:512], in_=pp[:, 0:512], func=AF.Copy), reads=[r_pp], writes=[r_a])
                q3 = a[:, 0:512].rearrange("p (h d) -> p h d", h=8)
                rope(q3, r_a, 8, 0, 8, i, 0)
                qb, r_qb = wk_bf.next()
                fw.op("dve", lambda e: e.tensor_copy(out=qb[:, 0:512], in_=a[:, 0:512]), reads=[r_a], writes=[r_qb])
                to_T(qb, r_qb, 4, 128, qiT, r_qiT, i)
            run_pairs(qi_tile, list(range(NT)), width=4)
            wd.close()
            if STOP == 2:
                raise _Stop()
            NLI = 3
            irow = Ring(lalloc, "irow", [128, S], F32, NLI)
            jd = Ring(lalloc, "jd", [128, S], BF16, NLI)
            ja_res = [Res("ja%d" % k) for k in range(NLI)]
            bm = Ring(lalloc, "bm", [128, 1], F32, NLI)
            bc = Ring(lalloc, "bc", [128, 2], F32, NLI)
            ba = Ring(lalloc, "ba", [128, 1], F32, NLI)
            MID0 = 0.0031415927

            rts = [Ring(lalloc, "rt%d_" % k, [128, 512], BF16, 2) for k in range(NLI)]
            dsg = Ring(lalloc, "dsg", [128, 8, 128], BF16, NLI)

            def idx_tile(i):
                L = (i + 1) * 128
                ln = LANE.get(threading.get_ident(), 0)
                ir, r_ir = irow.next()
                dg, r_dg = dsg.next()
                fw.op("dve", lambda e: e.tensor_tensor(out=dg[:, :, :], in0=identb[:, :].unsqueeze(1).to_broadcast([128, 8, 128]),
                                                      in1=wsgn[:, i, :].unsqueeze(2).to_broadcast([128, 8, 128]), op=ALU.mult),
                      reads=[r_c, r_wi], writes=[r_dg])
                for kc0 in range(0, L, 512):
                    n = min(512, L - kc0)
                    pacc, r_pacc = banks[3 + ln], bres[3 + ln]
                    pds = [(banks[ln], bres[ln]), (banks[ln], bres[ln])]
                    pend = None
                    for h in range(8):
                        hp = (h % 2) * 64
                        pd, r_pd = pds[h % 2]
                        fw.op("pe", lambda e: e.matmul(pd[:, 0:n], lhsT=qiT[hp:hp + 64, h // 2, i * 128:(i + 1) * 128],
                                                       rhs=kiT[hp:hp + 64, 0, kc0:kc0 + n], start=True, stop=True),
                              reads=[r_qiT, r_kiT], writes=[r_pd])
                        rt, r_rt = rts[ln % NLI].t[h % 2], rts[ln % NLI].r[h % 2]
                        fw.op("act", lambda e: e.activation(out=rt[:, 0:n], in_=pd[:, 0:n], func=AF.Relu, scale=wabs[:, i, h:h + 1]),
                              reads=[r_pd, r_wi], writes=[r_rt])
                        if pend is not None:
                            ph, prt, pr_rt = pend
                            fw.op("pe", lambda e: e.matmul(pacc[:, 0:n], lhsT=dg[:, ph, :], rhs=prt[:, 0:n], start=(ph == 0), stop=False),
                                  reads=[r_dg, pr_rt], writes=[r_pacc])
                        pend = (h, rt, r_rt)
                    ph, prt, pr_rt = pend
                    fw.op("pe", lambda e: e.matmul(pacc[:, 0:n], lhsT=dg[:, ph, :], rhs=prt[:, 0:n], start=False, stop=True),
                          reads=[r_dg, pr_rt], writes=[r_pacc])
                    if kc0 + n == L:
                        d0 = i * 128 - kc0
                        if d0 > 0:
                            fw.op("dve", lambda e: e.tensor_copy(out=ir[:, kc0:kc0 + d0], in_=pacc[:, 0:d0]), reads=[r_pacc], writes=[r_ir])
                        fw.op("dve", lambda e: e.tensor_tensor(out=ir[:, i * 128:L], in0=pacc[:, d0:d0 + 128], in1=cmaskf[:, :], op=ALU.add),
                              reads=[r_pacc, r_c], writes=[r_ir])
                    else:
                        fw.op("dve", lambda e: e.tensor_copy(out=ir[:, kc0:kc0 + n], in_=pacc[:, 0:n]), reads=[r_pacc], writes=[r_ir])
                L1 = max(128, (int(L * 0.5) // 128) * 128)
                L2 = L - L1
                mid_t, r_bm = bm.next()
                c_t, r_bc = bc.next()
                a_t, r_ba = ba.next()
                jdt, r_jd = jd.next()
                jat, r_ja = jdt[:, L1:L], ja_res[ln % NLI]
                mid, cnt, stp, asum = mid_t[:, 0:1], c_t[:, 0:1], c_t[:, 1:2], a_t[:, 0:1]
                fw.op("dve", lambda e: e.memset(mid, MID0), writes=[r_bm])
                W = 32.0
                for k in range(NBIS):
                    fw.op("dve", lambda e: e.tensor_scalar(out=jdt[:, 0:L1], in0=ir[:, 0:L1], scalar1=mid, scalar2=0.0, op0=ALU.is_ge, op1=ALU.add,
                                                          accum_out=cnt), reads=[r_ir, r_bm], writes=[r_jd, r_bc])
                    fw.op("act", lambda e: e.activation(out=jat, in_=ir[:, L1:L], func=AF.Sign, bias=mid, scale=-1.0, accum_out=asum),
                          reads=[r_ir, r_bm], writes=[r_ja, r_ba])
                    fw.op("dve", lambda e: e.scalar_tensor_tensor(out=stp, in0=cnt, scalar=2.0, in1=asum, op0=ALU.mult, op1=ALU.subtract),
                          reads=[r_bc, r_ba], writes=[r_bc])
                    fw.op("dve", lambda e: e.tensor_scalar(out=stp, in0=stp, scalar1=float(511.0 - L2), scalar2=float(W / 2), op0=ALU.is_ge, op1=ALU.mult),
                          reads=[r_bc], writes=[r_bc])
                    q = W / 4 if k < NBIS - 1 else W / 2
                    fw.op("dve", lambda e: e.scalar_tensor_tensor(out=mid, in0=stp, scalar=float(-q), in1=mid, op0=ALU.add, op1=ALU.add),
                          reads=[r_bc, r_bm], writes=[r_bm])
                    W = W / 2
                fw.op("dve", lambda e: e.tensor_scalar(out=jdt[:, 0:L], in0=ir[:, 0:L], scalar1=mid, scalar2=None, op0=ALU.is_ge),
                      reads=[r_ir, r_bm], writes=[r_jd, r_ja])
                Coop.hold.add(threading.get_ident())
                for j0 in range(0, i + 1, 4):
                    nb = min(4, i + 1 - j0)
                    pb, r_pb = banks[6 + (j0 // 4) % 2], bres[6 + (j0 // 4) % 2]
                    for b in range(nb):
                        transp(pb[:, b * 128:(b + 1) * 128], r_pb, jdt[:, (j0 + b) * 128:(j0 + b + 1) * 128], r_jd)
                    for b in range(nb):
                        j = j0 + b
                        d_ = mt_off[j] + (i - j) * 128
                        if (b + i) % 2 == 0:
                            fw.op("act", lambda e: e.activation(out=maskT[:, d_:d_ + 128], in_=pb[:, b * 128:(b + 1) * 128], func=AF.Copy), reads=[r_pb], writes=[r_mt])
                        else:
                            fw.op("dve", lambda e: e.tensor_copy(out=maskT[:, d_:d_ + 128], in_=pb[:, b * 128:(b + 1) * 128]), reads=[r_pb], writes=[r_mt])
                Coop.hold.discard(threading.get_ident())

            run_pairs(idx_tile, list(range(2, NT)), width=NLI, stagger=25)
            fw.barrier()

        print("ckpt", 3, fw.tot)
        if STOP == 3:
            fw.barrier()
            raise _Stop()
        with ExitStack() as pa:
            def lalloc(name, shape, dt):
                return sbt(pa, name, shape, dt)
            kaT = lalloc("kaT", [128, 4, S], BF16)
            vau = lalloc("vau", [128, NT, 8, 65], BF16)
            qch = Ring(lalloc, "qchA", [128, 4, 512], BF16, 2)
            r_k = Res("kA"); r_v = Res("vA")
            fw.op("pool", lambda e: e.memset(vau[:, :, :, 64:65], 1.0), writes=[r_v])

            def qk_tile(i, wb, r_wb, gain_ap, dstT, r_dst, slot):
                pp, r_pp = bank("pj")
                mm_tok(pp[:, 0:512], r_pp, hT, r_hT, i * 128, wb, r_wb, 8, 0, 512)
                a, r_a = wk_a.next()
                fw.op("act", lambda e: e.activation(out=a[:, 0:512], in_=pp[:, 0:512], func=AF.Copy), reads=[r_pp], writes=[r_a])
                a3 = a[:, 0:512].rearrange("p (h d) -> p h d", h=8)
                headnorm(a3, r_a, 8, 64, gain_ap, a3, r_a)
                rope(a3, r_a, 8, 0, 8, i, 0)
                qb, r_qb = wk_bf.next()
                fw.op("dve", lambda e: e.tensor_copy(out=qb[:, 0:512], in_=a[:, 0:512]), reads=[r_a], writes=[r_qb])
                to_T(qb, r_qb, 4, 128, dstT, r_dst, slot)

            wb, r_wb = wpipe.get("va")
            for i in range(NT):
                pp, r_pp = bank("pj")
                mm_tok(pp[:, 0:512], r_pp, hT, r_hT, i * 128, wb, r_wb, 8, 0, 512)
                fw.op("act", lambda e: e.activation(out=vau[:, i, :, 0:64], in_=pp[:, 0:512].rearrange("p (h d) -> p h d", h=8), func=AF.Copy),
                      reads=[r_pp], writes=[r_v])
            wb, r_wb = wpipe.get("ka")
            wq, r_wq = wpipe.get("qa", prefetch=False)
            wd = Widen([wk_a, wk_b, wk_bf, rp], 2)
            qc0, r_qc0 = qch.next()

            def kq_item(it):
                kind, i = it
                if kind == "k":
                    qk_tile(i, wb, r_wb, G_KNA, kaT, r_k, i)
                else:
                    qk_tile(12 + i, wq, r_wq, G_QNA, qc0, r_qc0, i)
            run_pairs(kq_item, [("k", i) for i in range(NT)] + [("q", i) for i in range(4)], width=4)
            wz, r_wz = wpipe.get("za", prefetch=False)
            wd.close()

            def make_q_A(c, yc, r_yc, first):
                items = [("g", s_) for s_ in range(4)]
                if first is not None:
                    qc, r_qc = first
                else:
                    qc, r_qc = qch.next()
                    items = [x for s_ in range(4) for x in (("q", s_), ("g", s_))]

                def item(it):
                    if it[0] == "q":
                        qk_tile(c * 4 + it[1], wq, r_wq, G_QNA, qc, r_qc, it[1])
                    else:
                        gate_tile(c * 4 + it[1], it[1], yc, r_yc, wz, r_wz)
                run_pairs(item, items, stagger=4)
                return (lambda h: qc[(h % 2) * 64:(h % 2) * 64 + 64, h // 2, :]), r_qc

            def mask_A(j, i0, i1):
                o_ = mt_off[j] + (i0 - j) * 128
                return maskT[:, o_:o_ + (i1 - i0) * 128], r_mt
            attention(lalloc, make_q_A,
                      lambda h: kaT[(h % 2) * 64:(h % 2) * 64 + 64, h // 2, :],
                      r_k, vau, r_v, 8, 64, lambda i: i + 1, 4, 64 ** -0.5, lambda i, j: None, yT[0], r_yT[0], wz, r_wz,
                      mask_of=mask_A, lag=4, first_q=(qc0, r_qc0), order=[3, 2, 1, 0])
            wpipe.prefetch()
            fw.barrier()
        es_mb.close()
        print("ckpt", 4, fw.tot)
        if STOP == 4:
            fw.barrier()
            raise _Stop()
        yT[1] = galloc("yT1", [128, 4, S], BF16)

        with ExitStack() as pb:
            def lalloc(name, shape, dt):
                return sbt(pb, name, shape, dt)
            kbT = lalloc("kbT", [96, 8, S], BF16)
            vbu = lalloc("vbu", [128, NT, 8, 65], BF16)
            r_k = Res("kB"); r_v = Res("vB")
            fw.op("pool", lambda e: e.memset(vbu[:, :, :, 64:65], 1.0), writes=[r_v])
            wuq = lalloc("wuq", [128, 3, 768], BF16); r_wuq = Res("wuq")
            latT = Ring(lalloc, "latT", [128, 3, 128], BF16, 2)
            qch = Ring(lalloc, "qchB", [96, 8, 512], BF16, 1)
            pbk = ExitStack()
            wukv = sbt(pbk, "wukv", [128, 2, 1024], BF16); r_wukv = Res("wukv")
            for (dst, r_dst, src, kch, ncols, gain) in ((wuq, r_wuq, wuq_d, 3, 768, GK_CQ), (wukv, r_wukv, wukv_d, 2, 1024, GK_CKV)):
                for s0 in range(0, ncols, 128):
                    st, r_st = stg.next()
                    fw.dma("sp", st[:, 0:kch, 0:128], src[0:kch * 128, s0:s0 + 128].rearrange("(kc p) n -> p kc n", p=128), writes=[r_st])
                    fw.op("pool", lambda e: e.tensor_tensor(out=dst[:, 0:kch, s0:s0 + 128], in0=st[:, 0:kch, 0:128],
                                                           in1=gain[:, 0:kch].unsqueeze(2).to_broadcast([128, kch, 128]), op=ALU.mult),
                          reads=[r_st, r_gk], writes=[r_dst])
            if STOP == 41:
                fw.barrier()
                raise _Stop()

            def latent(i, wb, r_wb, ncols_all, nlat):
                pp, r_pp = bank("pj")
                mm_tok(pp[:, 0:ncols_all], r_pp, hT, r_hT, i * 128, wb, r_wb, 8, 0, ncols_all)
                sq, r_sq = wk_b.next()
                st_, r_sm = sm.next()
                fw.op("act", lambda e: e.activation(out=sq[:, 0:nlat], in_=pp[:, 0:nlat], func=AF.Square, accum_out=st_[:, 0:1]), reads=[r_pp], writes=[r_sq, r_sm])
                rstd_from_ss(st_[:, 0:1], r_sm, nlat, st_[:, 1:2], r_sm)
                cb_, r_cb = wk_bf.next()
                fw.op("dve", lambda e: e.tensor_scalar(out=cb_[:, 0:nlat], in0=pp[:, 0:nlat], scalar1=st_[:, 1:2], scalar2=None, op0=ALU.mult),
                      reads=[r_pp, r_sm], writes=[r_cb])
                lt, r_lt = latT.next()
                to_T(cb_, r_cb, nlat // 128, 128, lt, r_lt, 0)
                return pp, r_pp, lt, r_lt

            def ckb(k):
                if STOP == k:
                    fw.barrier()
                    raise _Stop()
            wd = Widen([wk_a, wk_b, wk_bf, rp, latT], 2)
            wb, r_wb = wpipe.get("ckv")
            def kvb_tile(i):
                pp, r_pp, lt, r_lt = latent(i, wb, r_wb, 288, 256)
                a, r_a = wk_a.next()
                a3 = a[:, 0:768].rearrange("p (h d) -> p h d", h=8)
                fw.op("act", lambda e: e.activation(out=a3[:, :, 64:96], in_=pp[:, 256:288].unsqueeze(1).to_broadcast([128, 8, 32]), func=AF.Copy),
                      reads=[r_pp], writes=[r_a])
                for half in range(2):
                    pq, r_pq = bank("pj")
                    mm_tok(pq[:, 0:512], r_pq, lt, r_lt, 0, wukv, r_wukv, 2, half * 512, 512)
                    pq3 = pq[:, 0:512].rearrange("p (h d) -> p h d", h=4)
                    fw.op("act", lambda e: e.activation(out=a3[:, half * 4:(half + 1) * 4, 0:64], in_=pq3[:, :, 0:64], func=AF.Copy), reads=[r_pq], writes=[r_a])
                    fw.op("act", lambda e: e.activation(out=vbu[:, i, half * 4:(half + 1) * 4, 0:64], in_=pq3[:, :, 64:128], func=AF.Copy), reads=[r_pq], writes=[r_v])
                headnorm(a3, r_a, 8, 96, G_KNB, a3, r_a)
                rope(a3, r_a, 8, 64, 16, i, 8)
                kb, r_kb = wk_bf.next()
                fw.op("dve", lambda e: e.tensor_copy(out=kb[:, 0:768], in_=a[:, 0:768]), reads=[r_a], writes=[r_kb])
                to_T(kb, r_kb, 8, 96, kbT, r_k, i)
            wq, r_wq = wpipe.get("cq", prefetch=False)

            def qb_tile_into(i, qc, r_qc, s_):
                pp, r_pp, lt, r_lt = latent(i, wq, r_wq, 384, 384)
                a, r_a = wk_a.next()
                for half in range(2):
                    pq, r_pq = bank("pj")
                    mm_tok(pq[:, 0:384], r_pq, lt, r_lt, 0, wuq, r_wuq, 3, half * 384, 384)
                    fw.op("act", lambda e: e.activation(out=a[:, half * 384:(half + 1) * 384], in_=pq[:, 0:384], func=AF.Copy), reads=[r_pq], writes=[r_a])
                a3 = a[:, 0:768].rearrange("p (h d) -> p h d", h=8)
                headnorm(a3, r_a, 8, 96, G_QNB, a3, r_a)
                rope(a3, r_a, 8, 64, 16, i, 8)
                qb, r_qb = wk_bf.next()
                fw.op("dve", lambda e: e.tensor_copy(out=qb[:, 0:768], in_=a[:, 0:768]), reads=[r_a], writes=[r_qb])
                to_T(qb, r_qb, 8, 96, qc, r_qc, s_)

            qc0, r_qc0 = qch.next()

            def kq_item(it):
                kind, i = it
                if kind == "k":
                    kvb_tile(i)
                else:
                    qb_tile_into(12 + i, qc0, r_qc0, i)
            run_pairs(kq_item, [("k", i) for i in range(NT)] + [("q", i) for i in range(4)], width=4)
            wd.close()
            fw.barrier()
            pbk.close()
            qch.grow(lalloc, 1)
            wz, r_wz = wpipe.get("zb", prefetch=False)

            def make_q_B(c, yc, r_yc, first):
                items = [("g", s_) for s_ in range(4)]
                if first is not None:
                    qc, r_qc = first
                else:
                    qc, r_qc = qch.next()
                    items = [x for s_ in range(4) for x in (("q", s_), ("g", s_))]

                def item(it):
                    if it[0] == "q":
                        qb_tile_into(c * 4 + it[1], qc, r_qc, it[1])
                    else:
                        gate_tile(c * 4 + it[1], it[1], yc, r_yc, wz, r_wz)
                run_pairs(item, items, stagger=4)
                return (lambda h: qc[0:96, h, :]), r_qc

            def bias_B(i, j):
                if i == j:
                    return (cb01[:, :], r_c)
                return None
            attention(lalloc, make_q_B, lambda h: kbT[0:96, h, :], r_k, vbu, r_v, 8, 64,
                      lambda i: i + 1, 4, 96 ** -0.5, bias_B, yT[1], r_yT[1], wz, r_wz, first_q=(qc0, r_qc0), order=[3, 2, 1, 0], lag=4)
            wpipe.prefetch()
            fw.barrier()
        yT[2] = galloc("yT2", [128, 4, S], BF16)
        print("ckpt", 5, fw.tot)
        if STOP == 5:
            fw.barrier()
            raise _Stop()

        with ExitStack() as pm:
            def lalloc(name, shape, dt):
                return sbt(pm, name, shape, dt)
            kmT = lalloc("kmT", [128, 4, 256], BF16)
            vmu = lalloc("vmu", [128, 2, 4, 129], BF16)
            memT = lalloc("memT", [128, 8, 256], BF16); r_memT = Res("memT")
            qmT = lalloc("qmT", [128, 4, S], BF16); r_qm = Res("qmT")
            y_all = lalloc("yallM", [128, NT, 512], BF16); r_yall = Res("yallM")
            r_k = Res("kM"); r_v = Res("vM")
            fw.op("pool", lambda e: e.memset(vmu[:, :, :, 128:129], 1.0), writes=[r_v])
            pms = ExitStack()
            xin = Ring(lambda nm, sh, dt: sbt(pms, nm, sh, dt), "min", [128, D], F32, 2)
            wd = Widen([wk_a, wk_b, wk_bf, rp], 2)
            for t in range(2):
                xt, r_xt = xin.next()
                fw.dma("sp", xt[:], mem_d[t * 128:(t + 1) * 128, :], writes=[r_xt])
                sq, r_sq = wk_bf.next()
                st_, r_sm = sm.next()
                fw.op("act", lambda e: e.activation(out=sq[:], in_=xt[:], func=AF.Square, accum_out=st_[:, 0:1]), reads=[r_xt], writes=[r_sq, r_sm])
                rstd_from_ss(st_[:, 0:1], r_sm, D, st_[:, 1:2], r_sm)
                xb, r_xb = wk_bf.next()
                fw.op("dve", lambda e: e.tensor_scalar(out=xb[:], in0=xt[:], scalar1=st_[:, 1:2], scalar2=None, op0=ALU.mult), reads=[r_xt, r_sm], writes=[r_xb])
                to_T(xb, r_xb, 8, 128, memT, r_memT, t)

            def qk_tile_m(srcT, r_src, t0, wb, r_wb, gain_ap, dstT, r_dst, slot):
                pp, r_pp = bank("pj")
                mm_tok(pp[:, 0:512], r_pp, srcT, r_src, t0, wb, r_wb, 8, 0, 512)
                a, r_a = wk_a.next()
                fw.op("act", lambda e: e.activation(out=a[:, 0:512], in_=pp[:, 0:512], func=AF.Copy), reads=[r_pp], writes=[r_a])
                a3 = a[:, 0:512].rearrange("p (h d) -> p h d", h=4)
                headnorm(a3, r_a, 4, 128, gain_ap, a3, r_a)
                kb, r_kb = wk_bf.next()
                fw.op("dve", lambda e: e.tensor_copy(out=kb[:, 0:512], in_=a[:, 0:512]), reads=[r_a], writes=[r_kb])
                to_T(kb, r_kb, 4, 128, dstT, r_dst, slot)

            wb, r_wb = wpipe.get("km")
            for t in range(2):
                qk_tile_m(memT, r_memT, t * 128, wb, r_wb, G_KNM, kmT, r_k, t)
            wb, r_wb = wpipe.get("vm")
            for t in range(2):
                pp, r_pp = bank("pj")
                mm_tok(pp[:, 0:512], r_pp, memT, r_memT, t * 128, wb, r_wb, 8, 0, 512)
                fw.op("act", lambda e: e.activation(out=vmu[:, t, :, 0:128], in_=pp[:, 0:512].rearrange("p (h d) -> p h d", h=4), func=AF.Copy),
                      reads=[r_pp], writes=[r_v])
            wq, r_wq = wpipe.get("qm")
            wz, r_wz = wpipe.get("zm", prefetch=False)

            def mq_tile(i):
                qk_tile_m(hT, r_hT, i * 128, wq, r_wq, G_QNM, qmT, r_qm, i)
                gate_tile(i, i, y_all, r_yall, wz, r_wz)
            run_pairs(mq_tile, list(range(NT)), width=4)
            wd.close()
            pms.close()

            def make_q_M(c, yc, r_yc, first):
                return (lambda h: qmT[:, h, c * 256:(c + 1) * 256]), r_qm, y_all[:, 2 * c:2 * c + 2, :], r_yall
            attention(lalloc, make_q_M, lambda h: kmT[:, h, :], r_k, vmu, r_v, 4, 128,
                      lambda i: 2, 2, 128 ** -0.5, lambda i, j: None, yT[2], r_yT[2], wz, r_wz, lag=2)
            fw.barrier()

        print("ckpt", 6, fw.tot)
        if STOP == 6:
            fw.barrier()
            raise _Stop()
        with ExitStack() as pf:
            def lalloc(name, shape, dt):
                return sbt(pf, name, shape, dt)
            mT = lalloc("mT", [128, 8, S], BF16); r_mT = Res("mT")
            pfd = ExitStack()

            def dalloc(name, shape, dt):
                return sbt(pfd, name, shape, dt)
            wbrs = Ring(dalloc, "wbrs", [128, 3, 4, 128], BF16, 2)
            sg = Ring(dalloc, "sg", [128, 512], F32, 2)
            tmp = Ring(dalloc, "tmpf", [128, 512], F32, 2)
            macc = Ring(dalloc, "macc", [128, 512], F32, 2)
            fw.barrier()
            gst2 = dalloc("gst2", [128, 8, 128], F32)
            gst = [stg.t[0][:, :, :], stg.t[1][:, :, :], gst2[:, :, :]]
            r_gst = [stg.r[0], stg.r[1], Res("gst2")]
            bst = dalloc("bst", [128, 3, 4, 128], F32); r_bst = Res("bst")

            def issue_dc(dc):
                for n in range(3):
                    c0 = C_G + n * 1024 + dc * 128
                    fw.dma("sp", gst[n], win_d[0:1024, c0:c0 + 128].rearrange("(kc p) n -> p kc n", p=128), writes=[r_gst[n]])
                for n in range(3):
                    fw.dma("sp", bst[:, n, :, :], wbr_d[n * 512:(n + 1) * 512, dc * 128:(dc + 1) * 128].rearrange("(kc p) n -> p kc n", p=128), writes=[r_bst])

            def cast_dc(dc):
                wg, r_wg = wbr_.next()
                for n in range(3):
                    fw.op("dve", lambda e: e.tensor_tensor(out=wg[:, 0:8, n * 128:(n + 1) * 128], in0=gst[n],
                                                          in1=GK_NORM.unsqueeze(2).to_broadcast([128, 8, 128]), op=ALU.mult),
                          reads=[r_gst[n], r_gk], writes=[r_wg])
                wbn, r_wbn = wbrs.next()
                fw.op("dve", lambda e: e.tensor_copy(out=wbn[:], in_=bst[:]), reads=[r_bst], writes=[r_wbn])
                return wg, r_wg, wbn, r_wbn

            issue_dc(0)
            nxt_w = cast_dc(0)
            for dc in range(8):
                fw.maybe_barrier()
                wg, r_wg, wbn, r_wbn = nxt_w
                if dc + 1 < 8:
                    issue_dc(dc + 1)
                else:
                    wout0 = load_w(wout_d, 0, 8, 0, 512, None)
                for tc in range(4):
                    if tc == 2 and dc + 1 < 8:
                        nxt_w = cast_dc(dc + 1)
                    ma, r_ma = macc.next()
                    for n in range(3):
                        pg, r_pg = bank("pj")
                        for kc in range(8):
                            fw.op("pe", lambda e, kc=kc: e.matmul(pg[:, 0:512], lhsT=wg[:, kc, n * 128:(n + 1) * 128], rhs=hT[:, kc, tc * 512:(tc + 1) * 512],
                                                                  start=(kc == 0), stop=(kc == 7)), reads=[r_wg, r_hT], writes=[r_pg])
                        s1, r_s1 = sg.next()
                        fw.op("act", lambda e: e.activation(out=s1[:], in_=pg[:, 0:512], func=AF.Tanh, scale=0.5), reads=[r_pg], writes=[r_s1])
                        pbr, r_pbr = bank("s")
                        for kw in range(4):
                            fw.op("pe", lambda e, kw=kw: e.matmul(pbr[:, 0:512], lhsT=wbn[:, n, kw, :], rhs=yT[n][:, kw, tc * 512:(tc + 1) * 512],
                                                                  start=(kw == 0), stop=(kw == 3)), reads=[r_wbn, r_yT[n]], writes=[r_pbr])
                        if n == 0:
                            fw.op("dve", lambda e: e.scalar_tensor_tensor(out=ma[:], in0=s1[:], scalar=1.0, in1=pbr[:, 0:512], op0=ALU.add, op1=ALU.mult),
                                  reads=[r_s1, r_pbr], writes=[r_ma])
                        else:
                            t1, r_t1 = tmp.next()
                            fw.op("dve", lambda e: e.scalar_tensor_tensor(out=t1[:], in0=s1[:], scalar=1.0, in1=pbr[:, 0:512], op0=ALU.add, op1=ALU.mult),
                                  reads=[r_s1, r_pbr], writes=[r_t1])
                            fw.op("dve", lambda e: e.tensor_tensor(out=ma[:], in0=ma[:], in1=t1[:], op=ALU.add), reads=[r_ma, r_t1], writes=[r_ma])
                    fw.op("act", lambda e: e.activation(out=mT[:, dc, tc * 512:(tc + 1) * 512], in_=ma[:], func=AF.Copy, scale=0.5), reads=[r_ma], writes=[r_mT])
            fw.barrier()
            pfd.close()
            xin = Ring(lalloc, "xres", [128, 512], F32, 4)
            oo = Ring(lalloc, "oo", [128, 512], F32, 4)
            wouts = [wout0, load_w(wout_d, 0, 8, 512, 512, None)]
            for half in range(2):
                wb, r_wb = wouts[half]
                for i in range(NT):
                    xt, r_xt = xin.next()
                    fw.dma("sp", xt[:], x_d[i * 128:(i + 1) * 128, half * 512:(half + 1) * 512], writes=[r_xt])
                    pp, r_pp = bank("pj")
                    mm_tok(pp[:, 0:512], r_pp, mT, r_mT, i * 128, wb, r_wb, 8, 0, 512)
                    ot, r_ot = oo.next()
                    fw.op("dve", lambda e: e.tensor_tensor(out=ot[:], in0=pp[:, 0:512], in1=xt[:], op=ALU.add), reads=[r_pp, r_xt], writes=[r_ot])
                    fw.dma("act", out_d[i * 128:(i + 1) * 128, half * 512:(half + 1) * 512], ot[:], reads=[r_ot])
            fw.barrier()
        print("inst counts", fw.tot, "sems", fw.nsem, "epochs", fw.epoch)


_NC_CACHE = {}


def _host_consts():
    ident = np.eye(128, dtype=np.float32)
    q = np.arange(128)[:, None]
    k = np.arange(128)[None, :]
    cb = np.where(k > q, -1.0, 0.0).astype(np.float32)
    theta = 500000.0
    fa = theta ** (-(np.arange(0, 16, 2, dtype=np.float32) / 16.0))
    fb = theta ** (-(np.arange(0, 32, 2, dtype=np.float32) / 32.0))
    invf = (np.concatenate([fa, fb]).astype(np.float64) / (2 * np.pi)).astype(np.float32)
    c01t = np.where(q <= k, 1.0, 0.0).astype(np.float32)
    cst = np.concatenate([ident, cb, np.broadcast_to(invf[None, :], (128, 24)), c01t], axis=1)
    return np.ascontiguousarray(cst, dtype=np.float32)


def kernel(x, mem, positions, g_norm, w_in, g_qn_a, g_kn_a, g_cq, g_ckv, w_uq, w_ukv,
           g_qn_b, g_kn_b, g_mem, w_mem_kv, g_qn_m, g_kn_m, w_branch, w_out):
    f = lambda a: np.ascontiguousarray(np.asarray(a), dtype=np.float32)
    x = f(x); mem = f(mem)
    positions = np.asarray(positions).astype(np.int32)
    n = 8
    if "nc" not in _NC_CACHE:
        _NC_CACHE["nc"] = build_program(stop=int(os.environ.get("KSTOP", "99")))
    nc = _NC_CACHE["nc"]
    gk = np.concatenate([f(g_norm)[0].reshape(8, 128).T, f(g_cq)[0].reshape(3, 128).T,
                         f(g_ckv)[0].reshape(2, 128).T, f(g_mem)[0].reshape(8, 128).T], axis=1)
    ghv = np.concatenate([f(g_qn_a)[0], f(g_kn_a)[0], f(g_qn_b)[0], f(g_kn_b)[0], f(g_qn_m)[0], f(g_kn_m)[0]])
    gh = np.broadcast_to(ghv[None, :], (128, 576))
    shared = {
        "w_in": f(w_in)[0], "w_uq": f(w_uq)[0], "w_ukv": f(w_ukv)[0], "w_mem": f(w_mem_kv)[0],
        "w_br": f(w_branch)[0].reshape(3 * 512, D), "w_out": f(w_out)[0],
        "gk": np.ascontiguousarray(gk, dtype=np.float32), "gh": np.ascontiguousarray(gh, dtype=np.float32),
        "cst": _host_consts(),
    }
    in_maps = []
    for b in range(n):
        m = dict(shared)
        m["x"] = x[b]
        m["mem"] = mem[b]
        m["pos"] = np.ascontiguousarray(positions[b].reshape(NT, 128).T)
        in_maps.append(m)
    res = run_bass_kernel_spmd(nc, in_maps, core_ids=list(range(n)))
    return np.stack([np.asarray(r["out"], dtype=np.float32) for r in res.results], axis=0)
```

```python
import os
import threading
from contextlib import ExitStack
import numpy as np
import concourse.bass as bass
import concourse.mybir as mybir
from concourse.alu_op_type import AluOpType as ALU
from concourse.bass_utils import run_bass_kernel_spmd

F32 = mybir.dt.float32
BF16 = mybir.dt.bfloat16
I32 = mybir.dt.int32
AF = mybir.ActivationFunctionType
AX = mybir.AxisListType

S = 2048
D = 1024
NT = 16
D_IN = 7912
EPS = 1e-6
BIG = 30000.0
C_QA, C_KA, C_VA, C_QI, C_KI, C_WI, C_ZA = 0, 512, 1024, 1536, 2048, 2112, 2120
C_CQ, C_CKV, C_KR, C_ZB, C_QM, C_ZM, C_G = 2632, 3016, 3272, 3304, 3816, 4328, 4840
NBIS = 20


DBG_NAMES = {}


class Res:
    __slots__ = ("name", "w", "r", "sem", "cnt")

    def __init__(self, name):
        self.name = name
        self.w = None
        self.r = {}
        self.sem = None
        self.cnt = 0


class FW:
    ENG = ("pe", "act", "dve", "pool", "sp")

    def __init__(self, nc, es):
        self.nc = nc
        self.es = es
        self.eng = {"pe": nc.tensor, "act": nc.scalar, "dve": nc.vector,
                    "pool": nc.gpsimd, "sp": nc.sync}
        self.epoch = 0
        self.nsem = 0
        self._new_sems()
        self.tot = {e: 0 for e in self.ENG}
        self.waited = {e: {} for e in self.ENG}
        self.dma_res = []
        self.uid = 0

    def _new_sems(self):
        self.sem = {e: self.es.enter_context(self.nc.semaphore("s%d_%s" % (self.epoch, e))) for e in self.ENG}
        self.cnt = {e: 0 for e in self.ENG}
        self.nsem += len(self.ENG)

    def _wait(self, e, ev):
        if ev is None:
            return
        key, semh, val, ep = ev
        if ep is not None and ep < self.epoch:
            return
        if key == e and e == "pe":
            return
        w = self.waited[e]
        if w.get(key, 0) >= val:
            return
        w[key] = val
        self.eng[e].wait_ge(semh, val)

    def _deps(self, e, reads, writes):
        for r in reads:
            self._wait(e, r.w)
        for w in writes:
            self._wait(e, w.w)
            for k, ev in w.r.items():
                if k == e:
                    continue
                self._wait(e, ev)

    def _record(self, ev, reads, writes):
        for r in reads:
            r.r[ev[0]] = ev
        for w in writes:
            w.w = ev
            w.r = {}

    def op(self, e, fn, reads=(), writes=()):
        self._deps(e, reads, writes)
        inst = fn(self.eng[e])
        self.cnt[e] += 1
        self.tot[e] += 1
        inst.then_inc(self.sem[e], 1)
        self._record((e, self.sem[e], self.cnt[e], self.epoch), reads, writes)
        Coop.yield_point()

    def dma(self, q, out, in_, reads=(), writes=()):
        self._deps(q, reads, writes)
        tgt = writes[0] if writes else reads[0]
        if tgt.sem is None:
            self.uid += 1
            tgt.sem = self.es.enter_context(self.nc.semaphore("d%d" % self.uid))
            tgt.name = "d%d_%s" % (self.uid, tgt.name)
            self.dma_res.append(tgt)
            self.nsem += 1
        inst = self.eng[q].dma_start(out=out, in_=in_)
        tgt.cnt += 16
        inst.then_inc(tgt.sem, 16)
        self._record((tgt.name, tgt.sem, tgt.cnt, None), reads, writes)
        Coop.yield_point()

    def barrier(self):
        for e in self.ENG:
            for e2 in self.ENG:
                if e2 != e and self.cnt[e2] > 0:
                    self._wait(e, (e2, self.sem[e2], self.cnt[e2], self.epoch))
            for r in self.dma_res:
                if r.cnt > 0:
                    self._wait(e, (r.name, r.sem, r.cnt, None))
        self.epoch += 1
        self._new_sems()
        for e in self.ENG:
            for k in self.ENG:
                self.waited[e].pop(k, None)

    def maybe_barrier(self, limit=100000):
        if max(self.cnt.values()) > limit:
            self.barrier()


class _Stop(Exception):
    pass


class Coop:
    current = {}

    def __init__(self, fn):
        self.go = threading.Semaphore(0)
        self.done = threading.Semaphore(0)
        self.finished = False
        self.exc = None
        self.result = None

        def body():
            self.go.acquire()
            try:
                self.result = fn()
            except BaseException as e:
                self.exc = e
            self.finished = True
            Coop.current.pop(threading.get_ident(), None)
            self.done.release()
        self.t = threading.Thread(target=body)
        self.t.start()
        Coop.current[self.t.ident] = self

    def step(self):
        if self.finished:
            return False
        self.go.release()
        self.done.acquire()
        if self.exc is not None:
            raise self.exc
        if not self.finished:
            Coop.yield_point()
        return not self.finished

    def finish(self):
        while self.step():
            pass
        self.t.join()
        return self.result

    hold = set()

    @staticmethod
    def yield_point():
        c = Coop.current.get(threading.get_ident())
        if c is not None and threading.get_ident() not in Coop.hold:
            c.done.release()
            c.go.acquire()


class Ring:
    def __init__(self, alloc, name, shape, dt, n):
        self.name, self.shape, self.dt = name, shape, dt
        self.t = [alloc(name + str(k), shape, dt) for k in range(n)]
        self.r = [Res(name + str(k)) for k in range(n)]
        self.i = 0

    def grow(self, alloc, n):
        for _ in range(n):
            k = len(self.t)
            self.t.append(alloc(self.name + "x" + str(k), self.shape, self.dt))
            self.r.append(Res(self.name + "x" + str(k)))

    def shrink(self, n):
        for _ in range(n):
            self.t.pop()
            self.r.pop()

    def next(self):
        ln = LANE.get(threading.get_ident())
        if ln is not None and len(self.t) > 1:
            k = ln % len(self.t)
        else:
            k = self.i % len(self.t)
            self.i += 1
        return self.t[k], self.r[k]


LANE = {}


def run_pairs(fn, items, width=int(os.environ.get("KWIDTH", "2")), stagger=12):
    pending = list(items)
    active = [None] * width
    started = [False] * width
    steps = 0
    while pending or any(a is not None for a in active):
        for ln in range(width):
            if active[ln] is None and pending and (ln == 0 or started[ln] or steps >= stagger * ln):
                it = pending.pop(0)

                def body(it=it, ln=ln):
                    LANE[threading.get_ident()] = ln
                    try:
                        return fn(it)
                    finally:
                        LANE.pop(threading.get_ident(), None)
                active[ln] = Coop(body)
                started[ln] = True
            if active[ln] is not None:
                if not active[ln].step():
                    active[ln].finish()
                    active[ln] = None
        steps += 1


def build_program(dbg=False, stop=99):
    nc = bass.Bass("TRN2", target_bir_lowering=False)
    try:
        _build(nc, stop)
    except _Stop:
        pass
    return nc


def _build(nc, STOP):

    def dram(name, shape, dt=F32, kind="ExternalInput"):
        return nc.dram_tensor(name, list(shape), dt, kind=kind).ap()

    x_d = dram("x", [S, D])
    mem_d = dram("mem", [256, D])
    pos_d = dram("pos", [128, NT], I32)
    win_d = dram("w_in", [D, D_IN])
    wuq_d = dram("w_uq", [384, 768])
    wukv_d = dram("w_ukv", [256, 1024])
    wmem_d = dram("w_mem", [D, 1024])
    wbr_d = dram("w_br", [3 * 512, D])
    wout_d = dram("w_out", [D, D])
    gk_d = dram("gk", [128, 21])
    gh_d = dram("gh", [128, 576])
    cst_d = dram("cst", [128, 408])
    out_d = dram("out", [S, D], F32, kind="ExternalOutput")

    with ExitStack() as es:
        fw = FW(nc, es)

        uniq = [0]

        def sbt(stack, name, shape, dt):
            uniq[0] += 1
            nm = "sb%d_%s" % (uniq[0], name)
            DBG_NAMES[name] = nm
            return stack.enter_context(nc.sbuf_tensor(nm, list(shape), dt))

        def galloc(name, shape, dt):
            return sbt(es, name, shape, dt)

        banks = [es.enter_context(nc.psum_tensor("bank%d" % k, [128, 512], F32)) for k in range(8)]
        bres = [Res("bank%d" % k) for k in range(8)]
        rr = {"pj": 0, "tr": 0, "s": 0, "o": 0}
        role = {"pj": (0, 1), "tr": (2, 3), "s": (4, 5), "o": (6, 7)}

        def bank(kind):
            ln = LANE.get(threading.get_ident())
            if ln is not None and ln >= 2:
                k = {"pj": 4, "tr": 6, "s": 5, "o": 7}[kind] if ln == 2 else {"pj": 5, "tr": 7, "s": 4, "o": 6}[kind]
            elif ln is not None:
                k = role[kind][ln % 2]
            else:
                k = role[kind][rr[kind] % 2]
                rr[kind] += 1
            return banks[k], bres[k]

        cstf = galloc("cstf", [128, 408], F32); r_cstf = Res("cstf")
        gk = galloc("gk", [128, 21], F32); r_gk = Res("gk")
        gh = galloc("gh", [128, 576], F32); r_gh = Res("gh")
        posi = galloc("posi", [128, NT], I32); r_posi = Res("posi")
        fw.dma("sp", cstf[:], cst_d[:, :], writes=[r_cstf])
        fw.dma("sp", gk[:], gk_d[:, :], writes=[r_gk])
        fw.dma("sp", gh[:], gh_d[:, :], writes=[r_gh])
        fw.dma("sp", posi[:], pos_d[:, :], writes=[r_posi])
        identb = galloc("identb", [128, 128], BF16)
        bigI = galloc("bigI", [128, 128], BF16)
        cb01 = galloc("cb01", [128, 128], BF16)
        cmaskf = galloc("cmaskf", [128, 128], F32)
        epst = galloc("epst", [128, 1], F32)
        r_c = Res("consts")
        fw.op("dve", lambda e: e.tensor_copy(out=identb[:], in_=cstf[:, 0:128]), reads=[r_cstf], writes=[r_c])
        fw.op("dve", lambda e: e.tensor_scalar(out=bigI[:], in0=cstf[:, 0:128], scalar1=BIG, scalar2=None, op0=ALU.mult), reads=[r_cstf], writes=[r_c])
        fw.op("dve", lambda e: e.tensor_copy(out=cb01[:], in_=cstf[:, 128:256]), reads=[r_cstf], writes=[r_c])
        fw.op("dve", lambda e: e.tensor_scalar(out=cmaskf[:], in0=cstf[:, 128:256], scalar1=1e30, scalar2=None, op0=ALU.mult), reads=[r_cstf], writes=[r_c])
        fw.op("dve", lambda e: e.memset(epst[:], EPS), writes=[r_c])
        invf = cstf[:, 256:280]
        G_QNA, G_KNA, G_QNB, G_KNB, G_QNM, G_KNM = gh[:, 0:64], gh[:, 64:128], gh[:, 128:224], gh[:, 224:320], gh[:, 320:448], gh[:, 448:576]
        GK_NORM, GK_CQ, GK_CKV, GK_MEM = gk[:, 0:8], gk[:, 8:11], gk[:, 11:13], gk[:, 13:21]

        SIN = galloc("SIN", [128, NT, 24], F32)
        COS = galloc("COS", [128, NT, 24], F32)
        r_rope = Res("rope")

        print("ckpt", 0, fw.tot)
        if STOP == 0:
            fw.barrier()
            raise _Stop()
        hT = galloc("hT", [128, 8, S], BF16); r_hT = Res("hT")
        yT = [None, None, None]
        r_yT = [Res("yT%d" % n) for n in range(3)]

        stg = Ring(galloc, "stg", [128, 8, 128], F32, 2)
        SUBC = 128
        wbr_ = Ring(galloc, "wb", [128, 8, 512], BF16, 2)
        cast_rr = [0]

        def load_w(src, r0, kch, c0, ncols, gain=None):
            wb, r_wb = wbr_.next()
            for s0 in range(0, ncols, SUBC):
                n = min(SUBC, ncols - s0)
                st, r_st = stg.next()
                fw.dma("sp", st[:, 0:kch, 0:n],
                       src[r0:r0 + kch * 128, c0 + s0:c0 + s0 + n].rearrange("(kc p) n -> p kc n", p=128),
                       writes=[r_st])
                if gain is None:
                    fw.op("pool", lambda e: e.tensor_copy(out=wb[:, 0:kch, s0:s0 + n], in_=st[:, 0:kch, 0:n]), reads=[r_st], writes=[r_wb])
                else:
                    fw.op("pool", lambda e: e.tensor_tensor(out=wb[:, 0:kch, s0:s0 + n], in0=st[:, 0:kch, 0:n],
                                                           in1=gain[:, 0:kch].unsqueeze(2).to_broadcast([128, kch, n]), op=ALU.mult),
                          reads=[r_st, r_gk], writes=[r_wb])
            return wb, r_wb

        class WPipe:
            def __init__(self, specs):
                self.specs = specs
                self.k = 0
                self.loaded = {}

            def _load(self, k):
                if k < len(self.specs) and k not in self.loaded:
                    self.loaded[k] = self.specs[k][1]()

            def prefetch(self):
                self._load(self.k)

            def get(self, name, prefetch=True):
                assert self.specs[self.k][0] == name, (self.specs[self.k][0], name)
                self._load(self.k)
                res = self.loaded.pop(self.k)
                self.k += 1
                if prefetch:
                    self._load(self.k)
                return res

        wpipe = WPipe([
            ("ki", lambda: load_w(win_d, 0, 8, C_KI, 72, GK_NORM)),
            ("qi", lambda: load_w(win_d, 0, 8, C_QI, 512, GK_NORM)),
            ("va", lambda: load_w(win_d, 0, 8, C_VA, 512, GK_NORM)),
            ("ka", lambda: load_w(win_d, 0, 8, C_KA, 512, GK_NORM)),
            ("qa", lambda: load_w(win_d, 0, 8, C_QA, 512, GK_NORM)),
            ("za", lambda: load_w(win_d, 0, 8, C_ZA, 512, GK_NORM)),
            ("ckv", lambda: load_w(win_d, 0, 8, C_CKV, 288, GK_NORM)),
            ("cq", lambda: load_w(win_d, 0, 8, C_CQ, 384, GK_NORM)),
            ("zb", lambda: load_w(win_d, 0, 8, C_ZB, 512, GK_NORM)),
            ("km", lambda: load_w(wmem_d, 0, 8, 0, 512, GK_MEM)),
            ("vm", lambda: load_w(wmem_d, 0, 8, 512, 512, GK_MEM)),
            ("qm", lambda: load_w(win_d, 0, 8, C_QM, 512, GK_NORM)),
            ("zm", lambda: load_w(win_d, 0, 8, C_ZM, 512, GK_NORM)),
        ])

        def mm_tok(ps_ap, r_ps, srcT, r_src, t0, wb, r_wb, kch, c0, n):
            for kc in range(kch):
                fw.op("pe", lambda e, kc=kc: e.matmul(ps_ap, lhsT=srcT[:, kc, t0:t0 + 128], rhs=wb[:, kc, c0:c0 + n],
                                                      start=(kc == 0), stop=(kc == kch - 1)),
                      reads=[r_src, r_wb], writes=[r_ps])

        def transp(ps_ap, r_ps, src_ap, r_src):
            fw.op("pe", lambda e: e.matmul(ps_ap, lhsT=src_ap, rhs=identb[:, :], start=True, stop=True),
                  reads=[r_src, r_c], writes=[r_ps])

        class Widen:
            def __init__(self, rings, extra):
                self.rings, self.extra = rings, extra
                self.stack = ExitStack()
                for r in rings:
                    r.grow(lambda nm, sh, dt: sbt(self.stack, nm, sh, dt), extra)

            def close(self):
                fw.barrier()
                for r in self.rings:
                    r.shrink(self.extra)
                self.stack.close()

        wk_a = Ring(galloc, "wka", [128, 768], F32, 2)
        wk_b = Ring(galloc, "wkb", [128, 768], F32, 2)
        wk_bf = Ring(galloc, "wkbf", [128, 1024], BF16, 2)
        sm = Ring(galloc, "sm", [128, 16], F32, 4)
        rp = Ring(galloc, "rp", [128, 4, 128], F32, 2)

        def rstd_from_ss(ss_ap, r_ss, n_el, out_ap, r_out):
            fw.op("act", lambda e: e.activation(out=out_ap, in_=ss_ap, func=AF.Ln, bias=epst[:], scale=1.0 / n_el),
                  reads=[r_ss, r_c], writes=[r_out])
            fw.op("act", lambda e: e.activation(out=out_ap, in_=out_ap, func=AF.Exp, scale=-0.5), reads=[r_out], writes=[r_out])

        def headnorm(src3, r_src, H, d, gain_ap, dst3, r_dst):
            sq, r_sq = wk_b.next()
            sq3 = sq[:, 0:H * d].rearrange("p (h d) -> p h d", h=H)
            fw.op("act", lambda e: e.activation(out=sq3, in_=src3, func=AF.Square), reads=[r_src], writes=[r_sq])
            st_, r_sm = sm.next()
            fw.op("dve", lambda e: e.tensor_reduce(out=st_[:, 0:H], in_=sq3, axis=AX.X, op=ALU.add), reads=[r_sq], writes=[r_sm])
            rstd_from_ss(st_[:, 0:H], r_sm, d, st_[:, 8:8 + H], r_sm)
            fw.op("dve", lambda e: e.tensor_tensor(out=dst3, in0=src3, in1=st_[:, 8:8 + H].unsqueeze(2).to_broadcast([128, H, d]), op=ALU.mult),
                  reads=[r_src, r_sm], writes=[r_dst])
            fw.op("dve", lambda e: e.tensor_tensor(out=dst3, in0=dst3, in1=gain_ap.unsqueeze(1).to_broadcast([128, H, d]), op=ALU.mult),
                  reads=[r_dst, r_gh], writes=[r_dst])

        def rope(x3, r_x, H, o0, half, i, f0):
            t, r_t = rp.next()
            cs = COS[:, i, f0:f0 + half].unsqueeze(1).to_broadcast([128, H, half])
            sn = SIN[:, i, f0:f0 + half].unsqueeze(1).to_broadcast([128, H, half])
            x1 = x3[:, :, o0:o0 + half]
            x2 = x3[:, :, o0 + half:o0 + 2 * half]

            def tv(k):
                return t[:, k, 0:H * half].rearrange("p (h d) -> p h d", h=H)
            for k, (a, b) in enumerate(((x1, cs), (x2, sn), (x2, cs), (x1, sn))):
                fw.op("dve", lambda e, k=k, a=a, b=b: e.tensor_tensor(out=tv(k), in0=a, in1=b, op=ALU.mult),
                      reads=[r_x, r_rope], writes=[r_t])
            fw.op("dve", lambda e: e.tensor_tensor(out=x1, in0=tv(0), in1=tv(1), op=ALU.subtract), reads=[r_t], writes=[r_x])
            fw.op("dve", lambda e: e.tensor_tensor(out=x2, in0=tv(2), in1=tv(3), op=ALU.add), reads=[r_t], writes=[r_x])

        def to_T(srcbf, r_src, nblk, rows, dstT, r_dst, i, blk0=0, src_stride=None):
            stride = rows if src_stride is None else src_stride
            for b0 in range(0, nblk, 4):
                nb = min(4, nblk - b0)
                pb, r_pb = bank("tr")
                for b in range(nb):
                    transp(pb[0:rows, b * 128:(b + 1) * 128], r_pb, srcbf[:, (b0 + b) * stride:(b0 + b) * stride + rows], r_src)
                src_v = pb[0:rows, 0:nb * 128].rearrange("p (b t) -> p b t", b=nb)
                dst_v = dstT[0:rows, blk0 + b0:blk0 + b0 + nb, i * 128:(i + 1) * 128]
                eng = ("act", "dve")[(b0 // 4 + i) % 2]
                if eng == "act":
                    fw.op("act", lambda e: e.activation(out=dst_v, in_=src_v, func=AF.Copy), reads=[r_pb], writes=[r_dst])
                else:
                    fw.op("dve", lambda e: e.tensor_copy(out=dst_v, in_=src_v), reads=[r_pb], writes=[r_dst])

        with ExitStack() as p0:
            def lalloc(name, shape, dt):
                return sbt(p0, name, shape, dt)
            posf = sbt(p0, "posf", [128, NT], F32)
            u = sbt(p0, "rp_u", [128, NT, 24], F32)
            v = sbt(p0, "rp_v", [128, NT, 24], F32)
            ki = sbt(p0, "rp_ki", [128, NT, 24], I32)
            kf = sbt(p0, "rp_kf", [128, NT, 24], F32)
            m1 = sbt(p0, "rp_m1", [128, NT, 24], F32)
            r_t = Res("rp")
            fw.op("dve", lambda e: e.tensor_copy(out=posf[:], in_=posi[:]), reads=[r_posi], writes=[r_t])
            fw.op("dve", lambda e: e.tensor_tensor(out=u[:], in0=posf[:].unsqueeze(2).to_broadcast([128, NT, 24]),
                                                  in1=invf.unsqueeze(1).to_broadcast([128, NT, 24]), op=ALU.mult),
                  reads=[r_t, r_cstf], writes=[r_t])
            for shift, dst in ((0.0, SIN), (0.25, COS)):
                fw.op("dve", lambda e: e.tensor_scalar(out=v[:], in0=u[:], scalar1=shift, scalar2=None, op0=ALU.add), reads=[r_t], writes=[r_t])
                fw.op("dve", lambda e: e.tensor_copy(out=ki[:], in_=v[:]), reads=[r_t], writes=[r_t])
                fw.op("dve", lambda e: e.tensor_copy(out=kf[:], in_=ki[:]), reads=[r_t], writes=[r_t])
                fw.op("dve", lambda e: e.tensor_tensor(out=v[:], in0=v[:], in1=kf[:], op=ALU.subtract), reads=[r_t], writes=[r_t])
                fw.op("dve", lambda e: e.tensor_scalar(out=m1[:], in0=v[:], scalar1=0.5, scalar2=None, op0=ALU.is_gt), reads=[r_t], writes=[r_t])
                fw.op("dve", lambda e: e.tensor_tensor(out=v[:], in0=v[:], in1=m1[:], op=ALU.subtract), reads=[r_t], writes=[r_t])
                fw.op("dve", lambda e: e.tensor_scalar(out=m1[:], in0=v[:], scalar1=-0.5, scalar2=None, op0=ALU.is_lt), reads=[r_t], writes=[r_t])
                fw.op("dve", lambda e: e.tensor_tensor(out=v[:], in0=v[:], in1=m1[:], op=ALU.add), reads=[r_t], writes=[r_t])
                fw.op("act", lambda e, dst=dst: e.activation(out=dst[:], in_=v[:], func=AF.Sin, scale=float(2 * np.pi * (1 - 1e-6))),
                      reads=[r_t], writes=[r_rope])
            xin = Ring(lalloc, "xin", [128, D], F32, 4)
            wd = Widen([wk_bf], 2)
            wpipe.prefetch()
            def p0_tile(i):
                xt, r_xt = xin.next()
                fw.dma("sp", xt[:], x_d[i * 128:(i + 1) * 128, :], writes=[r_xt])
                sq, r_sq = wk_bf.next()
                st_, r_sm = sm.next()
                fw.op("act", lambda e: e.activation(out=sq[:], in_=xt[:], func=AF.Square, accum_out=st_[:, 0:1]), reads=[r_xt], writes=[r_sq, r_sm])
                rstd_from_ss(st_[:, 0:1], r_sm, D, st_[:, 1:2], r_sm)
                xb, r_xb = wk_bf.next()
                fw.op("dve", lambda e: e.tensor_scalar(out=xb[:], in0=xt[:], scalar1=st_[:, 1:2], scalar2=None, op0=ALU.mult),
                      reads=[r_xt, r_sm], writes=[r_xb])
                to_T(xb, r_xb, 8, 128, hT, r_hT, i)
            run_pairs(p0_tile, list(range(NT)), width=4)
            wd.close()

        print("ckpt", 1, fw.tot)
        if STOP == 1:
            fw.barrier()
            raise _Stop()
        def gate_tile(i, s_, yc, r_yc, wz, r_wz):
            pz, r_pz = bank("pj")
            mm_tok(pz[:, 0:512], r_pz, hT, r_hT, i * 128, wz, r_wz, 8, 0, 512)
            z1f, r_z1 = wk_a.next()
            z1 = z1f[:, 0:512]
            fw.op("act", lambda e: e.activation(out=z1, in_=pz[:, 0:512], func=AF.Exp, scale=-1.0), reads=[r_pz], writes=[r_z1])
            fw.op("dve", lambda e: e.tensor_scalar(out=z1, in0=z1, scalar1=1.0, scalar2=None, op0=ALU.add), reads=[r_z1], writes=[r_z1])
            fw.op("dve", lambda e: e.reciprocal(out=z1, in_=z1), reads=[r_z1], writes=[r_z1])
            fw.op("dve", lambda e: e.tensor_tensor(out=yc[:, s_, :], in0=z1, in1=pz[:, 0:512], op=ALU.mult), reads=[r_z1, r_pz], writes=[r_yc])

        def attention(p_alloc, make_q, kT_of, r_k, vaug, r_v, H, dv, n_ktiles_of, nq, scale, bias_of, yTn, r_yTn, wz, r_wz, mask_of=None, lag=1, first_q=None, order=None):
            dv1 = dv + 1
            pT = Ring(p_alloc, "pT", [128, 512], BF16, 2 + lag)
            ych = Ring(p_alloc, "ych", [128, nq, 512], BF16, 2)
            rec = Ring(p_alloc, "rec", [128, 8], F32, 2)
            et = Ring(p_alloc, "et", [128, nq, dv], F32, 2)
            nchunks = NT // nq

            def prep(c):
                yc, r_yc = ych.next()
                res = make_q(c, yc, r_yc, first_q if c == order[0] else None)
                if len(res) == 4:
                    return res
                return res[0], res[1], yc, r_yc
            order = list(range(nchunks)) if order is None else order
            cur = prep(order[0])
            for oc, c in enumerate(order):
                t_lo = c * nq
                fw.maybe_barrier()
                qT_of, r_q, yc, r_yc = cur
                nxt = Coop(lambda: prep(order[oc + 1])) if oc + 1 < nchunks else None
                nk_max = n_ktiles_of(t_lo + nq - 1)
                steps = [(h, j) for h in range(H) for j in range(nk_max)]
                pulls = max(2, -(-330 // len(steps)))
                state = {}

                def pv_and_epilogue(pv):
                    h, j, tiles, pt, r_pt, pov, r_po = pv
                    for i in tiles:
                        s_ = i - t_lo
                        fw.op("pe", lambda e: e.matmul(pov[:, s_, :], lhsT=pt[:, s_ * 128:(s_ + 1) * 128], rhs=vaug[:, j, h, :],
                                                       start=(j == 0 and s_ == 0), stop=(j == n_ktiles_of(i) - 1), skip_group_check=True),
                              reads=[r_pt, r_v], writes=[r_po])
                    if j == nk_max - 1:
                        rc, r_rc = rec.next()
                        t_, r_t = et.next()
                        fw.op("dve", lambda e: e.reciprocal(out=rc[:, 0:nq], in_=pov[:, :, dv]), reads=[r_po], writes=[r_rc])
                        fw.op("dve", lambda e: e.tensor_tensor(out=t_[:], in0=pov[:, :, 0:dv],
                                                              in1=rc[:, 0:nq].unsqueeze(2).to_broadcast([128, nq, dv]), op=ALU.mult),
                              reads=[r_po, r_rc], writes=[r_t])
                        fw.op("dve", lambda e: e.tensor_tensor(out=yc[:, :, h * dv:(h + 1) * dv], in0=t_[:], in1=yc[:, :, h * dv:(h + 1) * dv], op=ALU.mult),
                              reads=[r_t, r_yc], writes=[r_yc])

                queue = []
                for (h, j) in steps:
                    if j == 0:
                        po, r_po = bank("o")
                        state["pov"] = (po[:, 0:nq * dv1].rearrange("p (s d) -> p s d", s=nq), r_po)
                    pov, r_po = state["pov"]
                    tiles = [i for i in range(t_lo, t_lo + nq) if j < n_ktiles_of(i)]
                    col0 = (tiles[0] - t_lo) * 128
                    ncol = nq * 128 - col0
                    pss, r_pss = bank("s")
                    biases = [(i, bias_of(i, j)) for i in tiles]
                    biases = [(i, b) for i, b in biases if b is not None]
                    fw.op("pe", lambda e: e.matmul(pss[:, col0:col0 + ncol], lhsT=kT_of(h)[:, j * 128:(j + 1) * 128],
                                                   rhs=qT_of(h)[:, col0:nq * 128],
                                                   start=True, stop=(len(biases) == 0), skip_group_check=True),
                          reads=[r_k, r_q], writes=[r_pss])
                    for bi, (i, (b_ap, r_b)) in enumerate(biases):
                        cc = (i - t_lo) * 128
                        fw.op("pe", lambda e: e.matmul(pss[:, cc:cc + 128], lhsT=b_ap, rhs=bigI[:, :],
                                                       start=False, stop=(bi == len(biases) - 1), skip_group_check=True),
                              reads=[r_b, r_c], writes=[r_pss])
                    pt, r_pt = pT.next()
                    fw.op("act", lambda e: e.activation(out=pt[:, col0:col0 + ncol], in_=pss[:, col0:col0 + ncol], func=AF.Exp, scale=float(scale)),
                          reads=[r_pss], writes=[r_pt])
                    if mask_of is not None:
                        m_ap, r_m = mask_of(j, tiles[0], t_lo + nq)
                        fw.op("dve", lambda e: e.tensor_tensor(out=pt[:, col0:col0 + ncol], in0=pt[:, col0:col0 + ncol], in1=m_ap, op=ALU.mult),
                              reads=[r_pt, r_m], writes=[r_pt])
                    queue.append((h, j, tiles, pt, r_pt, pov, r_po))
                    if len(queue) > lag:
                        pv_and_epilogue(queue.pop(0))
                    if nxt is not None:
                        for _ in range(pulls):
                            if not nxt.step():
                                break
                while queue:
                    pv_and_epilogue(queue.pop(0))
                if nxt is not None:
                    cur = nxt.finish()
                for s_ in range(nq):
                    to_T(yc[:, s_, :], r_yc, 4, 128, yTn, r_yTn, t_lo + s_)

        yT[0] = galloc("yT0", [128, 4, S], BF16)
        mt_off = {}
        off = 0
        for j in range(NT):
            mt_off[j] = off
            off += (NT - j) * 128
        es_mb = ExitStack()
        maskT = sbt(es_mb, "maskT", [128, off], BF16); r_mt = Res("maskT")
        fw.op("dve", lambda e: e.tensor_copy(out=maskT[:, mt_off[0]:mt_off[0] + 128], in_=cstf[:, 280:408]), reads=[r_cstf], writes=[r_mt])
        fw.op("dve", lambda e: e.tensor_copy(out=maskT[:, mt_off[1]:mt_off[1] + 128], in_=cstf[:, 280:408]), reads=[r_cstf], writes=[r_mt])
        fw.op("dve", lambda e: e.memset(maskT[:, mt_off[0] + 128:mt_off[0] + 256], 1.0), writes=[r_mt])

        with ExitStack() as pi:
            def lalloc(name, shape, dt):
                return sbt(pi, name, shape, dt)
            qiT = lalloc("qiT", [128, 4, S], BF16); r_qiT = Res("qiT")
            kiT = lalloc("kiT", [128, 1, S], BF16); r_kiT = Res("kiT")
            wabs = lalloc("wabs", [128, NT, 8], F32)
            wsgn = lalloc("wsgn", [128, NT, 8], F32)
            r_wi = Res("wi")
            wb, r_wb = wpipe.get("ki")
            def ki_tile(i):
                pp, r_pp = bank("pj")
                mm_tok(pp[:, 0:72], r_pp, hT, r_hT, i * 128, wb, r_wb, 8, 0, 72)
                a, r_a = wk_a.next()
                fw.op("act", lambda e: e.activation(out=a[:, 0:72], in_=pp[:, 0:72], func=AF.Copy), reads=[r_pp], writes=[r_a])
                k3 = a[:, 0:64].rearrange("p (h d) -> p h d", h=1)
                rope(k3, r_a, 1, 0, 8, i, 0)
                kb, r_kb = wk_bf.next()
                fw.op("dve", lambda e: e.tensor_copy(out=kb[:, 0:64], in_=a[:, 0:64]), reads=[r_a], writes=[r_kb])
                fw.op("dve", lambda e: e.tensor_copy(out=kb[:, 64:128], in_=a[:, 0:64]), reads=[r_a], writes=[r_kb])
                to_T(kb, r_kb, 1, 128, kiT, r_kiT, i)
                cst_w = float(64 ** -0.5 * 8 ** -0.5)
                fw.op("act", lambda e: e.activation(out=wabs[:, i, :], in_=a[:, 64:72], func=AF.Abs, scale=cst_w),
                      reads=[r_a], writes=[r_wi])
                fw.op("dve", lambda e: e.tensor_scalar(out=wsgn[:, i, :], in0=a[:, 64:72], scalar1=0.0, scalar2=2.0, op0=ALU.is_ge, op1=ALU.mult),
                      reads=[r_a], writes=[r_wi])
                fw.op("dve", lambda e: e.tensor_scalar(out=wsgn[:, i, :], in0=wsgn[:, i, :], scalar1=-1.0, scalar2=None, op0=ALU.add),
                      reads=[r_wi], writes=[r_wi])
            wd = Widen([wk_a, wk_bf, rp], 2)
            run_pairs(ki_tile, list(range(NT)), width=4)
            wb, r_wb = wpipe.get("qi")
            def qi_tile(i):
                pp, r_pp = bank("pj")
                mm_tok(pp[:, 0:512], r_pp, hT, r_hT, i * 128, wb, r_wb, 8, 0, 512)
                a, r_a = wk_a.next()
                fw.op("act", lambda e: e.activation(out=a[:, 0:512], in_=pp[:, 0:512], func=AF.Copy), reads=[r_pp], writes=[r_a])
                q3 = a[:, 0:512].rearrange("p (h d) -> p h d", h=8)
                rope(q3, r_a, 8, 0, 8, i, 0)
                qb, r_qb = wk_bf.next()
                fw.op("dve", lambda e: e.tensor_copy(out=qb[:, 0:512], in_=a[:, 0:512]), reads=[r_a], writes=[r_qb])
                to_T(qb, r_qb, 4, 128, qiT, r_qiT, i)
            run_pairs(qi_tile, list(range(NT)), width=4)
            wd.close()
            if STOP == 2:
                raise _Stop()
            NLI = 3
            irow = Ring(lalloc, "irow", [128, S], F32, NLI)
            jd = Ring(lalloc, "jd", [128, S], BF16, NLI)
            ja_res = [Res("ja%d" % k) for k in range(NLI)]
            bm = Ring(lalloc, "bm", [128, 1], F32, NLI)
            bc = Ring(lalloc, "bc", [128, 2], F32, NLI)
            ba = Ring(lalloc, "ba", [128, 1], F32, NLI)
            MID0 = 0.0031415927

            rts = [Ring(lalloc, "rt%d_" % k, [128, 512], BF16, 2) for k in range(NLI)]
            dsg = Ring(lalloc, "dsg", [128, 8, 128], BF16, NLI)

            def idx_tile(i):
                L = (i + 1) * 128
                ln = LANE.get(threading.get_ident(), 0)
                ir, r_ir = irow.next()
                dg, r_dg = dsg.next()
                fw.op("dve", lambda e: e.tensor_tensor(out=dg[:, :, :], in0=identb[:, :].unsqueeze(1).to_broadcast([128, 8, 128]),
                                                      in1=wsgn[:, i, :].unsqueeze(2).to_broadcast([128, 8, 128]), op=ALU.mult),
                      reads=[r_c, r_wi], writes=[r_dg])
                for kc0 in range(0, L, 512):
                    n = min(512, L - kc0)
                    pacc, r_pacc = banks[3 + ln], bres[3 + ln]
                    pds = [(banks[ln], bres[ln]), (banks[ln], bres[ln])]
                    pend = None
                    for h in range(8):
                        hp = (h % 2) * 64
                        pd, r_pd = pds[h % 2]
                        fw.op("pe", lambda e: e.matmul(pd[:, 0:n], lhsT=qiT[hp:hp + 64, h // 2, i * 128:(i + 1) * 128],
                                                       rhs=kiT[hp:hp + 64, 0, kc0:kc0 + n], start=True, stop=True),
                              reads=[r_qiT, r_kiT], writes=[r_pd])
                        rt, r_rt = rts[ln % NLI].t[h % 2], rts[ln % NLI].r[h % 2]
                        fw.op("act", lambda e: e.activation(out=rt[:, 0:n], in_=pd[:, 0:n], func=AF.Relu, scale=wabs[:, i, h:h + 1]),
                              reads=[r_pd, r_wi], writes=[r_rt])
                        if pend is not None:
                            ph, prt, pr_rt = pend
                            fw.op("pe", lambda e: e.matmul(pacc[:, 0:n], lhsT=dg[:, ph, :], rhs=prt[:, 0:n], start=(ph == 0), stop=False),
                                  reads=[r_dg, pr_rt], writes=[r_pacc])
                        pend = (h, rt, r_rt)
                    ph, prt, pr_rt = pend
                    fw.op("pe", lambda e: e.matmul(pacc[:, 0:n], lhsT=dg[:, ph, :], rhs=prt[:, 0:n], start=False, stop=True),
                          reads=[r_dg, pr_rt], writes=[r_pacc])
                    if kc0 + n == L:
                        d0 = i * 128 - kc0
                        if d0 > 0:
                            fw.op("dve", lambda e: e.tensor_copy(out=ir[:, kc0:kc0 + d0], in_=pacc[:, 0:d0]), reads=[r_pacc], writes=[r_ir])
                        fw.op("dve", lambda e: e.tensor_tensor(out=ir[:, i * 128:L], in0=pacc[:, d0:d0 + 128], in1=cmaskf[:, :], op=ALU.add),
                              reads=[r_pacc, r_c], writes=[r_ir])
                    else:
                        fw.op("dve", lambda e: e.tensor_copy(out=ir[:, kc0:kc0 + n], in_=pacc[:, 0:n]), reads=[r_pacc], writes=[r_ir])
                L1 = max(128, (int(L * 0.5) // 128) * 128)
                L2 = L - L1
                mid_t, r_bm = bm.next()
                c_t, r_bc = bc.next()
                a_t, r_ba = ba.next()
                jdt, r_jd = jd.next()
                jat, r_ja = jdt[:, L1:L], ja_res[ln % NLI]
                mid, cnt, stp, asum = mid_t[:, 0:1], c_t[:, 0:1], c_t[:, 1:2], a_t[:, 0:1]
                fw.op("dve", lambda e: e.memset(mid, MID0), writes=[r_bm])
                W = 32.0
                for k in range(NBIS):
                    fw.op("dve", lambda e: e.tensor_scalar(out=jdt[:, 0:L1], in0=ir[:, 0:L1], scalar1=mid, scalar2=0.0, op0=ALU.is_ge, op1=ALU.add,
                                                          accum_out=cnt), reads=[r_ir, r_bm], writes=[r_jd, r_bc])
                    fw.op("act", lambda e: e.activation(out=jat, in_=ir[:, L1:L], func=AF.Sign, bias=mid, scale=-1.0, accum_out=asum),
                          reads=[r_ir, r_bm], writes=[r_ja, r_ba])
                    fw.op("dve", lambda e: e.scalar_tensor_tensor(out=stp, in0=cnt, scalar=2.0, in1=asum, op0=ALU.mult, op1=ALU.subtract),
                          reads=[r_bc, r_ba], writes=[r_bc])
                    fw.op("dve", lambda e: e.tensor_scalar(out=stp, in0=stp, scalar1=float(511.0 - L2), scalar2=float(W / 2), op0=ALU.is_ge, op1=ALU.mult),
                          reads=[r_bc], writes=[r_bc])
                    q = W / 4 if k < NBIS - 1 else W / 2
                    fw.op("dve", lambda e: e.scalar_tensor_tensor(out=mid, in0=stp, scalar=float(-q), in1=mid, op0=ALU.add, op1=ALU.add),
                          reads=[r_bc, r_bm], writes=[r_bm])
                    W = W / 2
                fw.op("dve", lambda e: e.tensor_scalar(out=jdt[:, 0:L], in0=ir[:, 0:L], scalar1=mid, scalar2=None, op0=ALU.is_ge),
                      reads=[r_ir, r_bm], writes=[r_jd, r_ja])
                Coop.hold.add(threading.get_ident())
                for j0 in range(0, i + 1, 4):
                    nb = min(4, i + 1 - j0)
                    pb, r_pb = banks[6 + (j0 // 4) % 2], bres[6 + (j0 // 4) % 2]
                    for b in range(nb):
                        transp(pb[:, b * 128:(b + 1) * 128], r_pb, jdt[:, (j0 + b) * 128:(j0 + b + 1) * 128], r_jd)
                    for b in range(nb):
                        j = j0 + b
                        d_ = mt_off[j] + (i - j) * 128
                        if (b + i) % 2 == 0:
                            fw.op("act", lambda e: e.activation(out=maskT[:, d_:d_ + 128], in_=pb[:, b * 128:(b + 1) * 128], func=AF.Copy), reads=[r_pb], writes=[r_mt])
                        else:
                            fw.op("dve", lambda e: e.tensor_copy(out=maskT[:, d_:d_ + 128], in_=pb[:, b * 128:(b + 1) * 128]), reads=[r_pb], writes=[r_mt])
                Coop.hold.discard(threading.get_ident())

            run_pairs(idx_tile, list(range(2, NT)), width=NLI, stagger=25)
            fw.barrier()

        print("ckpt", 3, fw.tot)
        if STOP == 3:
            fw.barrier()
            raise _Stop()
        with ExitStack() as pa:
            def lalloc(name, shape, dt):
                return sbt(pa, name, shape, dt)
            kaT = lalloc("kaT", [128, 4, S], BF16)
            vau = lalloc("vau", [128, NT, 8, 65], BF16)
            qch = Ring(lalloc, "qchA", [128, 4, 512], BF16, 2)
            r_k = Res("kA"); r_v = Res("vA")
            fw.op("pool", lambda e: e.memset(vau[:, :, :, 64:65], 1.0), writes=[r_v])

            def qk_tile(i, wb, r_wb, gain_ap, dstT, r_dst, slot):
                pp, r_pp = bank("pj")
                mm_tok(pp[:, 0:512], r_pp, hT, r_hT, i * 128, wb, r_wb, 8, 0, 512)
                a, r_a = wk_a.next()
                fw.op("act", lambda e: e.activation(out=a[:, 0:512], in_=pp[:, 0:512], func=AF.Copy), reads=[r_pp], writes=[r_a])
                a3 = a[:, 0:512].rearrange("p (h d) -> p h d", h=8)
                headnorm(a3, r_a, 8, 64, gain_ap, a3, r_a)
                rope(a3, r_a, 8, 0, 8, i, 0)
                qb, r_qb = wk_bf.next()
                fw.op("dve", lambda e: e.tensor_copy(out=qb[:, 0:512], in_=a[:, 0:512]), reads=[r_a], writes=[r_qb])
                to_T(qb, r_qb, 4, 128, dstT, r_dst, slot)

            wb, r_wb = wpipe.get("va")
            for i in range(NT):
                pp, r_pp = bank("pj")
                mm_tok(pp[:, 0:512], r_pp, hT, r_hT, i * 128, wb, r_wb, 8, 0, 512)
                fw.op("act", lambda e: e.activation(out=vau[:, i, :, 0:64], in_=pp[:, 0:512].rearrange("p (h d) -> p h d", h=8), func=AF.Copy),
                      reads=[r_pp], writes=[r_v])
            wb, r_wb = wpipe.get("ka")
            wq, r_wq = wpipe.get("qa", prefetch=False)
            wd = Widen([wk_a, wk_b, wk_bf, rp], 2)
            qc0, r_qc0 = qch.next()

            def kq_item(it):
                kind, i = it
                if kind == "k":
                    qk_tile(i, wb, r_wb, G_KNA, kaT, r_k, i)
                else:
                    qk_tile(12 + i, wq, r_wq, G_QNA, qc0, r_qc0, i)
            run_pairs(kq_item, [("k", i) for i in range(NT)] + [("q", i) for i in range(4)], width=4)
            wz, r_wz = wpipe.get("za", prefetch=False)
            wd.close()

            def make_q_A(c, yc, r_yc, first):
                items = [("g", s_) for s_ in range(4)]
                if first is not None:
                    qc, r_qc = first
                else:
                    qc, r_qc = qch.next()
                    items = [x for s_ in range(4) for x in (("q", s_), ("g", s_))]

                def item(it):
                    if it[0] == "q":
                        qk_tile(c * 4 + it[1], wq, r_wq, G_QNA, qc, r_qc, it[1])
                    else:
                        gate_tile(c * 4 + it[1], it[1], yc, r_yc, wz, r_wz)
                run_pairs(item, items, stagger=8)
                return (lambda h: qc[(h % 2) * 64:(h % 2) * 64 + 64, h // 2, :]), r_qc

            def mask_A(j, i0, i1):
                o_ = mt_off[j] + (i0 - j) * 128
                return maskT[:, o_:o_ + (i1 - i0) * 128], r_mt
            attention(lalloc, make_q_A,
                      lambda h: kaT[(h % 2) * 64:(h % 2) * 64 + 64, h // 2, :],
                      r_k, vau, r_v, 8, 64, lambda i: i + 1, 4, 64 ** -0.5, lambda i, j: None, yT[0], r_yT[0], wz, r_wz,
                      mask_of=mask_A, lag=4, first_q=(qc0, r_qc0), order=[3, 2, 1, 0])
            wpipe.prefetch()
            fw.barrier()
        es_mb.close()
        print("ckpt", 4, fw.tot)
        if STOP == 4:
            fw.barrier()
            raise _Stop()
        yT[1] = galloc("yT1", [128, 4, S], BF16)

        with ExitStack() as pb:
            def lalloc(name, shape, dt):
                return sbt(pb, name, shape, dt)
            kbT = lalloc("kbT", [96, 8, S], BF16)
            vbu = lalloc("vbu", [128, NT, 8, 65], BF16)
            r_k = Res("kB"); r_v = Res("vB")
            fw.op("pool", lambda e: e.memset(vbu[:, :, :, 64:65], 1.0), writes=[r_v])
            wuq = lalloc("wuq", [128, 3, 768], BF16); r_wuq = Res("wuq")
            latT = Ring(lalloc, "latT", [128, 3, 128], BF16, 2)
            qch = Ring(lalloc, "qchB", [96, 8, 512], BF16, 1)
            pbk = ExitStack()
            wukv = sbt(pbk, "wukv", [128, 2, 1024], BF16); r_wukv = Res("wukv")
            for (dst, r_dst, src, kch, ncols, gain) in ((wuq, r_wuq, wuq_d, 3, 768, GK_CQ), (wukv, r_wukv, wukv_d, 2, 1024, GK_CKV)):
                for s0 in range(0, ncols, 128):
                    st, r_st = stg.next()
                    fw.dma("sp", st[:, 0:kch, 0:128], src[0:kch * 128, s0:s0 + 128].rearrange("(kc p) n -> p kc n", p=128), writes=[r_st])
                    fw.op("pool", lambda e: e.tensor_tensor(out=dst[:, 0:kch, s0:s0 + 128], in0=st[:, 0:kch, 0:128],
                                                           in1=gain[:, 0:kch].unsqueeze(2).to_broadcast([128, kch, 128]), op=ALU.mult),
                          reads=[r_st, r_gk], writes=[r_dst])
            if STOP == 41:
                fw.barrier()
                raise _Stop()

            def latent(i, wb, r_wb, ncols_all, nlat):
                pp, r_pp = bank("pj")
                mm_tok(pp[:, 0:ncols_all], r_pp, hT, r_hT, i * 128, wb, r_wb, 8, 0, ncols_all)
                sq, r_sq = wk_b.next()
                st_, r_sm = sm.next()
                fw.op("act", lambda e: e.activation(out=sq[:, 0:nlat], in_=pp[:, 0:nlat], func=AF.Square, accum_out=st_[:, 0:1]), reads=[r_pp], writes=[r_sq, r_sm])
                rstd_from_ss(st_[:, 0:1], r_sm, nlat, st_[:, 1:2], r_sm)
                cb_, r_cb = wk_bf.next()
                fw.op("dve", lambda e: e.tensor_scalar(out=cb_[:, 0:nlat], in0=pp[:, 0:nlat], scalar1=st_[:, 1:2], scalar2=None, op0=ALU.mult),
                      reads=[r_pp, r_sm], writes=[r_cb])
                lt, r_lt = latT.next()
                to_T(cb_, r_cb, nlat // 128, 128, lt, r_lt, 0)
                return pp, r_pp, lt, r_lt

            def ckb(k):
                if STOP == k:
                    fw.barrier()
                    raise _Stop()
            wd = Widen([wk_a, wk_b, wk_bf, rp, latT], 2)
            wb, r_wb = wpipe.get("ckv")
            def kvb_tile(i):
                pp, r_pp, lt, r_lt = latent(i, wb, r_wb, 288, 256)
                a, r_a = wk_a.next()
                a3 = a[:, 0:768].rearrange("p (h d) -> p h d", h=8)
                fw.op("act", lambda e: e.activation(out=a3[:, :, 64:96], in_=pp[:, 256:288].unsqueeze(1).to_broadcast([128, 8, 32]), func=AF.Copy),
                      reads=[r_pp], writes=[r_a])
                for half in range(2):
                    pq, r_pq = bank("pj")
                    mm_tok(pq[:, 0:512], r_pq, lt, r_lt, 0, wukv, r_wukv, 2, half * 512, 512)
                    pq3 = pq[:, 0:512].rearrange("p (h d) -> p h d", h=4)
                    fw.op("act", lambda e: e.activation(out=a3[:, half * 4:(half + 1) * 4, 0:64], in_=pq3[:, :, 0:64], func=AF.Copy), reads=[r_pq], writes=[r_a])
                    fw.op("act", lambda e: e.activation(out=vbu[:, i, half * 4:(half + 1) * 4, 0:64], in_=pq3[:, :, 64:128], func=AF.Copy), reads=[r_pq], writes=[r_v])
                headnorm(a3, r_a, 8, 96, G_KNB, a3, r_a)
                rope(a3, r_a, 8, 64, 16, i, 8)
                kb, r_kb = wk_bf.next()
                fw.op("dve", lambda e: e.tensor_copy(out=kb[:, 0:768], in_=a[:, 0:768]), reads=[r_a], writes=[r_kb])
                to_T(kb, r_kb, 8, 96, kbT, r_k, i)
            wq, r_wq = wpipe.get("cq", prefetch=False)

            def qb_tile_into(i, qc, r_qc, s_):
                pp, r_pp, lt, r_lt = latent(i, wq, r_wq, 384, 384)
                a, r_a = wk_a.next()
                for half in range(2):
                    pq, r_pq = bank("pj")
                    mm_tok(pq[:, 0:384], r_pq, lt, r_lt, 0, wuq, r_wuq, 3, half * 384, 384)
                    fw.op("act", lambda e: e.activation(out=a[:, half * 384:(half + 1) * 384], in_=pq[:, 0:384], func=AF.Copy), reads=[r_pq], writes=[r_a])
                a3 = a[:, 0:768].rearrange("p (h d) -> p h d", h=8)
                headnorm(a3, r_a, 8, 96, G_QNB, a3, r_a)
                rope(a3, r_a, 8, 64, 16, i, 8)
                qb, r_qb = wk_bf.next()
                fw.op("dve", lambda e: e.tensor_copy(out=qb[:, 0:768], in_=a[:, 0:768]), reads=[r_a], writes=[r_qb])
                to_T(qb, r_qb, 8, 96, qc, r_qc, s_)

            qc0, r_qc0 = qch.next()

            def kq_item(it):
                kind, i = it
                if kind == "k":
                    kvb_tile(i)
                else:
                    qb_tile_into(12 + i, qc0, r_qc0, i)
            run_pairs(kq_item, [("k", i) for i in range(NT)] + [("q", i) for i in range(4)], width=4)
            wd.close()
            fw.barrier()
            pbk.close()
            qch.grow(lalloc, 1)
            wz, r_wz = wpipe.get("zb", prefetch=False)

            def make_q_B(c, yc, r_yc, first):
                items = [("g", s_) for s_ in range(4)]
                if first is not None:
                    qc, r_qc = first
                else:
                    qc, r_qc = qch.next()
                    items = [x for s_ in range(4) for x in (("q", s_), ("g", s_))]

                def item(it):
                    if it[0] == "q":
                        qb_tile_into(c * 4 + it[1], qc, r_qc, it[1])
                    else:
                        gate_tile(c * 4 + it[1], it[1], yc, r_yc, wz, r_wz)
                run_pairs(item, items, stagger=8)
                return (lambda h: qc[0:96, h, :]), r_qc

            def bias_B(i, j):
                if i == j:
                    return (cb01[:, :], r_c)
                return None
            attention(lalloc, make_q_B, lambda h: kbT[0:96, h, :], r_k, vbu, r_v, 8, 64,
                      lambda i: i + 1, 4, 96 ** -0.5, bias_B, yT[1], r_yT[1], wz, r_wz, first_q=(qc0, r_qc0), order=[3, 2, 1, 0], lag=4)
            wpipe.prefetch()
            fw.barrier()
        yT[2] = galloc("yT2", [128, 4, S], BF16)
        print("ckpt", 5, fw.tot)
        if STOP == 5:
            fw.barrier()
            raise _Stop()

        with ExitStack() as pm:
            def lalloc(name, shape, dt):
                return sbt(pm, name, shape, dt)
            kmT = lalloc("kmT", [128, 4, 256], BF16)
            vmu = lalloc("vmu", [128, 2, 4, 129], BF16)
            memT = lalloc("memT", [128, 8, 256], BF16); r_memT = Res("memT")
            qmT = lalloc("qmT", [128, 4, S], BF16); r_qm = Res("qmT")
            y_all = lalloc("yallM", [128, NT, 512], BF16); r_yall = Res("yallM")
            r_k = Res("kM"); r_v = Res("vM")
            fw.op("pool", lambda e: e.memset(vmu[:, :, :, 128:129], 1.0), writes=[r_v])
            pms = ExitStack()
            xin = Ring(lambda nm, sh, dt: sbt(pms, nm, sh, dt), "min", [128, D], F32, 2)
            wd = Widen([wk_a, wk_b, wk_bf, rp], 2)
            for t in range(2):
                xt, r_xt = xin.next()
                fw.dma("sp", xt[:], mem_d[t * 128:(t + 1) * 128, :], writes=[r_xt])
                sq, r_sq = wk_bf.next()
                st_, r_sm = sm.next()
                fw.op("act", lambda e: e.activation(out=sq[:], in_=xt[:], func=AF.Square, accum_out=st_[:, 0:1]), reads=[r_xt], writes=[r_sq, r_sm])
                rstd_from_ss(st_[:, 0:1], r_sm, D, st_[:, 1:2], r_sm)
                xb, r_xb = wk_bf.next()
                fw.op("dve", lambda e: e.tensor_scalar(out=xb[:], in0=xt[:], scalar1=st_[:, 1:2], scalar2=None, op0=ALU.mult), reads=[r_xt, r_sm], writes=[r_xb])
                to_T(xb, r_xb, 8, 128, memT, r_memT, t)

            def qk_tile_m(srcT, r_src, t0, wb, r_wb, gain_ap, dstT, r_dst, slot):
                pp, r_pp = bank("pj")
                mm_tok(pp[:, 0:512], r_pp, srcT, r_src, t0, wb, r_wb, 8, 0, 512)
                a, r_a = wk_a.next()
                fw.op("act", lambda e: e.activation(out=a[:, 0:512], in_=pp[:, 0:512], func=AF.Copy), reads=[r_pp], writes=[r_a])
                a3 = a[:, 0:512].rearrange("p (h d) -> p h d", h=4)
                headnorm(a3, r_a, 4, 128, gain_ap, a3, r_a)
                kb, r_kb = wk_bf.next()
                fw.op("dve", lambda e: e.tensor_copy(out=kb[:, 0:512], in_=a[:, 0:512]), reads=[r_a], writes=[r_kb])
                to_T(kb, r_kb, 4, 128, dstT, r_dst, slot)

            wb, r_wb = wpipe.get("km")
            for t in range(2):
                qk_tile_m(memT, r_memT, t * 128, wb, r_wb, G_KNM, kmT, r_k, t)
            wb, r_wb = wpipe.get("vm")
            for t in range(2):
                pp, r_pp = bank("pj")
                mm_tok(pp[:, 0:512], r_pp, memT, r_memT, t * 128, wb, r_wb, 8, 0, 512)
                fw.op("act", lambda e: e.activation(out=vmu[:, t, :, 0:128], in_=pp[:, 0:512].rearrange("p (h d) -> p h d", h=4), func=AF.Copy),
                      reads=[r_pp], writes=[r_v])
            wq, r_wq = wpipe.get("qm")
            wz, r_wz = wpipe.get("zm", prefetch=False)

            def mq_tile(i):
                qk_tile_m(hT, r_hT, i * 128, wq, r_wq, G_QNM, qmT, r_qm, i)
                gate_tile(i, i, y_all, r_yall, wz, r_wz)
            run_pairs(mq_tile, list(range(NT)), width=4)
            wd.close()
            pms.close()

            def make_q_M(c, yc, r_yc, first):
                return (lambda h: qmT[:, h, c * 256:(c + 1) * 256]), r_qm, y_all[:, 2 * c:2 * c + 2, :], r_yall
            attention(lalloc, make_q_M, lambda h: kmT[:, h, :], r_k, vmu, r_v, 4, 128,
                      lambda i: 2, 2, 128 ** -0.5, lambda i, j: None, yT[2], r_yT[2], wz, r_wz, lag=2)
            fw.barrier()

        print("ckpt", 6, fw.tot)
        if STOP == 6:
            fw.barrier()
            raise _Stop()
        with ExitStack() as pf:
            def lalloc(name, shape, dt):
                return sbt(pf, name, shape, dt)
            mT = lalloc("mT", [128, 8, S], BF16); r_mT = Res("mT")
            pfd = ExitStack()

            def dalloc(name, shape, dt):
                return sbt(pfd, name, shape, dt)
            wbrs = Ring(dalloc, "wbrs", [128, 3, 4, 128], BF16, 2)
            sg = Ring(dalloc, "sg", [128, 512], F32, 2)
            tmp = Ring(dalloc, "tmpf", [128, 512], F32, 2)
            macc = Ring(dalloc, "macc", [128, 512], F32, 2)
            fw.barrier()
            gst2 = dalloc("gst2", [128, 8, 128], F32)
            gst = [stg.t[0][:, :, :], stg.t[1][:, :, :], gst2[:, :, :]]
            r_gst = [stg.r[0], stg.r[1], Res("gst2")]
            bst = dalloc("bst", [128, 3, 4, 128], F32); r_bst = Res("bst")

            def issue_dc(dc):
                for n in range(3):
                    c0 = C_G + n * 1024 + dc * 128
                    fw.dma("sp", gst[n], win_d[0:1024, c0:c0 + 128].rearrange("(kc p) n -> p kc n", p=128), writes=[r_gst[n]])
                for n in range(3):
                    fw.dma("sp", bst[:, n, :, :], wbr_d[n * 512:(n + 1) * 512, dc * 128:(dc + 1) * 128].rearrange("(kc p) n -> p kc n", p=128), writes=[r_bst])

            def cast_dc(dc):
                wg, r_wg = wbr_.next()
                for n in range(3):
                    fw.op("dve", lambda e: e.tensor_tensor(out=wg[:, 0:8, n * 128:(n + 1) * 128], in0=gst[n],
                                                          in1=GK_NORM.unsqueeze(2).to_broadcast([128, 8, 128]), op=ALU.mult),
                          reads=[r_gst[n], r_gk], writes=[r_wg])
                wbn, r_wbn = wbrs.next()
                fw.op("dve", lambda e: e.tensor_copy(out=wbn[:], in_=bst[:]), reads=[r_bst], writes=[r_wbn])
                return wg, r_wg, wbn, r_wbn

            issue_dc(0)
            nxt_w = cast_dc(0)
            for dc in range(8):
                fw.maybe_barrier()
                wg, r_wg, wbn, r_wbn = nxt_w
                if dc + 1 < 8:
                    issue_dc(dc + 1)
                else:
                    wout0 = load_w(wout_d, 0, 8, 0, 512, None)
                for tc in range(4):
                    if tc == 2 and dc + 1 < 8:
                        nxt_w = cast_dc(dc + 1)
                    ma, r_ma = macc.next()
                    for n in range(3):
                        pg, r_pg = bank("pj")
                        for kc in range(8):
                            fw.op("pe", lambda e, kc=kc: e.matmul(pg[:, 0:512], lhsT=wg[:, kc, n * 128:(n + 1) * 128], rhs=hT[:, kc, tc * 512:(tc + 1) * 512],
                                                                  start=(kc == 0), stop=(kc == 7)), reads=[r_wg, r_hT], writes=[r_pg])
                        s1, r_s1 = sg.next()
                        fw.op("act", lambda e: e.activation(out=s1[:], in_=pg[:, 0:512], func=AF.Tanh, scale=0.5), reads=[r_pg], writes=[r_s1])
                        pbr, r_pbr = bank("s")
                        for kw in range(4):
                            fw.op("pe", lambda e, kw=kw: e.matmul(pbr[:, 0:512], lhsT=wbn[:, n, kw, :], rhs=yT[n][:, kw, tc * 512:(tc + 1) * 512],
                                                                  start=(kw == 0), stop=(kw == 3)), reads=[r_wbn, r_yT[n]], writes=[r_pbr])
                        if n == 0:
                            fw.op("dve", lambda e: e.scalar_tensor_tensor(out=ma[:], in0=s1[:], scalar=1.0, in1=pbr[:, 0:512], op0=ALU.add, op1=ALU.mult),
                                  reads=[r_s1, r_pbr], writes=[r_ma])
                        else:
                            t1, r_t1 = tmp.next()
                            fw.op("dve", lambda e: e.scalar_tensor_tensor(out=t1[:], in0=s1[:], scalar=1.0, in1=pbr[:, 0:512], op0=ALU.add, op1=ALU.mult),
                                  reads=[r_s1, r_pbr], writes=[r_t1])
                            fw.op("dve", lambda e: e.tensor_tensor(out=ma[:], in0=ma[:], in1=t1[:], op=ALU.add), reads=[r_ma, r_t1], writes=[r_ma])
                    fw.op("act", lambda e: e.activation(out=mT[:, dc, tc * 512:(tc + 1) * 512], in_=ma[:], func=AF.Copy, scale=0.5), reads=[r_ma], writes=[r_mT])
            fw.barrier()
            pfd.close()
            xin = Ring(lalloc, "xres", [128, 512], F32, 4)
            oo = Ring(lalloc, "oo", [128, 512], F32, 4)
            wouts = [wout0, load_w(wout_d, 0, 8, 512, 512, None)]
            for half in range(2):
                wb, r_wb = wouts[half]
                for i in range(NT):
                    xt, r_xt = xin.next()
                    fw.dma("sp", xt[:], x_d[i * 128:(i + 1) * 128, half * 512:(half + 1) * 512], writes=[r_xt])
                    pp, r_pp = bank("pj")
                    mm_tok(pp[:, 0:512], r_pp, mT, r_mT, i * 128, wb, r_wb, 8, 0, 512)
                    ot, r_ot = oo.next()
                    fw.op("dve", lambda e: e.tensor_tensor(out=ot[:], in0=pp[:, 0:512], in1=xt[:], op=ALU.add), reads=[r_pp, r_xt], writes=[r_ot])
                    fw.dma("act", out_d[i * 128:(i + 1) * 128, half * 512:(half + 1) * 512], ot[:], reads=[r_ot])
            fw.barrier()
        print("inst counts", fw.tot, "sems", fw.nsem, "epochs", fw.epoch)


_NC_CACHE = {}


def _host_consts():
    ident = np.eye(128, dtype=np.float32)
    q = np.arange(128)[:, None]
    k = np.arange(128)[None, :]
    cb = np.where(k > q, -1.0, 0.0).astype(np.float32)
    theta = 500000.0
    fa = theta ** (-(np.arange(0, 16, 2, dtype=np.float32) / 16.0))
    fb = theta ** (-(np.arange(0, 32, 2, dtype=np.float32) / 32.0))
    invf = (np.concatenate([fa, fb]).astype(np.float64) / (2 * np.pi)).astype(np.float32)
    c01t = np.where(q <= k, 1.0, 0.0).astype(np.float32)
    cst = np.concatenate([ident, cb, np.broadcast_to(invf[None, :], (128, 24)), c01t], axis=1)
    return np.ascontiguousarray(cst, dtype=np.float32)


def kernel(x, mem, positions, g_norm, w_in, g_qn_a, g_kn_a, g_cq, g_ckv, w_uq, w_ukv,
           g_qn_b, g_kn_b, g_mem, w_mem_kv, g_qn_m, g_kn_m, w_branch, w_out):
    f = lambda a: np.ascontiguousarray(np.asarray(a), dtype=np.float32)
    x = f(x); mem = f(mem)
    positions = np.asarray(positions).astype(np.int32)
    n = 8
    if "nc" not in _NC_CACHE:
        _NC_CACHE["nc"] = build_program(stop=int(os.environ.get("KSTOP", "99")))
    nc = _NC_CACHE["nc"]
    gk = np.concatenate([f(g_norm)[0].reshape(8, 128).T, f(g_cq)[0].reshape(3, 128).T,
                         f(g_ckv)[0].reshape(2, 128).T, f(g_mem)[0].reshape(8, 128).T], axis=1)
    ghv = np.concatenate([f(g_qn_a)[0], f(g_kn_a)[0], f(g_qn_b)[0], f(g_kn_b)[0], f(g_qn_m)[0], f(g_kn_m)[0]])
    gh = np.broadcast_to(ghv[None, :], (128, 576))
    shared = {
        "w_in": f(w_in)[0], "w_uq": f(w_uq)[0], "w_ukv": f(w_ukv)[0], "w_mem": f(w_mem_kv)[0],
        "w_br": f(w_branch)[0].reshape(3 * 512, D), "w_out": f(w_out)[0],
        "gk": np.ascontiguousarray(gk, dtype=np.float32), "gh": np.ascontiguousarray(gh, dtype=np.float32),
        "cst": _host_consts(),
    }
    in_maps = []
    for b in range(n):
        m = dict(shared)
        m["x"] = x[b]
        m["mem"] = mem[b]
        m["pos"] = np.ascontiguousarray(positions[b].reshape(NT, 128).T)
        in_maps.append(m)
    res = run_bass_kernel_spmd(nc, in_maps, core_ids=list(range(n)))
    return np.stack([np.asarray(r["out"], dtype=np.float32) for r in res.results], axis=0)
```
